# Optimizing a Trainium2 kernel written in Bass

```python
import jax, jax.numpy as jnp
from jax import lax
import numpy as np

D_MODEL = 1024
BATCH = 16
SEQ = 2048
DEPTH = 2

CHUNK = 64
MIX_WIDTH = D_MODEL
RWKV_WIDTH = MIX_WIDTH // 2
CONV_WIDTH = MIX_WIDTH - RWKV_WIDTH
RWKV_HEAD = 64
RWKV_HEADS = RWKV_WIDTH // RWKV_HEAD
DECAY_LORA = 64
ICLR_LORA = 64
GATE_LORA = 128
CONV_K = 31
D_FF = 4 * D_MODEL
NORM_EPS = 1e-5
LN_EPS = 1e-5
GN_EPS = 64e-5
L2_EPS = 1e-12

OFF_R = 0
OFF_K = OFF_R + RWKV_WIDTH
OFF_V = OFF_K + RWKV_WIDTH
OFF_WD = OFF_V + RWKV_WIDTH
OFF_AD = OFF_WD + DECAY_LORA
OFF_GD = OFF_AD + ICLR_LORA
RWKV_COLS = OFF_GD + GATE_LORA
CONV_COLS = 2 * CONV_WIDTH
IN_COLS = RWKV_COLS + CONV_COLS

kernel_name = "hybrid_rwkv7_conformer_conv_block"


def rmsnorm(x, g):
    xf = x.astype(jnp.float32)
    y = xf * lax.rsqrt(jnp.mean(xf * xf, axis=-1, keepdims=True) + NORM_EPS)
    return (y * g.astype(jnp.float32)).astype(x.dtype)


def layernorm(x, g, b):
    xf = x.astype(jnp.float32)
    mu = jnp.mean(xf, axis=-1, keepdims=True)
    xc = xf - mu
    var = jnp.mean(xc * xc, axis=-1, keepdims=True)
    y = xc * lax.rsqrt(var + LN_EPS) * g.astype(jnp.float32) + b.astype(jnp.float32)
    return y.astype(x.dtype)


def token_shift(p):
    return jnp.pad(p[:, :-1], ((0, 0), (1, 0), (0, 0)))


def rwkv7_recurrence(r, w, k, v, a, b):
    B, T, H, N = r.shape
    nc = T // CHUNK

    def to_chunks(t):
        return jnp.moveaxis(t.astype(jnp.float32), 1, 0).reshape(nc, CHUNK, B, H, N)

    xs = tuple(to_chunks(t) for t in (r, w, k, v, a, b))

    def step(S, inp):
        r_t, w_t, k_t, v_t, a_t, b_t = inp
        Sa = jnp.einsum('bhij,bhj->bhi', S, a_t)
        S = (S * w_t[:, :, None, :]
             + Sa[..., None] * b_t[:, :, None, :]
             + v_t[..., None] * k_t[:, :, None, :])
        y_t = jnp.einsum('bhij,bhj->bhi', S, r_t)
        return S, y_t

    def chunk_step(S, chunk_inp):
        return lax.scan(step, S, chunk_inp)

    S0 = jnp.zeros((B, H, N, N), jnp.float32)
    _, ys = lax.scan(chunk_step, S0, xs)
    return jnp.moveaxis(ys.reshape(T, B, H, N), 0, 1)


def hybrid_mixer(h, w_in, mu_shift, w0, w_up, a0, a_up, g_up, k_k, k_a, r_k,
                 gn_g, gn_b, dw_w, dw_b, cln_g, cln_b, w_out):
    B, T, _ = h.shape
    dt = h.dtype
    p = h @ w_in
    pr = p[..., :RWKV_COLS]
    pc = p[..., RWKV_COLS:]

    pr = pr + (token_shift(pr) - pr) * mu_shift
    r = pr[..., OFF_R:OFF_K]
    k = pr[..., OFF_K:OFF_V]
    v = pr[..., OFF_V:OFF_WD]
    wd = pr[..., OFF_WD:OFF_AD]
    ad = pr[..., OFF_AD:OFF_GD]
    gd = pr[..., OFF_GD:RWKV_COLS]

    w_log = -jax.nn.softplus(-(w0 + jnp.tanh(wd) @ w_up)) - 0.5
    decay = jnp.exp(-jnp.exp(w_log.astype(jnp.float32)))
    a = jax.nn.sigmoid(a0 + ad @ a_up)
    g = jax.nn.sigmoid(gd) @ g_up

    hs = (B, T, RWKV_HEADS, RWKV_HEAD)
    kk = (k * k_k).reshape(hs).astype(jnp.float32)
    kk = kk / jnp.maximum(jnp.linalg.norm(kk, axis=-1, keepdims=True), L2_EPS)
    k = k * (1.0 + (a - 1.0) * k_a)

    r_h = r.reshape(hs)
    k_h = k.reshape(hs)
    v_h = v.reshape(hs)
    a_h = a.reshape(hs).astype(jnp.float32)
    y = rwkv7_recurrence(r_h, decay.reshape(hs), k_h, v_h, -kk, kk * a_h)

    mu = jnp.mean(y, axis=-1, keepdims=True)
    yc = y - mu
    y = yc * lax.rsqrt(jnp.mean(yc * yc, axis=-1, keepdims=True) + GN_EPS)
    y = y.reshape(B, T, RWKV_WIDTH) * gn_g.astype(jnp.float32) + gn_b.astype(jnp.float32)
    bonus = (jnp.sum(r_h * k_h * r_k, axis=-1, keepdims=True) * v_h).reshape(B, T, RWKV_WIDTH)
    y_rwkv = ((y.astype(dt) + bonus) * g).astype(dt)

    u = pc[..., :CONV_WIDTH] * jax.nn.sigmoid(pc[..., CONV_WIDTH:])
    u = lax.conv_general_dilated(
        u, dw_w[:, None, :].astype(u.dtype), window_strides=(1,),
        padding=[(CONV_K - 1, 0)], dimension_numbers=('NWC', 'WIO', 'NWC'),
        feature_group_count=CONV_WIDTH) + dw_b
    u = jax.nn.silu(layernorm(u, cln_g, cln_b))

    mixed = jnp.concatenate([y_rwkv, u.astype(dt)], axis=-1)
    return mixed @ w_out


def squared_relu_mlp(h, w1, w2):
    return jnp.square(jax.nn.relu(h @ w1)) @ w2


def setup_inputs(seed: int = 0) -> dict:
    key = jax.random.key(seed)
    ks = jax.random.split(key, 32)
    f32 = jnp.float32
    L = DEPTH

    def nrm(k, shape, scale):
        return jax.random.normal(k, shape, f32) * scale

    decay_base = jnp.linspace(-6.0, -1.0, RWKV_WIDTH, dtype=f32)
    return {
        "x": nrm(ks[0], (BATCH, SEQ, D_MODEL), 1.0),
        "norm1_g": 1.0 + nrm(ks[1], (L, D_MODEL), 0.02),
        "w_in": nrm(ks[2], (L, D_MODEL, IN_COLS), D_MODEL ** -0.5),
        "mu_shift": jax.random.uniform(ks[3], (L, RWKV_COLS), f32),
        "w0": decay_base[None, :] + nrm(ks[4], (L, RWKV_WIDTH), 0.1),
        "w_up": nrm(ks[5], (L, DECAY_LORA, RWKV_WIDTH), 0.5 * DECAY_LORA ** -0.5),
        "a0": nrm(ks[6], (L, RWKV_WIDTH), 0.1),
        "a_up": nrm(ks[7], (L, ICLR_LORA, RWKV_WIDTH), 0.5 * ICLR_LORA ** -0.5),
        "g_up": nrm(ks[8], (L, GATE_LORA, RWKV_WIDTH), GATE_LORA ** -0.5),
        "k_k": 0.85 + nrm(ks[9], (L, RWKV_WIDTH), 0.02),
        "k_a": 1.0 + nrm(ks[10], (L, RWKV_WIDTH), 0.02),
        "r_k": -0.04 + nrm(ks[11], (L, RWKV_HEADS, RWKV_HEAD), 0.1),
        "gn_g": 1.0 + nrm(ks[12], (L, RWKV_WIDTH), 0.02),
        "gn_b": nrm(ks[13], (L, RWKV_WIDTH), 0.02),
        "dw_w": nrm(ks[14], (L, CONV_K, CONV_WIDTH), CONV_K ** -0.5),
        "dw_b": nrm(ks[15], (L, CONV_WIDTH), 0.02),
        "cln_g": 1.0 + nrm(ks[16], (L, CONV_WIDTH), 0.02),
        "cln_b": nrm(ks[17], (L, CONV_WIDTH), 0.02),
        "w_out": nrm(ks[18], (L, MIX_WIDTH, D_MODEL), MIX_WIDTH ** -0.5),
        "norm2_g": 1.0 + nrm(ks[19], (L, D_MODEL), 0.02),
        "w_ff1": nrm(ks[20], (L, D_MODEL, D_FF), D_MODEL ** -0.5),
        "w_ff2": nrm(ks[21], (L, D_FF, D_MODEL), D_FF ** -0.5),
        "final_g": 1.0 + nrm(ks[22], (D_MODEL,), 0.02),
    }


def reference(x, norm1_g, w_in, mu_shift, w0, w_up, a0, a_up, g_up, k_k, k_a, r_k,
              gn_g, gn_b, dw_w, dw_b, cln_g, cln_b, w_out, norm2_g, w_ff1, w_ff2,
              final_g):
    for l in range(DEPTH):
        h = rmsnorm(x, norm1_g[l])
        x = x + hybrid_mixer(h, w_in[l], mu_shift[l], w0[l], w_up[l], a0[l], a_up[l],
                             g_up[l], k_k[l], k_a[l], r_k[l], gn_g[l], gn_b[l],
                             dw_w[l], dw_b[l], cln_g[l], cln_b[l], w_out[l])
        h = rmsnorm(x, norm2_g[l])
        x = x + squared_relu_mlp(h, w_ff1[l], w_ff2[l])
    return rmsnorm(x, final_g)
```

```python
import numpy as np
from contextlib import ExitStack
import concourse.bass as bass
import concourse.mybir as mybir
from concourse.bass_utils import run_bass_kernel_spmd

F32 = mybir.dt.float32
BF16 = mybir.dt.bfloat16
AF = mybir.ActivationFunctionType
ALU = mybir.AluOpType
AX = mybir.AxisListType

D = 1024
SEQ = 2048
L = 2
INC = 2816
DFF = 4096
TB = 256
NBLK = SEQ // TB
NCH = TB // 64
C0 = float(np.exp(-0.5))

SAME_ENGINE_SYNC = True
import os as _os
STOP = float(_os.environ.get('DEV_STOP', '99'))
SEM_ROTATE = 20000

CO = {}
_o = 0
for _n, _w in (("g1", 8), ("g2", 8), ("mu", 14), ("w0", 4), ("a0", 4), ("kk", 4), ("ka", 4), ("rk", 4),
               ("gng", 4), ("gnb", 4), ("dwb", 4), ("clg", 4), ("clb", 4), ("dww", 124)):
    CO[_n] = _o
    _o += _w
NCST = _o


class Prog:
    QUEUES = ("pe", "act", "dve", "pool", "sp")

    def __init__(self, nc):
        self.nc = nc
        self.ops = []
        self.last_w = {}
        self.readers = {}
        self.last_q = {}
        self.last_dma = {}

    @staticmethod
    def _bank(k):
        if isinstance(k, tuple) and k[0] in ("psA", "psB"):
            return k
        if k in ("psE", "psE2"):
            return "psE"
        if k in ("psCx", "psCu", "psCs"):
            return "psC"
        if isinstance(k, tuple) and k[0] == "psDy":
            return "psD"
        return None

    def add(self, eng, fn, r=(), w=(), dma=None, extra=()):
        i = len(self.ops)
        deps = set(extra)
        banks = [self._bank(k) for k in list(r) + list(w)]
        banks = [b for b in banks if b is not None]
        if banks:
            r = [k for k in r if self._bank(k) is None]
            w = [k for k in w if self._bank(k) is None] + sorted(set(banks), key=str)
        for k in r:
            j = self.last_w.get(k)
            if j is not None:
                deps.add(j)
        for k in w:
            j = self.last_w.get(k)
            if j is not None:
                deps.add(j)
            deps.update(self.readers.get(k, ()))
        for k in r:
            self.readers.setdefault(k, []).append(i)
        for k in w:
            self.last_w[k] = i
            self.readers[k] = []
        deps.discard(i)
        self.ops.append(dict(eng=eng, fn=fn, deps=deps, dma=dma, sig=False))
        if dma is not None:
            self.last_dma[dma] = i
        elif fn is not None:
            self.last_q[eng] = i
        return i

    def barrier(self):
        ex = set(self.last_q.values()) | set(self.last_dma.values())
        for q in self.QUEUES:
            self.add(q, None, extra=ex)

    def finalize(self, stack):
        nc = self.nc
        ops = self.ops
        for i, o in enumerate(ops):
            for j in o["deps"]:
                d = ops[j]
                if d["dma"] is not None:
                    continue
                if d["eng"] == o["eng"] and (d["eng"] == "pe" or not SAME_ENGINE_SYNC):
                    continue
                d["sig"] = True
        eng_sem, eng_cnt, dma_sem, dma_cnt = {}, {}, {}, {}
        nsem = [0]

        def new_sem(name):
            nsem[0] += 1
            return stack.enter_context(nc.semaphore(name))

        for i, o in enumerate(ops):
            if o["dma"] is not None:
                key = o["dma"]
                if key not in dma_sem:
                    dma_sem[key] = new_sem("d%d" % len(dma_sem))
                    dma_cnt[key] = 0
                dma_cnt[key] += 16
                o["sem"] = dma_sem[key]
                o["val"] = dma_cnt[key]
            elif o["sig"]:
                e = o["eng"]
                if e not in eng_sem or eng_cnt[e] >= SEM_ROTATE:
                    eng_sem[e] = new_sem("e%s%d" % (e, nsem[0]))
                    eng_cnt[e] = 0
                eng_cnt[e] += 1
                o["sem"] = eng_sem[e]
                o["val"] = eng_cnt[e]
        known = {q: {} for q in self.QUEUES}
        latest_dma = {}
        nwaits = 0
        for i, o in enumerate(ops):
            waits = {}
            for j in o["deps"]:
                d = ops[j]
                if d["dma"] is not None:
                    sem, val = d["sem"], latest_dma[d["dma"]]
                else:
                    if not d["sig"]:
                        continue
                    sem, val = d["sem"], d["val"]
                sid = id(sem)
                if sid not in waits or waits[sid][1] < val:
                    waits[sid] = (sem, val)
            kn = known[o["eng"]]
            wl = []
            for sid, (sem, val) in waits.items():
                if kn.get(sid, 0) >= val:
                    continue
                kn[sid] = val
                wl.append((sem, val))
            o["waits"] = wl
            nwaits += len(wl)
            if o["dma"] is not None:
                latest_dma[o["dma"]] = o["val"]
        self.stats = dict(n_ops=len(ops), n_sems=nsem[0], n_waits=nwaits,
                          per_eng={q: sum(1 for o in ops if o["eng"] == q) for q in self.QUEUES})
        return self.stats

    def emit_queue(self, q, eng):
        for o in self.ops:
            if o["eng"] != q:
                continue
            for sem, val in o["waits"]:
                eng.wait_ge(sem, val)
            if o["fn"] is None:
                continue
            ins = o["fn"](eng)
            if o["dma"] is not None:
                ins.then_inc(o["sem"], 16)
            elif o["sig"]:
                ins.then_inc(o["sem"], 1)

    def emit(self):
        with self.nc.Block() as block:
            @block.tensor
            def _(e):
                self.emit_queue("pe", e)

            @block.scalar
            def _(e):
                self.emit_queue("act", e)

            @block.vector
            def _(e):
                self.emit_queue("dve", e)

            @block.gpsimd
            def _(e):
                self.emit_queue("pool", e)

            @block.sync
            def _(e):
                self.emit_queue("sp", e)


def build_nc(nseq=2, nlayers=L, dbg=None):
    nc = bass.Bass("TRN2", target_bir_lowering=False)
    x_d = nc.dram_tensor("x", [nseq, SEQ, D], F32, kind="ExternalInput").ap()
    cst_d = nc.dram_tensor("cst", [L, 128, NCST], F32, kind="ExternalInput").ap()
    fg_d = nc.dram_tensor("fg", [128, 8], F32, kind="ExternalInput").ap()
    waup_d = nc.dram_tensor("waup", [L, 2, 128, 512], F32, kind="ExternalInput").ap()
    gup_d = nc.dram_tensor("gup", [L, 128, 512], F32, kind="ExternalInput").ap()
    win_d = nc.dram_tensor("w_in", [L, D, INC], F32, kind="ExternalInput").ap()
    wout_d = nc.dram_tensor("w_out", [L, D, D], F32, kind="ExternalInput").ap()
    w1_d = nc.dram_tensor("w_ff1", [L, D, DFF], F32, kind="ExternalInput").ap()
    w2_d = nc.dram_tensor("w_ff2", [L, DFF, D], F32, kind="ExternalInput").ap()
    y_d = nc.dram_tensor("y", [nseq, SEQ, D], F32, kind="ExternalOutput").ap()
    dbg_d = None
    if dbg:
        dbg_d = nc.dram_tensor("dbg", [dbg, 128, SEQ], F32, kind="ExternalOutput").ap()

    st = ExitStack()
    with st:
        def sb(name, shape, dt=F32):
            return st.enter_context(nc.sbuf_tensor(name, shape, dt))

        def psum(name, dt=F32):
            return st.enter_context(nc.psum_tensor(name, [128, 512], dt))

        P = Prog(nc)

        def TT(eng, out, a, b, op, r, w):
            P.add(eng, lambda e: e.tensor_tensor(out=out, in0=a, in1=b, op=op), r, w)

        def TS(eng, out, a, s1, op0, r, w, s2=None, op1=None):
            if op1 is None:
                P.add(eng, lambda e: e.tensor_scalar(out=out, in0=a, scalar1=s1, scalar2=None, op0=op0), r, w)
            else:
                P.add(eng, lambda e: e.tensor_scalar(out=out, in0=a, scalar1=s1, scalar2=s2, op0=op0, op1=op1), r, w)

        def STT(out, a, s, b, op0, op1, r, w):
            P.add("dve", lambda e: e.scalar_tensor_tensor(out=out, in0=a, scalar=s, in1=b, op0=op0, op1=op1), r, w)

        def ACT(out, in_, func, r, w, bias=None, scale=None):
            kw = {}
            if bias is not None:
                kw["bias"] = bias
            if scale is not None:
                kw["scale"] = scale
            P.add("act", lambda e: e.activation(out=out, in_=in_, func=func, **kw), r, w)

        def MM(ps, lhsT, rhs, start, stop, r, w):
            P.add("pe", lambda e: e.matmul(ps, lhsT, rhs, start=start, stop=stop), r, w)

        def MEMSET(eng, ap, val, w):
            P.add(eng, lambda e: e.memset(ap, val), (), w)

        def POW(out, a, r, w, n):
            P.add("pool", lambda e: e.tensor_tensor(out=out, in0=a, in1=nhalf[0:a.shape[0], 0:1].broadcast_to([a.shape[0], n]), op=ALU.pow),
                  list(r) + ["nhalf"], w)

        xT = sb("xT", [128, 8, SEQ])
        arena = sb("arena", [128, 32768], BF16)
        Win = arena[:, 0:8 * INC].rearrange("p (k n) -> p k n", k=8)
        Wout = arena[:, 8 * INC:8 * INC + 8 * D].rearrange("p (k n) -> p k n", k=8)
        h2T = arena[:, 0:8 * SEQ].rearrange("p (k n) -> p k n", k=8)
        W1c = [arena[:, 8 * SEQ + i * 4096: 8 * SEQ + (i + 1) * 4096].rearrange("p (k n) -> p k n", k=8) for i in range(2)]
        W2c = [arena[:, 8 * SEQ + 8192 + i * 4096: 8 * SEQ + 8192 + (i + 1) * 4096].rearrange("p (k n) -> p k n", k=4)
               for i in range(2)]
        cst = sb("cst_s", [128, NCST])
        cst2 = sb("cst2", [128, 16])
        fg = sb("fgs", [128, 8])
        waup = sb("waup_s", [128, 2, 512], BF16)
        gup = sb("gup_s", [128, 512], BF16)
        identb = sb("identb", [128, 128], BF16)
        identf = sb("identf", [128, 128])
        onesf = sb("onesf", [128, 128])
        onesb = sb("onesb", [128, 128], BF16)
        blk1 = sb("blk1", [128, 128], BF16)
        stackI = sb("stackI", [128, 64], BF16)
        m_su = sb("m_su", [128, 128])
        m_ue = sb("m_ue", [128, 128])
        m_sl = sb("m_sl", [128, 128])
        m01 = sb("m01", [128, TB])
        nhalf = sb("nhalf", [128, 2])

        NF, NB_ = 8064, 19456
        scrF = sb("scrF", [128, NF])
        scrB = sb("scrB", [128, NB_], BF16)
        aoff = {"F": 0, "B": 0}

        def areset():
            aoff["F"] = 0
            aoff["B"] = 0

        def take(shape, dt=F32):
            kind = "F" if dt == F32 else "B"
            t, cap = (scrF, NF) if kind == "F" else (scrB, NB_)
            size = int(np.prod(shape[1:]))
            o = aoff[kind]
            aoff[kind] = o + ((size + 15) // 16) * 16
            assert aoff[kind] <= cap, (kind, aoff[kind], cap)
            a = t[:, o:o + size]
            if len(shape) == 3:
                a = a.rearrange("p (a b) -> p a b", a=shape[1])
            return a

        def c_(name, j=0):
            o = CO[name] + j
            return cst[:, o:o + 1]

        areset()
        hT = take([128, 8, TB], BF16)
        mixT = take([128, 8, TB], BF16)
        gT = take([128, 4, TB], BF16)
        bon = take([128, 4, TB])
        ubuf = take([128, 4, 30 + TB], BF16)
        big1 = take([128, 4, TB])
        big2 = take([128, 4, TB])
        cvz = big1
        cvb = big2
        tR = big1[:, 0, :]; tK = big1[:, 1, :]; tV = big1[:, 2, :]; lw = big1[:, 3, :]
        aa = big2[:, 0, :]; cs = big2[:, 1, :]; csm = big2[:, 2, :]; Epos = big2[:, 3, :]
        cvt = take([128, 4, TB], BF16)
        carry = take([128, 16])
        praw = [take([128, TB + 2]) for i in range(2)]
        dtmp = take([128, TB])
        lwin = take([128, TB], BF16)
        gsb = take([128, TB], BF16)
        gdm = take([128, TB])
        sqk = [take([128, TB], BF16) for i in range(2)]
        rstd = take([128, TB])
        mean = take([128, TB])
        diag = [take([128, 128], BF16) for i in range(6)]
        Eneg = take([128, TB]); Eprev = take([128, TB])
        kkt = take([128, TB]); k2 = take([128, TB]); tmp1 = take([128, TB]); tmp2 = take([128, TB])
        rn = take([128, TB])
        sqb = take([128, TB], BF16); rkb = take([128, TB], BF16)
        XN = ("ATx", "BTx", "KTx", "RTx", "KhTx", "BhTx", "vTx")
        X = {n: take([128, NCH, 128], BF16) for n in XN}
        Pm = [take([128, NCH, 128], BF16) for i in range(2)]
        Ptm = [take([128, NCH, 128], BF16) for i in range(2)]
        MrbT = take([128, NCH, 128], BF16)
        LakT = take([128, NCH, 128], BF16)
        MrkT = take([128, NCH, 128], BF16)
        TTf = take([128, NCH, 128])
        TTb = take([128, NCH, 128], BF16)
        Khx = take([128, NCH, 128], BF16)
        Bhx = take([128, NCH, 128], BF16)
        V2 = take([128, NCH, 64], BF16)
        X1b = take([128, 64], BF16)
        Ub = take([128, 64], BF16)
        STf = take([128, 4, 64])
        STb = take([128, 4, 64], BF16)
        ysq = take([128, NCH, 64])
        ycen = take([128, NCH, 64])
        ystat = take([128, 16])
        ynx = take([128, NCH, 128], BF16)
        t3 = take([128, TB])
        print("mixer scratch", dict(aoff))
        areset()
        fsq = take([128, 512], BF16)
        frl = take([128, 512])
        fT = [take([128, 4, 512], BF16) for i in range(2)]
        rs5 = take([128, 512])
        areset()
        xin = [take([128, D]) for i in range(2)]
        areset()
        yout = [take([128, D]) for i in range(2)]
        hn = take([128, 8, 128])
        rstd_s = take([128, 128])
        sqk_s = [take([128, 128], BF16) for i in range(2)]

        psA = [psum("psA%d" % i) for i in range(3)]
        psB = [psum("psB%d" % i) for i in range(2)]
        psC = psum("psC")
        psD = psum("psD")
        psE = psum("psE")
        cnt = {"A": 0, "B": 0, "praw": 0, "sqk": 0, "diag": 0, "xin": 0, "yout": 0}

        def nextA():
            i = cnt["A"] % 3
            cnt["A"] += 1
            return psA[i], ("psA", i)

        def nextB():
            i = cnt["B"] % 2
            cnt["B"] += 1
            return psB[i], ("psB", i)

        MEMSET("dve", onesf[:, :], 1.0, ["onesf"])
        MEMSET("dve", onesb[:, :], 1.0, ["onesb"])
        MEMSET("dve", nhalf[:, :], -0.5, ["nhalf"])
        MEMSET("dve", m01[:, :], 1.0, ["m01"])
        MEMSET("dve", m01[:, :].rearrange("p (c t) -> p c t", t=64)[:, :, 0:1], 0.0, ["m01"])
        MEMSET("dve", blk1[:, :], 0.0, ["blk1"])
        MEMSET("dve", blk1[0:64, 0:64], 1.0, ["blk1"])
        MEMSET("dve", blk1[64:128, 64:128], 1.0, ["blk1"])
        P.add("pool", lambda e: e.affine_select(out=m_su[:, :], in_=onesf[:, :], pattern=[[1, 128]], compare_op=ALU.is_gt, fill=0.0,
                                                base=0, channel_multiplier=-1), ["onesf"], ["m_su"])
        P.add("pool", lambda e: e.affine_select(out=m_ue[:, :], in_=onesf[:, :], pattern=[[1, 128]], compare_op=ALU.is_ge, fill=0.0,
                                                base=0, channel_multiplier=-1), ["onesf"], ["m_ue"])
        P.add("pool", lambda e: e.affine_select(out=m_sl[:, :], in_=onesf[:, :], pattern=[[-1, 128]], compare_op=ALU.is_gt, fill=0.0,
                                                base=0, channel_multiplier=1), ["onesf"], ["m_sl"])
        TT("dve", identf[:, :], m_ue[:, :], m_su[:, :], ALU.subtract, ["m_ue", "m_su"], ["identf"])
        P.add("dve", lambda e: e.tensor_copy(out=identb[:, :], in_=identf[:, :]), ["identf"], ["identb"])
        P.add("dve", lambda e: e.tensor_copy(out=stackI[0:64, :], in_=identf[0:64, 0:64]), ["identf"], ["stackI"])
        P.add("dve", lambda e: e.tensor_copy(out=stackI[64:128, :], in_=identf[64:128, 64:128]), ["identf"], ["stackI"])
        for n in XN:
            MEMSET("dve", X[n][:, :, :], 0.0, [n])
        MEMSET("dve", ynx[:, :, :], 0.0, ["ynx"])
        P.add("sp", lambda e: e.dma_start(out=fg[:, :], in_=fg_d[:, :]), (), ["fg"], dma="c0")

        def dbg_dump(idx, ap, keys, n):
            if dbg_d is None:
                return
            P.add("sp", lambda e: e.dma_start(out=dbg_d[idx, 0:ap.shape[0], 0:n], in_=ap), keys, [("dbg", idx)], dma="dbg")

        def rms_stats(t0, n, eps_rs, rs_out, rs_key, sq_bufs=None):
            ps = psE
            for k in range(8):
                i = cnt["sqk"] % 2
                cnt["sqk"] += 1
                if sq_bufs is not None:
                    s = sq_bufs[i][:, 0:n]
                    skey = ("sqs", i)
                elif n > TB:
                    s = fsq[:, 0:n]
                    skey = "fsq"
                else:
                    s = sqk[i][:, 0:n]
                    skey = ("sqk", i)
                ACT(s, xT[:, k, t0:t0 + n], AF.Square, [("xT", k)], [skey])
                MM(ps[:, 0:n], onesb[:, :], s, k == 0, k == 7, ["onesb", skey], ["psE"])
            TS("dve", rs_out, ps[:, 0:n], 1.0 / D, ALU.mult, ["psE"], [rs_key], s2=1e-5, op1=ALU.add)

        def load_seq(s):
            for tt_ in range(SEQ // 128):
                i = cnt["xin"] % 2
                cnt["xin"] += 1
                P.add("sp", lambda e, i=i, tt_=tt_: e.dma_start(out=xin[i][:, :], in_=x_d[s, tt_ * 128:(tt_ + 1) * 128, :]),
                      (), [("xin", i)], dma=("xin", i))
                for half in range(2):
                    ps, pk = nextA()
                    for k4 in range(4):
                        k = half * 4 + k4
                        P.add("pe", lambda e, ps=ps, k=k, k4=k4, i=i: e.transpose(ps[:, k4 * 128:(k4 + 1) * 128], xin[i][:, k * 128:(k + 1) * 128], identf[:, :]),
                              [("xin", i), "identf"], [pk])
                    P.add("act", lambda e, ps=ps, half=half, tt_=tt_: e.activation(
                        out=xT[:, half * 4:half * 4 + 4, tt_ * 128:(tt_ + 1) * 128],
                        in_=ps[:, :].rearrange("p (k t) -> p k t", k=4), func=AF.Copy),
                        [pk], [("xT", half * 4 + j) for j in range(4)])

        def store_seq(s):
            for tt_ in range(SEQ // 128):
                t0 = tt_ * 128
                rms_stats(t0, 128, 1e-5, rstd_s[:, :], "rstd_s", sq_bufs=sqk_s)
                POW(rstd_s[:, :], rstd_s[:, :], ["rstd_s"], ["rstd_s"], 128)
                for k in range(8):
                    STT(hn[:, k, :], xT[:, k, t0:t0 + 128], fg[:, k:k + 1], rstd_s[:, :], ALU.mult, ALU.mult,
                        [("xT", k), "fg", "rstd_s"], [("hn", k)])
                i = cnt["yout"] % 2
                cnt["yout"] += 1
                for half in range(2):
                    ps, pk = nextA()
                    for k4 in range(4):
                        k = half * 4 + k4
                        P.add("pe", lambda e, ps=ps, k=k, k4=k4: e.transpose(ps[:, k4 * 128:(k4 + 1) * 128], hn[:, k, :], identf[:, :]),
                              [("hn", k), "identf"], [pk])
                    P.add("act", lambda e, ps=ps, half=half, i=i: e.activation(out=yout[i][:, half * 512:(half + 1) * 512], in_=ps[:, :], func=AF.Copy),
                          [pk], [("yout", i)])
                P.add("sp", lambda e, i=i, tt_=tt_: e.dma_start(out=y_d[s, tt_ * 128:(tt_ + 1) * 128, :], in_=yout[i][:, :]),
                      [("yout", i)], [("y", s, tt_)], dma=("yout", i))

        def load_layer_consts(l):
            P.add("sp", lambda e: e.dma_start(out=cst[:, :], in_=cst_d[l, :, :]), (), ["cst"], dma="c0")
            for j in range(2):
                P.add("pool", lambda e, j=j: e.dma_start(out=waup[:, j, :], in_=waup_d[l, j, :, :]), (), ["waup"], dma="c1")
            P.add("pool", lambda e: e.dma_start(out=gup[:, :], in_=gup_d[l, :, :]), (), ["gup"], dma="c1")
            TS("dve", cst2[:, 0:4], cst[:, CO["w0"]:CO["w0"] + 4], 0.5, ALU.mult, ["cst"], ["cst2"])
            TS("dve", cst2[:, 4:8], cst[:, CO["a0"]:CO["a0"] + 4], 0.5, ALU.mult, ["cst"], ["cst2"])

        def load_mixer_weights(l):
            wv = win_d[l].rearrange("(k p) n -> p k n", p=128)
            for k in range(8):
                for c0 in range(0, INC, 704):
                    P.add("pool", lambda e, k=k, c0=c0: e.dma_start(out=Win[:, k, c0:c0 + 704], in_=wv[:, k, c0:c0 + 704]),
                          (), [("Win", k)], dma="win")
            wo = wout_d[l].rearrange("(k p) n -> p k n", p=128)
            for k in range(8):
                P.add("pool", lambda e, k=k: e.dma_start(out=Wout[:, k, :], in_=wo[:, k, :]), (), [("Wout", k)], dma="wout")

        CVB = ["aa", "cs", "csm", "Epos"]
        CVZ = ["tR", "tK", "tV", "lw"]

        def mixer_block(l, b, first):
            t0 = b * TB
            xk = [("xT", k) for k in range(8)]
            rms_stats(t0, TB, 1e-5, rstd[:, :], "rstd")
            POW(rstd[:, :], rstd[:, :], ["rstd"], ["rstd"], TB)
            for k in range(8):
                STT(hT[:, k, :], xT[:, k, t0:t0 + TB], c_("g1", k), rstd[:, :], ALU.mult, ALU.mult,
                    [("xT", k), "cst", "rstd"], [("hT", k)])
            if STOP <= 1:
                return

            def inproj(c):
                ps, pk = nextA()
                for k in range(8):
                    MM(ps[:, 0:TB], Win[:, k, c * 128:(c + 1) * 128], hT[:, k, :], k == 0, k == 7, [("Win", k), ("hT", k)], [pk])
                return ps, pk

            def shift_mix(c, ps, pk, dst, dkey):
                i = cnt["praw"] % 2
                cnt["praw"] += 1
                pr = praw[i]
                prk = ("praw", i)
                ACT(pr[:, 1:TB + 1], ps[:, 0:TB], AF.Copy, [pk], [prk])
                ACT(pr[:, 0:1], carry[:, c:c + 1], AF.Copy, [("carry", c)], [prk])
                ACT(carry[:, c:c + 1], pr[:, TB:TB + 1], AF.Copy, [prk], [("carry", c)])
                TT("dve", dtmp[:, :], pr[:, 0:TB], pr[:, 1:TB + 1], ALU.subtract, [prk], ["dtmp"])
                STT(dst, dtmp[:, :], c_("mu", c), pr[:, 1:TB + 1], ALU.mult, ALU.add, ["dtmp", "cst", prk], [dkey])

            ps, pk = inproj(12)
            shift_mix(12, ps, pk, tmp1[:, :], "tmp1")
            ACT(lwin[0:64, :], tmp1[0:64, :], AF.Tanh, ["tmp1"], ["lwin"])
            ACT(lwin[64:128, :], tmp1[64:128, :], AF.Copy, ["tmp1"], ["lwin"])
            ps, pk = inproj(13)
            shift_mix(13, ps, pk, gdm[:, :], "gdm")
            ACT(gdm[:, :], gdm[:, :], AF.Tanh, ["gdm"], ["gdm"], scale=0.5)
            TS("dve", gsb[:, :], gdm[:, :], 0.5, ALU.mult, ["gdm"], ["gsb"], s2=0.5, op1=ALU.add)

            if STOP <= 2:
                return
            for c in range(4):
                psv, pkv = inproj(14 + c)
                psg, pkg = inproj(18 + c)
                ACT(tmp2[:, :], psg[:, 0:TB], AF.Tanh, [pkg], ["tmp2"], scale=0.5)
                TS("dve", tmp2[:, :], tmp2[:, :], 0.5, ALU.mult, ["tmp2"], ["tmp2"], s2=0.5, op1=ALU.add)
                TT("dve", ubuf[:, c, 30:30 + TB], psv[:, 0:TB], tmp2[:, :], ALU.mult, [pkv, "tmp2"], [("ubuf", c)])
            for c in range(4):
                ps, pk = nextA()
                for kt in range(31):
                    i = cnt["diag"] % 6
                    cnt["diag"] += 1
                    o = CO["dww"] + c * 31 + kt
                    TS("pool", diag[i][:, :], identf[:, :], cst[:, o:o + 1], ALU.mult, ["identf", "cst"], [("diag", i)])
                    MM(ps[:, 0:TB], diag[i][:, :], ubuf[:, c, kt:kt + TB], kt == 0, kt == 30, [("diag", i), ("ubuf", c)], [pk])
                ACT(cvb[:, c, :], ps[:, 0:TB], AF.Identity, [pk, "cst"], [CVB[c]], bias=c_("dwb", c))
                ACT(cvt[:, c, :], cvb[:, c, :], AF.Copy, [CVB[c]], [("cvt", c)])
                ACT(ubuf[:, c, 0:30], ubuf[:, c, TB:TB + 30], AF.Copy, [("ubuf", c)], [("ubuf", c)])
            cvk = list(CVB)
            for c in range(4):
                MM(psE[:, 0:TB], onesb[:, :], cvt[:, c, :], c == 0, c == 3, ["onesb", ("cvt", c)], ["psE"])
            TS("dve", mean[:, :], psE[:, 0:TB], 1.0 / 512, ALU.mult, ["psE"], ["mean"])
            TT("dve", cvb[:, :, :], cvb[:, :, :], mean[:, :].unsqueeze(1).broadcast_to([128, 4, TB]), ALU.subtract, cvk + ["mean"], cvk)
            ACT(cvt[:, :, :], cvb[:, :, :], AF.Square, cvk, [("cvt", c) for c in range(4)])
            for c in range(4):
                MM(psE[:, 0:TB], onesb[:, :], cvt[:, c, :], c == 0, c == 3, ["onesb", ("cvt", c)], ["psE"])
            TS("dve", mean[:, :], psE[:, 0:TB], 1.0 / 512, ALU.mult, ["psE"], ["mean"], s2=1e-5, op1=ALU.add)
            POW(mean[:, :], mean[:, :], ["mean"], ["mean"], TB)
            TT("dve", cvb[:, :, :], cvb[:, :, :], mean[:, :].unsqueeze(1).broadcast_to([128, 4, TB]), ALU.mult, cvk + ["mean"], cvk)
            for c in range(4):
                TS("dve", cvb[:, c, :], cvb[:, c, :], c_("clg", c), ALU.mult, [CVB[c], "cst"], [CVB[c]], s2=c_("clb", c), op1=ALU.add)
            ACT(cvz[:, :, :], cvb[:, :, :], AF.Tanh, cvk, CVZ, scale=0.5)
            TS("dve", cvz[:, :, :], cvz[:, :, :], 0.5, ALU.mult, CVZ, CVZ, s2=0.5, op1=ALU.add)
            TT("dve", mixT[:, 4:8, :], cvb[:, :, :], cvz[:, :, :], ALU.mult, cvk + CVZ, [("mixT", 4 + c) for c in range(4)])

            if STOP <= 3:
                return
            for cc in range(4):
                ps, pk = inproj(cc)
                shift_mix(cc, ps, pk, tR[:, :], "tR")
                ps, pk = inproj(4 + cc)
                shift_mix(4 + cc, ps, pk, tK[:, :], "tK")
                ps, pk = inproj(8 + cc)
                shift_mix(8 + cc, ps, pk, tV[:, :], "tV")
                if STOP <= 3.1:
                    continue
                MM(psE[:, 0:TB], waup[:, 0, cc * 128:(cc + 1) * 128], lwin[:, :], True, True, ["waup", "lwin"], ["psE"])
                ACT(lw[:, :], psE[:, 0:TB], AF.Tanh, ["psE", "cst2"], ["lw"], bias=cst2[:, cc:cc + 1], scale=0.5)
                TS("dve", lw[:, :], lw[:, :], -0.5 * C0, ALU.mult, ["lw"], ["lw"], s2=-0.5 * C0, op1=ALU.add)
                MM(psE[:, TB:2 * TB], waup[:, 1, cc * 128:(cc + 1) * 128], lwin[:, :], True, True, ["waup", "lwin"], ["psE2"])
                ACT(aa[:, :], psE[:, TB:2 * TB], AF.Tanh, ["psE2", "cst2"], ["aa"], bias=cst2[:, 4 + cc:5 + cc], scale=0.5)
                TS("dve", aa[:, :], aa[:, :], 0.5, ALU.mult, ["aa"], ["aa"], s2=0.5, op1=ALU.add)
                psg, pkg = nextB()
                MM(psg[:, 0:TB], gup[:, cc * 128:(cc + 1) * 128], gsb[:, :], True, True, ["gup", "gsb"], [pkg])
                ACT(gT[:, cc, :], psg[:, 0:TB], AF.Copy, [pkg], [("gT", cc)])
                if STOP <= 3.2:
                    continue
                P.add("dve", lambda e: e.tensor_tensor_scan(out=cs[:, :], data0=m01[:, :], data1=lw[:, :], initial=0.0, op0=ALU.mult, op1=ALU.add),
                      ["m01", "lw"], ["cs"])
                TT("dve", csm[:, :], cs[:, :], lw[:, :], ALU.subtract, ["cs", "lw"], ["csm"])
                ACT(Epos[:, :], cs[:, :], AF.Exp, ["cs"], ["Epos"])
                ACT(Eneg[:, :], cs[:, :], AF.Exp, ["cs"], ["Eneg"], scale=-1.0)
                ACT(Eprev[:, :], csm[:, :], AF.Exp, ["csm"], ["Eprev"])
                if STOP <= 3.3:
                    continue
                ACT(sqb[:, :], tK[:, :], AF.Square, ["tK", "cst"], ["sqb"], scale=c_("kk", cc))
                psn, pkn = nextB()
                MM(psn[:, 0:TB], blk1[:, :], sqb[:, :], True, True, ["blk1", "sqb"], [pkn])
                TS("dve", rn[:, :], psn[:, 0:TB], 1e-24, ALU.max, [pkn], ["rn"])
                POW(rn[:, :], rn[:, :], ["rn"], ["rn"], TB)
                STT(kkt[:, :], tK[:, :], c_("kk", cc), rn[:, :], ALU.mult, ALU.mult, ["tK", "cst", "rn"], ["kkt"])
                if STOP <= 3.4:
                    continue
                TS("dve", tmp1[:, :], aa[:, :], -1.0, ALU.add, ["aa", "cst"], ["tmp1"], s2=c_("ka", cc), op1=ALU.mult)
                STT(k2[:, :], tmp1[:, :], 1.0, tK[:, :], ALU.add, ALU.mult, ["tmp1", "tK"], ["k2"])
                STT(rkb[:, :], tR[:, :], c_("rk", cc), k2[:, :], ALU.mult, ALU.mult, ["tR", "cst", "k2"], ["rkb"])
                psb_, pkb = nextB()
                MM(psb_[:, 0:TB], blk1[:, :], rkb[:, :], True, True, ["blk1", "rkb"], [pkb])
                TT("dve", bon[:, cc, :], psb_[:, 0:TB], tV[:, :], ALU.mult, [pkb, "tV"], [("bon", cc)])
                if STOP <= 3.5:
                    continue
                TT("dve", tmp1[:, :], kkt[:, :], aa[:, :], ALU.mult, ["kkt", "aa"], ["tmp1"])
                TT("dve", tmp1[:, :], tmp1[:, :], Eneg[:, :], ALU.mult, ["tmp1", "Eneg"], ["tmp1"])
                TT("dve", tmp2[:, :], k2[:, :], Eneg[:, :], ALU.mult, ["k2", "Eneg"], ["tmp2"])
                E3 = Epos[:, :].rearrange("p (c t) -> p c t", t=64)

                def v3(t):
                    return t[:, :].rearrange("p (c t) -> p c t", t=64)
                for h in range(2):
                    sl = slice(h * 64, (h + 1) * 64)
                    cl = slice(h * 64, (h + 1) * 64)
                    wc = E3[sl, :, 63:64].broadcast_to([64, NCH, 64])
                    STT(X["ATx"][sl, :, cl], v3(kkt)[sl], -1.0, v3(Eprev)[sl], ALU.mult, ALU.mult, ["kkt", "Eprev"], ["ATx"])
                    ACT(X["BTx"][sl, :, cl], v3(tmp1)[sl], AF.Copy, ["tmp1"], ["BTx"])
                    TT("dve", X["BhTx"][sl, :, cl], v3(tmp1)[sl], wc, ALU.mult, ["tmp1", "Epos"], ["BhTx"])
                    ACT(X["KTx"][sl, :, cl], v3(tmp2)[sl], AF.Copy, ["tmp2"], ["KTx"])
                    TT("dve", X["KhTx"][sl, :, cl], v3(tmp2)[sl], wc, ALU.mult, ["tmp2", "Epos"], ["KhTx"])
                    TT("dve", X["RTx"][sl, :, cl], v3(tR)[sl], v3(Epos)[sl], ALU.mult, ["tR", "Epos"], ["RTx"])
                    ACT(X["vTx"][sl, :, cl], v3(tV)[sl], AF.Copy, ["tV"], ["vTx"])

                if STOP <= 4:
                    continue
                def intra(lname, rname, mask, dst, dkey):
                    ps, pk = nextB()
                    for c4 in range(NCH):
                        MM(ps[:, c4 * 128:(c4 + 1) * 128], X[lname][:, c4, :], X[rname][:, c4, :], True, True, [lname, rname], [pk])
                    TT("dve", dst[:, :, :], ps[:, :].rearrange("p (c t) -> p c t", t=128),
                       mask[:, :].unsqueeze(1).broadcast_to([128, NCH, 128]), ALU.mult, [pk, "m_su", "m_ue", "m_sl"], [dkey])
                intra("BTx", "ATx", m_su, Pm[0], ("Pm", 0))
                intra("ATx", "BTx", m_sl, Ptm[0], ("Ptm", 0))
                intra("BTx", "RTx", m_ue, MrbT, "MrbT")
                intra("KTx", "ATx", m_su, LakT, "LakT")
                intra("KTx", "RTx", m_ue, MrkT, "MrkT")
                TT("dve", TTf[:, :, :], Pm[0][:, :, :], identf[:, :].unsqueeze(1).broadcast_to([128, NCH, 128]), ALU.add,
                   [("Pm", 0), "identf"], ["TTf"])
                P.add("act", lambda e: e.activation(out=TTb[:, :, :], in_=TTf[:, :, :], func=AF.Copy), ["TTf"], ["TTb"])
                cur = 0
                for lev in range(5):
                    nxt = 1 - cur
                    if lev < 4:
                        ps, pk = nextB()
                        for c4 in range(NCH):
                            MM(ps[:, c4 * 128:(c4 + 1) * 128], Ptm[cur][:, c4, :], Pm[cur][:, c4, :], True, True, [("Ptm", cur), ("Pm", cur)], [pk])
                        ACT(Pm[nxt][:, :, :], ps[:, :].rearrange("p (c t) -> p c t", t=128), AF.Copy, [pk], [("Pm", nxt)])
                    ps, pk = nextB()
                    for c4 in range(NCH):
                        MM(ps[:, c4 * 128:(c4 + 1) * 128], Pm[cur][:, c4, :], Ptm[cur][:, c4, :], True, True, [("Ptm", cur), ("Pm", cur)], [pk])
                    ACT(Ptm[nxt][:, :, :], ps[:, :].rearrange("p (c t) -> p c t", t=128), AF.Copy, [pk], [("Ptm", nxt)])
                    ps, pk = nextB()
                    for c4 in range(NCH):
                        MM(ps[:, c4 * 128:(c4 + 1) * 128], Ptm[nxt][:, c4, :], TTb[:, c4, :], True, True, [("Ptm", nxt), "TTb"], [pk])
                    TT("dve", TTf[:, :, :], TTf[:, :, :], ps[:, :].rearrange("p (c t) -> p c t", t=128), ALU.add, ["TTf", pk], ["TTf"])
                    ACT(TTb[:, :, :], TTf[:, :, :], AF.Copy, ["TTf"], ["TTb"])
                    cur = nxt
                if STOP <= 5:
                    continue
                for (src, dst, dkey) in (("KhTx", Khx, "Khx"), ("BhTx", Bhx, "Bhx")):
                    ps, pk = nextB()
                    for c4 in range(NCH):
                        MM(ps[:, c4 * 128:(c4 + 1) * 128], X[src][:, c4, :], identb[:, :], True, True, [src, "identb"], [pk])
                    ACT(dst[:, :, :], ps[:, :].rearrange("p (c t) -> p c t", t=128), AF.Copy, [pk], [dkey])
                ps, pk = nextB()
                for c4 in range(NCH):
                    MM(ps[:, c4 * 64:(c4 + 1) * 64], X["vTx"][:, c4, :], stackI[:, :], True, True, ["vTx", "stackI"], [pk])
                ACT(V2[:, :, :], ps[:, 0:NCH * 64].rearrange("p (c t) -> p c t", t=64), AF.Copy, [pk], ["V2"])
                skey = ("ST", cc)
                for c4 in range(NCH):
                    MM(psC[:, 0:64], X["ATx"][:, c4, :], STb[:, cc, :], True, False, ["ATx", skey], ["psCx"])
                    MM(psC[:, 0:64], LakT[:, c4, :], V2[:, c4, :], False, True, ["LakT", "V2"], ["psCx"])
                    ACT(X1b[:, :], psC[:, 0:64], AF.Copy, ["psCx"], ["X1b"])
                    MM(psC[:, 64:128], TTb[:, c4, :], X1b[:, :], True, True, ["TTb", "X1b"], ["psCu"])
                    ACT(Ub[:, :], psC[:, 64:128], AF.Copy, ["psCu"], ["Ub"])
                    yk = ("psDy", c4)
                    MM(psD[:, c4 * 64:(c4 + 1) * 64], X["RTx"][:, c4, :], STb[:, cc, :], True, False, ["RTx", skey], [yk])
                    MM(psD[:, c4 * 64:(c4 + 1) * 64], MrbT[:, c4, :], Ub[:, :], False, False, ["MrbT", "Ub"], [yk])
                    MM(psD[:, c4 * 64:(c4 + 1) * 64], MrkT[:, c4, :], V2[:, c4, :], False, True, ["MrkT", "V2"], [yk])
                    MM(psC[:, 128:192], Khx[:, c4, :], V2[:, c4, :], True, False, ["Khx", "V2"], ["psCs"])
                    MM(psC[:, 128:192], Bhx[:, c4, :], Ub[:, :], False, True, ["Bhx", "Ub"], ["psCs"])
                    STT(STf[:, cc, :], STf[:, cc, :], Epos[:, c4 * 64 + 63:c4 * 64 + 64], psC[:, 128:192], ALU.mult, ALU.add,
                        [("STf", cc), "Epos", "psCs"], [("STf", cc)])
                    ACT(STb[:, cc, :], STf[:, cc, :], AF.Copy, [("STf", cc)], [skey])
                yks = [("psDy", c4) for c4 in range(NCH)]
                y3 = psD[:, 0:NCH * 64].rearrange("p (c t) -> p c t", t=64)
                P.add("dve", lambda e: e.tensor_reduce(out=ystat[:, 0:NCH], in_=y3, axis=AX.X, op=ALU.add), yks, ["ystat"])
                TS("dve", ystat[:, 0:NCH], ystat[:, 0:NCH], 1.0 / 64, ALU.mult, ["ystat"], ["ystat"])
                TT("dve", ycen[:, :, :], y3, ystat[:, 0:NCH].unsqueeze(2).broadcast_to([128, NCH, 64]), ALU.subtract, yks + ["ystat"], ["ycen"])
                ACT(ysq[:, :, :], ycen[:, :, :], AF.Square, ["ycen"], ["ysq"])
                P.add("dve", lambda e: e.tensor_reduce(out=ystat[:, 4:4 + NCH], in_=ysq[:, :, :], axis=AX.X, op=ALU.add), ["ysq"], ["ystat2"])
                TS("dve", ystat[:, 4:4 + NCH], ystat[:, 4:4 + NCH], 1.0 / 64, ALU.mult, ["ystat2"], ["ystat2"], s2=64e-5, op1=ALU.add)
                POW(ystat[:, 4:4 + NCH], ystat[:, 4:4 + NCH], ["ystat2"], ["ystat2"], NCH)
                for h in range(2):
                    sl = slice(h * 64, (h + 1) * 64)
                    TT("dve", ynx[sl, :, h * 64:(h + 1) * 64], ycen[sl, :, :],
                       ystat[sl, 4:4 + NCH].unsqueeze(2).broadcast_to([64, NCH, 64]), ALU.mult, ["ycen", "ystat2"], ["ynx"])
                ps, pk = nextB()
                for c4 in range(NCH):
                    MM(ps[:, c4 * 64:(c4 + 1) * 64], ynx[:, c4, :], stackI[:, :], True, True, ["ynx", "stackI"], [pk])
                TS("dve", t3[:, :], ps[:, 0:TB], c_("gng", cc), ALU.mult, [pk, "cst"], ["t3"], s2=c_("gnb", cc), op1=ALU.add)
                TT("dve", t3[:, :], t3[:, :], bon[:, cc, :], ALU.add, ["t3", ("bon", cc)], ["t3"])
                TT("dve", mixT[:, cc, :], t3[:, :], gT[:, cc, :], ALU.mult, ["t3", ("gT", cc)], [("mixT", cc)])

            if STOP <= 6:
                return
            for m in range(8):
                ps, pk = nextA()
                for k in range(8):
                    MM(ps[:, 0:TB], Wout[:, k, m * 128:(m + 1) * 128], mixT[:, k, :], k == 0, k == 7, [("Wout", k), ("mixT", k)], [pk])
                TT("dve", xT[:, m, t0:t0 + TB], xT[:, m, t0:t0 + TB], ps[:, 0:TB], ALU.add, [("xT", m), pk], [("xT", m)])

        def ffn_phase(l):
            for tg in range(4):
                t0 = tg * 512
                rms_stats(t0, 512, 1e-5, rs5[:, :], "rs5")
                for q in range(2):
                    POW(rs5[:, q * TB:(q + 1) * TB], rs5[:, q * TB:(q + 1) * TB], ["rs5"], ["rs5"], TB)
                for k in range(8):
                    STT(h2T[:, k, t0:t0 + 512], xT[:, k, t0:t0 + 512], c_("g2", k), rs5[:, :], ALU.mult, ALU.mult,
                        [("xT", k), "cst", "rs5"], [("h2T", tg)])
            w1v = w1_d[l].rearrange("(k p) n -> p k n", p=128)
            w2v = w2_d[l].rearrange("(k p) n -> p k n", p=128)
            for hc in range(8):
                i = hc % 2
                for k in range(8):
                    P.add("pool", lambda e, k=k, i=i, hc=hc: e.dma_start(out=W1c[i][:, k, :], in_=w1v[:, k, hc * 512:(hc + 1) * 512]),
                          (), [("W1c", i)], dma=("w1", i))
                for k in range(4):
                    P.add("pool", lambda e, k=k, i=i, hc=hc: e.dma_start(out=W2c[i][:, k, :], in_=w2v[:, hc * 4 + k, :]),
                          (), [("W2c", i)], dma=("w2", i))
                for tg in range(4):
                    t0 = tg * 512
                    j = (hc * 4 + tg) % 2
                    for hs in range(4):
                        ps, pk = nextA()
                        for k in range(8):
                            MM(ps[:, :], W1c[i][:, k, hs * 128:(hs + 1) * 128], h2T[:, k, t0:t0 + 512], k == 0, k == 7,
                               [("W1c", i), ("h2T", tg)], [pk])
                        ACT(frl[:, :], ps[:, :], AF.Relu, [pk], ["frl"])
                        ACT(fT[j][:, hs, :], frl[:, :], AF.Square, ["frl"], [("fT", j)])
                    for m in range(8):
                        ps, pk = nextA()
                        for hs in range(4):
                            MM(ps[:, :], W2c[i][:, hs, m * 128:(m + 1) * 128], fT[j][:, hs, :], hs == 0, hs == 3,
                               [("W2c", i), ("fT", j)], [pk])
                        TT("dve", xT[:, m, t0:t0 + 512], xT[:, m, t0:t0 + 512], ps[:, :], ALU.add, [("xT", m), pk], [("xT", m)])

        for s in range(nseq):
            load_seq(s)
            for l in range(nlayers):
                P.barrier()
                load_layer_consts(l)
                load_mixer_weights(l)
                MEMSET("dve", carry[:, :], 0.0, [("carry", c) for c in range(14)])
                MEMSET("dve", ubuf[:, :, 0:30], 0.0, [("ubuf", c) for c in range(4)])
                MEMSET("dve", STf[:, :, :], 0.0, [("STf", c) for c in range(4)])
                MEMSET("dve", STb[:, :, :], 0.0, [("ST", c) for c in range(4)])
                for b in range(NBLK if STOP >= 8 else 1):
                    mixer_block(l, b, b == 0)
                P.barrier()
                if STOP >= 9:
                    ffn_phase(l)
            P.barrier()
            store_seq(s)
        P.add("sp", None, r=[("y", s, t) for s in range(nseq) for t in range(SEQ // 128)])
        stats = P.finalize(st)
        print("PROG", stats)
        P.emit()
    return nc


def host_consts(inp):
    f = np.float32
    cst = np.zeros((L, 128, NCST), f)

    def put(l, name, vec):
        v = np.asarray(vec, f).reshape(-1, 128).T
        cst[l, :, CO[name]:CO[name] + v.shape[1]] = v
    for l in range(L):
        put(l, "g1", inp["norm1_g"][l]); put(l, "g2", inp["norm2_g"][l]); put(l, "mu", inp["mu_shift"][l])
        put(l, "w0", inp["w0"][l]); put(l, "a0", inp["a0"][l]); put(l, "kk", inp["k_k"][l]); put(l, "ka", inp["k_a"][l])
        put(l, "rk", inp["r_k"][l].reshape(-1)); put(l, "gng", inp["gn_g"][l]); put(l, "gnb", inp["gn_b"][l])
        put(l, "dwb", inp["dw_b"][l]); put(l, "clg", inp["cln_g"][l]); put(l, "clb", inp["cln_b"][l])
        dw = np.asarray(inp["dw_w"][l], f)
        cst[l, :, CO["dww"]:CO["dww"] + 124] = dw.T.reshape(4, 128, 31).transpose(1, 0, 2).reshape(128, 124)
    fg = np.ascontiguousarray(np.asarray(inp["final_g"], f).reshape(8, 128).T)
    waup = np.zeros((L, 2, 128, 512), f)
    waup[:, 0, 0:64, :] = np.asarray(inp["w_up"], f)
    waup[:, 1, 64:128, :] = np.asarray(inp["a_up"], f)
    return cst, fg, waup


_NC_CACHE = {}


def kernel(**inputs):
    inp = {k: np.asarray(v) for k, v in inputs.items()}
    n = 8
    cst, fg, waup = host_consts(inp)
    if "nc" not in _NC_CACHE:
        _NC_CACHE["nc"] = build_nc()
    nc = _NC_CACHE["nc"]
    x = np.ascontiguousarray(inp["x"], np.float32)
    shared = dict(cst=cst, fg=fg, waup=waup, gup=np.ascontiguousarray(inp["g_up"], np.float32),
                  w_in=np.ascontiguousarray(inp["w_in"], np.float32), w_out=np.ascontiguousarray(inp["w_out"], np.float32),
                  w_ff1=np.ascontiguousarray(inp["w_ff1"], np.float32), w_ff2=np.ascontiguousarray(inp["w_ff2"], np.float32))
    in_maps = [dict(shared, x=x[2 * c:2 * c + 2]) for c in range(n)]
    res = run_bass_kernel_spmd(nc, in_maps, core_ids=list(range(n)))
    return np.concatenate([r["y"] for r in res.results], axis=0)
```

```python
import numpy as np
from contextlib import ExitStack
import concourse.bass as bass
import concourse.mybir as mybir
from concourse.bass_utils import run_bass_kernel_spmd

F32 = mybir.dt.float32
BF16 = mybir.dt.bfloat16
AF = mybir.ActivationFunctionType
ALU = mybir.AluOpType
AX = mybir.AxisListType

D = 1024
SEQ = 2048
L = 2
INC = 2816
DFF = 4096
TB = 128
NBLK = SEQ // TB
NCH = TB // 64
C0 = float(np.exp(-0.5))

import os as _os
SAME_ENGINE_SYNC = bool(int(_os.environ.get("SES", "1")))

STOP = float(_os.environ.get('DEV_STOP', '99'))
SEM_ROTATE = 20000

CO = {}
_o = 0
for _n, _w in (("g1", 8), ("g2", 8), ("mu", 14), ("w0", 4), ("a0", 4), ("kk", 4), ("ka", 4), ("rk", 4),
               ("gng", 4), ("gnb", 4), ("dwb", 4), ("clg", 4), ("clb", 4), ("dww", 124)):
    CO[_n] = _o
    _o += _w
NCST = _o


class Prog:
    QUEUES = ("pe", "act", "dve", "pool", "sp")

    def __init__(self, nc):
        self.nc = nc
        self.ops = []
        self.last_w = {}
        self.readers = {}
        self.last_q = {}
        self.last_dma = {}

    @staticmethod
    def _bank(k):
        if isinstance(k, tuple) and k[0] in ("psA", "psB"):
            return k
        if k in ("psE", "psE2"):
            return "psE"
        if k in ("psCx", "psCu", "psCs"):
            return "psC"
        if isinstance(k, tuple) and k[0] == "psDy":
            return "psD"
        return None

    def add(self, eng, fn, r=(), w=(), dma=None, extra=()):
        i = len(self.ops)
        deps = set(extra)
        banks = [self._bank(k) for k in list(r) + list(w)]
        banks = [b for b in banks if b is not None]
        if banks:
            r = [k for k in r if self._bank(k) is None]
            w = [k for k in w if self._bank(k) is None] + sorted(set(banks), key=str)
        for k in r:
            j = self.last_w.get(k)
            if j is not None:
                deps.add(j)
        for k in w:
            j = self.last_w.get(k)
            if j is not None:
                deps.add(j)
            deps.update(self.readers.get(k, ()))
        for k in r:
            self.readers.setdefault(k, []).append(i)
        for k in w:
            self.last_w[k] = i
            self.readers[k] = []
        deps.discard(i)
        self.ops.append(dict(eng=eng, fn=fn, deps=deps, dma=dma, sig=False))
        if dma is not None:
            self.last_dma[dma] = i
        elif fn is not None:
            self.last_q[eng] = i
        return i

    def barrier(self):
        ex = set(self.last_q.values()) | set(self.last_dma.values())
        for q in self.QUEUES:
            self.add(q, None, extra=ex)

    def finalize(self, stack):
        nc = self.nc
        ops = self.ops
        for i, o in enumerate(ops):
            for j in o["deps"]:
                d = ops[j]
                if d["dma"] is not None:
                    continue
                if d["eng"] == o["eng"] and (d["eng"] == "pe" or not SAME_ENGINE_SYNC):
                    continue
                d["sig"] = True
        eng_sem, eng_cnt, dma_sem, dma_cnt = {}, {}, {}, {}
        nsem = [0]

        def new_sem(name):
            nsem[0] += 1
            return stack.enter_context(nc.semaphore(name))

        for i, o in enumerate(ops):
            if o["dma"] is not None:
                key = o["dma"]
                if key not in dma_sem:
                    dma_sem[key] = new_sem("d%d" % len(dma_sem))
                    dma_cnt[key] = 0
                dma_cnt[key] += 16
                o["sem"] = dma_sem[key]
                o["val"] = dma_cnt[key]
            elif o["sig"]:
                e = o["eng"]
                if e not in eng_sem or eng_cnt[e] >= SEM_ROTATE:
                    eng_sem[e] = new_sem("e%s%d" % (e, nsem[0]))
                    eng_cnt[e] = 0
                eng_cnt[e] += 1
                o["sem"] = eng_sem[e]
                o["val"] = eng_cnt[e]
        known = {q: {} for q in self.QUEUES}
        latest_dma = {}
        nwaits = 0
        for i, o in enumerate(ops):
            waits = {}
            for j in o["deps"]:
                d = ops[j]
                if d["dma"] is not None:
                    sem, val = d["sem"], latest_dma[d["dma"]]
                else:
                    if not d["sig"]:
                        continue
                    sem, val = d["sem"], d["val"]
                sid = id(sem)
                if sid not in waits or waits[sid][1] < val:
                    waits[sid] = (sem, val)
            kn = known[o["eng"]]
            wl = []
            for sid, (sem, val) in waits.items():
                if kn.get(sid, 0) >= val:
                    continue
                kn[sid] = val
                wl.append((sem, val))
            o["waits"] = wl
            nwaits += len(wl)
            if o["dma"] is not None:
                latest_dma[o["dma"]] = o["val"]
        self.stats = dict(n_ops=len(ops), n_sems=nsem[0], n_waits=nwaits,
                          per_eng={q: sum(1 for o in ops if o["eng"] == q) for q in self.QUEUES})
        return self.stats

    def emit_queue(self, q, eng):
        for o in self.ops:
            if o["eng"] != q:
                continue
            for sem, val in o["waits"]:
                eng.wait_ge(sem, val)
            if o["fn"] is None:
                continue
            ins = o["fn"](eng)
            if o["dma"] is not None:
                ins.then_inc(o["sem"], 16)
            elif o["sig"]:
                ins.then_inc(o["sem"], 1)

    def emit(self):
        with self.nc.Block() as block:
            @block.tensor
            def _(e):
                self.emit_queue("pe", e)

            @block.scalar
            def _(e):
                self.emit_queue("act", e)

            @block.vector
            def _(e):
                self.emit_queue("dve", e)

            @block.gpsimd
            def _(e):
                self.emit_queue("pool", e)

            @block.sync
            def _(e):
                self.emit_queue("sp", e)


def build_nc(nseq=2, nlayers=L, dbg=None):
    nc = bass.Bass("TRN2", target_bir_lowering=False)
    x_d = nc.dram_tensor("x", [nseq, SEQ, D], F32, kind="ExternalInput").ap()
    cst_d = nc.dram_tensor("cst", [L, 128, NCST], F32, kind="ExternalInput").ap()
    fg_d = nc.dram_tensor("fg", [128, 8], F32, kind="ExternalInput").ap()
    waup_d = nc.dram_tensor("waup", [L, 2, 128, 512], F32, kind="ExternalInput").ap()
    gup_d = nc.dram_tensor("gup", [L, 128, 512], F32, kind="ExternalInput").ap()
    win_d = nc.dram_tensor("w_in", [L, D, INC], F32, kind="ExternalInput").ap()
    wout_d = nc.dram_tensor("w_out", [L, D, D], F32, kind="ExternalInput").ap()
    w1_d = nc.dram_tensor("w_ff1", [L, D, DFF], F32, kind="ExternalInput").ap()
    w2_d = nc.dram_tensor("w_ff2", [L, DFF, D], F32, kind="ExternalInput").ap()
    y_d = nc.dram_tensor("y", [nseq, SEQ, D], F32, kind="ExternalOutput").ap()
    dbg_d = None
    if dbg:
        dbg_d = nc.dram_tensor("dbg", [dbg, 128, SEQ], F32, kind="ExternalOutput").ap()

    st = ExitStack()
    with st:
        def sb(name, shape, dt=F32):
            return st.enter_context(nc.sbuf_tensor(name, shape, dt))

        def psum(name, dt=F32):
            return st.enter_context(nc.psum_tensor(name, [128, 512], dt))

        P = Prog(nc)

        def TT(eng, out, a, b, op, r, w):
            P.add(eng, lambda e: e.tensor_tensor(out=out, in0=a, in1=b, op=op), r, w)

        def TS(eng, out, a, s1, op0, r, w, s2=None, op1=None):
            if op1 is None:
                P.add(eng, lambda e: e.tensor_scalar(out=out, in0=a, scalar1=s1, scalar2=None, op0=op0), r, w)
            else:
                P.add(eng, lambda e: e.tensor_scalar(out=out, in0=a, scalar1=s1, scalar2=s2, op0=op0, op1=op1), r, w)

        def STT(out, a, s, b, op0, op1, r, w):
            P.add("dve", lambda e: e.scalar_tensor_tensor(out=out, in0=a, scalar=s, in1=b, op0=op0, op1=op1), r, w)

        def ACT(out, in_, func, r, w, bias=None, scale=None):
            kw = {}
            if bias is not None:
                kw["bias"] = bias
            if scale is not None:
                kw["scale"] = scale
            P.add("act", lambda e: e.activation(out=out, in_=in_, func=func, **kw), r, w)

        def MM(ps, lhsT, rhs, start, stop, r, w):
            P.add("pe", lambda e: e.matmul(ps, lhsT, rhs, start=start, stop=stop), r, w)

        def MEMSET(eng, ap, val, w):
            P.add(eng, lambda e: e.memset(ap, val), (), w)

        def POW(out, a, r, w, n):
            ACT(out, a, AF.Sqrt, r, w)
            P.add("dve", lambda e: e.reciprocal(out=out, in_=out), w, w)

        xT = sb("xT", [128, 8, SEQ])
        arena = sb("arena", [128, 32768], BF16)
        Win = arena[:, 0:8 * INC].rearrange("p (k n) -> p k n", k=8)
        Wout = arena[:, 8 * INC:8 * INC + 8 * D].rearrange("p (k n) -> p k n", k=8)
        h2T = arena[:, 0:8 * SEQ].rearrange("p (k n) -> p k n", k=8)
        W1c = [arena[:, 8 * SEQ + i * 4096: 8 * SEQ + (i + 1) * 4096].rearrange("p (k n) -> p k n", k=8) for i in range(2)]
        W2c = [arena[:, 8 * SEQ + 8192 + i * 4096: 8 * SEQ + 8192 + (i + 1) * 4096].rearrange("p (k n) -> p k n", k=4)
               for i in range(2)]
        cst = sb("cst_s", [128, NCST])
        cst2 = sb("cst2", [128, 16])
        fg = sb("fgs", [128, 8])
        waup = sb("waup_s", [128, 2, 512], BF16)
        gup = sb("gup_s", [128, 512], BF16)
        identb = sb("identb", [128, 128], BF16)
        identf = sb("identf", [128, 128])
        onesf = sb("onesf", [128, 128])
        onesb = sb("onesb", [128, 128], BF16)
        blk1 = sb("blk1", [128, 128], BF16)
        stackI = sb("stackI", [128, 64], BF16)
        m_su = sb("m_su", [128, 128])
        m_ue = sb("m_ue", [128, 128])
        m_sl = sb("m_sl", [128, 128])
        m01 = sb("m01", [128, TB])
        nhalf = sb("nhalf", [128, 2])

        NF, NB_ = 6640, 22660
        scrF = sb("scrF", [128, NF])
        scrB = sb("scrB", [128, NB_], BF16)
        aoff = {"F": 0, "B": 0}

        def areset():
            aoff["F"] = 0
            aoff["B"] = 0

        def take(shape, dt=F32):
            kind = "F" if dt == F32 else "B"
            t, cap = (scrF, NF) if kind == "F" else (scrB, NB_)
            size = int(np.prod(shape[1:]))
            o = aoff[kind]
            aoff[kind] = o + ((size + 15) // 16) * 16
            assert aoff[kind] <= cap, (kind, aoff[kind], cap)
            a = t[:, o:o + size]
            if len(shape) == 3:
                a = a.rearrange("p (a b) -> p a b", a=shape[1])
            return a

        def c_(name, j=0):
            o = CO[name] + j
            return cst[:, o:o + 1]

        areset()
        hT = take([128, 8, TB], BF16)
        mixT = take([128, 8, TB], BF16)
        gT = take([128, 4, TB], BF16)
        bon = take([128, 4, TB], BF16)
        ubuf = take([128, 4, 30 + TB], BF16)
        big1 = take([128, 4, TB])
        big2 = take([128, 4, TB])
        cvz = take([128, 4, TB])
        cvb = take([128, 4, TB])
        tR = big1[:, 0, :]; tK = big1[:, 1, :]; tV = big1[:, 2, :]; lw = big1[:, 3, :]
        aa = big2[:, 0, :]; cs = big2[:, 1, :]; csm = big2[:, 2, :]; Epos = big2[:, 3, :]
        cvt = take([128, 4, TB], BF16)
        carry = take([128, 16])
        praw = [take([128, TB + 2]) for i in range(2)]
        dtmp = take([128, TB])
        lwin = take([128, TB], BF16)
        gsb = take([128, TB], BF16)
        gdm = take([128, TB])
        sqk = [take([128, TB], BF16) for i in range(2)]
        rstd = take([128, TB])
        mean = take([128, TB])
        diag = [take([128, 8, 128], BF16) for i in range(2)]
        Eneg = take([128, TB]); Eprev = take([128, TB])
        kkt = take([128, TB]); k2 = take([128, TB]); tmp1 = take([128, TB]); tmp2 = take([128, TB])
        rn = take([128, TB])
        sqb = take([128, TB], BF16); rkb = take([128, TB], BF16)
        XN = ("ATx", "BTx", "KTx", "RTx", "KhTx", "BhTx", "vTx")
        X = {n: take([128, 4 * NCH, 128], BF16) for n in XN}
        Pm = [take([128, 4 * NCH, 128], BF16), X["BTx"]]
        Ptm = [take([128, 4 * NCH, 128], BF16), X["KTx"]]
        PMK = ["Pm0", "BTx"]
        PTK = ["Ptm0", "KTx"]
        MrbT = take([128, 4 * NCH, 128], BF16)
        LakT = take([128, 4 * NCH, 128], BF16)
        MrkT = take([128, 4 * NCH, 128], BF16)
        TTf = take([128, 4 * NCH, 128])
        TTb = take([128, 4 * NCH, 128], BF16)
        Khx = Pm[0]
        Bhx = Ptm[0]
        V2 = take([128, 4 * NCH, 64], BF16)
        X1b = take([128, 4, 64], BF16)
        Ub = take([128, 4, 64], BF16)
        STf = take([128, 4, 64])
        STb = take([128, 4, 64], BF16)
        WCs = take([128, 4, NCH])
        ysq = take([128, 4 * NCH, 64])
        ycen = take([128, 4 * NCH, 64])
        ystat = take([128, 32])
        ynx = take([128, 4 * NCH, 128], BF16)
        t3 = take([128, 4, TB])
        print("mixer scratch", dict(aoff))
        areset()
        fsq = take([128, 512], BF16)
        frl = take([128, 512])
        fT = [take([128, 4, 512], BF16) for i in range(2)]
        rs5 = take([128, 512])
        areset()
        xin = [take([128, D]) for i in range(2)]
        areset()
        yout = [take([128, D]) for i in range(2)]
        hn = take([128, 8, 128])
        rstd_s = take([128, 128])
        sqk_s = [take([128, 128], BF16) for i in range(2)]

        psA = [psum("psA%d" % i) for i in range(3)]
        psB = [psum("psB%d" % i) for i in range(2)]
        psC = psum("psC")
        psD = psum("psD")
        psE = psum("psE")
        cnt = {"A": 0, "B": 0, "praw": 0, "sqk": 0, "diag": 0, "xin": 0, "yout": 0}

        psP = psA + psB

        def nextA():
            i = cnt["A"] % 5
            cnt["A"] += 1
            return psP[i], ("psA", i)

        nextB = nextA

        MEMSET("dve", onesf[:, :], 1.0, ["onesf"])
        MEMSET("dve", onesb[:, :], 1.0, ["onesb"])
        MEMSET("dve", nhalf[:, :], -0.5, ["nhalf"])
        MEMSET("dve", m01[:, :], 1.0, ["m01"])
        MEMSET("dve", m01[:, :].rearrange("p (c t) -> p c t", t=64)[:, :, 0:1], 0.0, ["m01"])
        MEMSET("dve", blk1[:, :], 0.0, ["blk1"])
        MEMSET("dve", blk1[0:64, 0:64], 1.0, ["blk1"])
        MEMSET("dve", blk1[64:128, 64:128], 1.0, ["blk1"])
        P.add("pool", lambda e: e.affine_select(out=m_su[:, :], in_=onesf[:, :], pattern=[[1, 128]], compare_op=ALU.is_gt, fill=0.0,
                                                base=0, channel_multiplier=-1), ["onesf"], ["m_su"])
        P.add("pool", lambda e: e.affine_select(out=m_ue[:, :], in_=onesf[:, :], pattern=[[1, 128]], compare_op=ALU.is_ge, fill=0.0,
                                                base=0, channel_multiplier=-1), ["onesf"], ["m_ue"])
        P.add("pool", lambda e: e.affine_select(out=m_sl[:, :], in_=onesf[:, :], pattern=[[-1, 128]], compare_op=ALU.is_gt, fill=0.0,
                                                base=0, channel_multiplier=1), ["onesf"], ["m_sl"])
        TT("dve", identf[:, :], m_ue[:, :], m_su[:, :], ALU.subtract, ["m_ue", "m_su"], ["identf"])
        P.add("dve", lambda e: e.tensor_copy(out=identb[:, :], in_=identf[:, :]), ["identf"], ["identb"])
        P.add("dve", lambda e: e.tensor_copy(out=stackI[0:64, :], in_=identf[0:64, 0:64]), ["identf"], ["stackI"])
        P.add("dve", lambda e: e.tensor_copy(out=stackI[64:128, :], in_=identf[64:128, 64:128]), ["identf"], ["stackI"])
        for n in XN:
            MEMSET("dve", X[n][:, :, :], 0.0, [(n, cc) for cc in range(4)])
        MEMSET("dve", ynx[:, :, :], 0.0, [("ynx_", cc) for cc in range(4)])
        P.add("sp", lambda e: e.dma_start(out=fg[:, :], in_=fg_d[:, :]), (), ["fg"], dma="c0")

        def dbg_dump(idx, ap, keys, n):
            if dbg_d is None:
                return
            P.add("sp", lambda e: e.dma_start(out=dbg_d[idx, 0:ap.shape[0], 0:n], in_=ap), keys, [("dbg", idx)], dma="dbg")

        def rms_stats(t0, n, eps_rs, rs_out, rs_key, sq_bufs=None):
            ps = psE
            for k in range(8):
                i = cnt["sqk"] % 2
                cnt["sqk"] += 1
                if sq_bufs is not None:
                    s = sq_bufs[i][:, 0:n]
                    skey = ("sqs", i)
                elif n > TB:
                    s = fsq[:, 0:n]
                    skey = "fsq"
                else:
                    s = sqk[i][:, 0:n]
                    skey = ("sqk", i)
                ACT(s, xT[:, k, t0:t0 + n], AF.Square, [("xT", k)], [skey])
                MM(ps[:, 0:n], onesb[:, :], s, k == 0, k == 7, ["onesb", skey], ["psE"])
            TS("dve", rs_out, ps[:, 0:n], 1.0 / D, ALU.mult, ["psE"], [rs_key], s2=1e-5, op1=ALU.add)

        def load_seq(s):
            for tt_ in range(SEQ // 128):
                i = cnt["xin"] % 2
                cnt["xin"] += 1
                P.add("sp", lambda e, i=i, tt_=tt_: e.dma_start(out=xin[i][:, :], in_=x_d[s, tt_ * 128:(tt_ + 1) * 128, :]),
                      (), [("xin", i)], dma=("xin", i))
                for half in range(2):
                    ps, pk = nextA()
                    for k4 in range(4):
                        k = half * 4 + k4
                        P.add("pe", lambda e, ps=ps, k=k, k4=k4, i=i: e.transpose(ps[:, k4 * 128:(k4 + 1) * 128], xin[i][:, k * 128:(k + 1) * 128], identf[:, :]),
                              [("xin", i), "identf"], [pk])
                    P.add("act", lambda e, ps=ps, half=half, tt_=tt_: e.activation(
                        out=xT[:, half * 4:half * 4 + 4, tt_ * 128:(tt_ + 1) * 128],
                        in_=ps[:, :].rearrange("p (k t) -> p k t", k=4), func=AF.Copy),
                        [pk], [("xT", half * 4 + j) for j in range(4)])

        def store_seq(s):
            for tt_ in range(SEQ // 128):
                t0 = tt_ * 128
                rms_stats(t0, 128, 1e-5, rstd_s[:, :], "rstd_s", sq_bufs=sqk_s)
                POW(rstd_s[:, :], rstd_s[:, :], ["rstd_s"], ["rstd_s"], 128)
                for k in range(8):
                    STT(hn[:, k, :], xT[:, k, t0:t0 + 128], fg[:, k:k + 1], rstd_s[:, :], ALU.mult, ALU.mult,
                        [("xT", k), "fg", "rstd_s"], [("hn", k)])
                i = cnt["yout"] % 2
                cnt["yout"] += 1
                for half in range(2):
                    ps, pk = nextA()
                    for k4 in range(4):
                        k = half * 4 + k4
                        P.add("pe", lambda e, ps=ps, k=k, k4=k4: e.transpose(ps[:, k4 * 128:(k4 + 1) * 128], hn[:, k, :], identf[:, :]),
                              [("hn", k), "identf"], [pk])
                    P.add("act", lambda e, ps=ps, half=half, i=i: e.activation(out=yout[i][:, half * 512:(half + 1) * 512], in_=ps[:, :], func=AF.Copy),
                          [pk], [("yout", i)])
                P.add("sp", lambda e, i=i, tt_=tt_: e.dma_start(out=y_d[s, tt_ * 128:(tt_ + 1) * 128, :], in_=yout[i][:, :]),
                      [("yout", i)], [("y", s, tt_)], dma=("yout", i))

        def load_layer_consts(l):
            P.add("sp", lambda e: e.dma_start(out=cst[:, :], in_=cst_d[l, :, :]), (), ["cst"], dma="c0")
            for j in range(2):
                P.add("pool", lambda e, j=j: e.dma_start(out=waup[:, j, :], in_=waup_d[l, j, :, :]), (), ["waup"], dma="c1")
            P.add("pool", lambda e: e.dma_start(out=gup[:, :], in_=gup_d[l, :, :]), (), ["gup"], dma="c1")
            TS("dve", cst2[:, 0:4], cst[:, CO["w0"]:CO["w0"] + 4], 0.5, ALU.mult, ["cst"], ["cst2"])
            TS("dve", cst2[:, 4:8], cst[:, CO["a0"]:CO["a0"] + 4], 0.5, ALU.mult, ["cst"], ["cst2"])

        def load_mixer_weights(l):
            wv = win_d[l].rearrange("(k p) n -> p k n", p=128)
            for k in range(8):
                for c0 in range(0, INC, 704):
                    P.add("pool", lambda e, k=k, c0=c0: e.dma_start(out=Win[:, k, c0:c0 + 704], in_=wv[:, k, c0:c0 + 704]),
                          (), [("Win", k)], dma="win")
            wo = wout_d[l].rearrange("(k p) n -> p k n", p=128)
            for k in range(8):
                P.add("pool", lambda e, k=k: e.dma_start(out=Wout[:, k, :], in_=wo[:, k, :]), (), [("Wout", k)], dma="wout")

        def mixer_block(l, b, first):
            t0 = b * TB
            NP = 4 * NCH
            rms_stats(t0, TB, 1e-5, rstd[:, :], "rstd")
            POW(rstd[:, :], rstd[:, :], ["rstd"], ["rstd"], TB)
            for k in range(8):
                STT(hT[:, k, :], xT[:, k, t0:t0 + TB], c_("g1", k), rstd[:, :], ALU.mult, ALU.mult,
                    [("xT", k), "cst", "rstd"], [("hT", k)])
            if STOP <= 1:
                return

            def inproj(c):
                ps, pk = nextA()
                for k in range(8):
                    MM(ps[:, 0:TB], Win[:, k, c * 128:(c + 1) * 128], hT[:, k, :], k == 0, k == 7, [("Win", k), ("hT", k)], [pk])
                return ps, pk

            def shift_mix(c, ps, pk, dst, dkey):
                i = cnt["praw"] % 2
                cnt["praw"] += 1
                pr = praw[i]
                prk = ("praw", i)
                ACT(pr[:, 1:TB + 1], ps[:, 0:TB], AF.Copy, [pk], [prk])
                ACT(pr[:, 0:1], carry[:, c:c + 1], AF.Copy, [("carry", c)], [prk])
                ACT(carry[:, c:c + 1], pr[:, TB:TB + 1], AF.Copy, [prk], [("carry", c)])
                TT("dve", dtmp[:, :], pr[:, 0:TB], pr[:, 1:TB + 1], ALU.subtract, [prk], ["dtmp"])
                STT(dst, dtmp[:, :], c_("mu", c), pr[:, 1:TB + 1], ALU.mult, ALU.add, ["dtmp", "cst", prk], [dkey])

            ps, pk = inproj(12)
            shift_mix(12, ps, pk, tmp1[:, :], "tmp1")
            ACT(lwin[0:64, :], tmp1[0:64, :], AF.Tanh, ["tmp1"], ["lwin"])
            ACT(lwin[64:128, :], tmp1[64:128, :], AF.Copy, ["tmp1"], ["lwin"])
            ps, pk = inproj(13)
            shift_mix(13, ps, pk, gdm[:, :], "gdm")
            ACT(gdm[:, :], gdm[:, :], AF.Tanh, ["gdm"], ["gdm"], scale=0.5)
            TS("dve", gsb[:, :], gdm[:, :], 0.5, ALU.mult, ["gdm"], ["gsb"], s2=0.5, op1=ALU.add)
            if STOP <= 2:
                return

            def v3(t):
                return t[:, :].rearrange("p (c t) -> p c t", t=64)

            def conv_chunk(c):
                psv, pkv = inproj(14 + c)
                psg, pkg = inproj(18 + c)
                ACT(rn[:, :], psg[:, 0:TB], AF.Tanh, [pkg], ["rn"], scale=0.5)
                TS("dve", rn[:, :], rn[:, :], 0.5, ALU.mult, ["rn"], ["rn"], s2=0.5, op1=ALU.add)
                TT("dve", ubuf[:, c, 30:30 + TB], psv[:, 0:TB], rn[:, :], ALU.mult, [pkv, "rn"], [("ubuf", c)])
                ps, pk = nextA()
                for g0 in range(0, 31, 8):
                    G = min(8, 31 - g0)
                    i = cnt["diag"] % 2
                    cnt["diag"] += 1
                    o = CO["dww"] + c * 31 + g0
                    TT("dve", diag[i][:, 0:G, :], identf[:, :].unsqueeze(1).broadcast_to([128, G, 128]),
                       cst[:, o:o + G].unsqueeze(2).broadcast_to([128, G, 128]), ALU.mult, ["identf", "cst"], [("diag", i)])
                    for j in range(G):
                        kt = g0 + j
                        MM(ps[:, 0:TB], diag[i][:, j, :], ubuf[:, c, kt:kt + TB], kt == 0, kt == 30, [("diag", i), ("ubuf", c)], [pk])
                ACT(cvb[:, c, :], ps[:, 0:TB], AF.Identity, [pk, "cst"], [("cvb", c)], bias=c_("dwb", c))
                ACT(cvt[:, c, :], cvb[:, c, :], AF.Copy, [("cvb", c)], [("cvt", c)])
                ACT(ubuf[:, c, 0:30], ubuf[:, c, TB:TB + 30], AF.Copy, [("ubuf", c)], [("ubuf", c)])

            def conv_ln():
                cvk = [("cvb", c) for c in range(4)]
                for c in range(4):
                    MM(psE[:, 0:TB], onesb[:, :], cvt[:, c, :], c == 0, c == 3, ["onesb", ("cvt", c)], ["psE"])
                TS("dve", mean[:, :], psE[:, 0:TB], 1.0 / 512, ALU.mult, ["psE"], ["mean"])
                TT("dve", cvb[:, :, :], cvb[:, :, :], mean[:, :].unsqueeze(1).broadcast_to([128, 4, TB]), ALU.subtract, cvk + ["mean"], cvk)
                ACT(cvt[:, :, :], cvb[:, :, :], AF.Square, cvk, [("cvt", c) for c in range(4)])
                for c in range(4):
                    MM(psE[:, 0:TB], onesb[:, :], cvt[:, c, :], c == 0, c == 3, ["onesb", ("cvt", c)], ["psE"])
                TS("dve", mean[:, :], psE[:, 0:TB], 1.0 / 512, ALU.mult, ["psE"], ["mean"], s2=1e-5, op1=ALU.add)
                POW(mean[:, :], mean[:, :], ["mean"], ["mean"], TB)
                TT("dve", cvb[:, :, :], cvb[:, :, :], mean[:, :].unsqueeze(1).broadcast_to([128, 4, TB]), ALU.mult, cvk + ["mean"], cvk)
                for c in range(4):
                    TS("dve", cvb[:, c, :], cvb[:, c, :], c_("clg", c), ALU.mult, [("cvb", c), "cst"], [("cvb", c)], s2=c_("clb", c), op1=ALU.add)
                ACT(cvz[:, :, :], cvb[:, :, :], AF.Tanh, cvk, ["cvz"], scale=0.5)
                TS("dve", cvz[:, :, :], cvz[:, :, :], 0.5, ALU.mult, ["cvz"], ["cvz"], s2=0.5, op1=ALU.add)
                TT("dve", mixT[:, 4:8, :], cvb[:, :, :], cvz[:, :, :], ALU.mult, cvk + ["cvz"], [("mixT", 4 + c) for c in range(4)])

            def stage1(cc):
                pc = slice(cc * NCH, (cc + 1) * NCH)
                ps, pk = inproj(cc)
                shift_mix(cc, ps, pk, tR[:, :], "tR")
                ps, pk = inproj(4 + cc)
                shift_mix(4 + cc, ps, pk, tK[:, :], "tK")
                ps, pk = inproj(8 + cc)
                shift_mix(8 + cc, ps, pk, tV[:, :], "tV")
                MM(psE[:, 0:TB], waup[:, 0, cc * 128:(cc + 1) * 128], lwin[:, :], True, True, ["waup", "lwin"], ["psE"])
                ACT(lw[:, :], psE[:, 0:TB], AF.Tanh, ["psE", "cst2"], ["lw"], bias=cst2[:, cc:cc + 1], scale=0.5)
                TS("dve", lw[:, :], lw[:, :], -0.5 * C0, ALU.mult, ["lw"], ["lw"], s2=-0.5 * C0, op1=ALU.add)
                MM(psE[:, TB:2 * TB], waup[:, 1, cc * 128:(cc + 1) * 128], lwin[:, :], True, True, ["waup", "lwin"], ["psE2"])
                ACT(aa[:, :], psE[:, TB:2 * TB], AF.Tanh, ["psE2", "cst2"], ["aa"], bias=cst2[:, 4 + cc:5 + cc], scale=0.5)
                TS("dve", aa[:, :], aa[:, :], 0.5, ALU.mult, ["aa"], ["aa"], s2=0.5, op1=ALU.add)
                MM(psE[:, 2 * TB:3 * TB], gup[:, cc * 128:(cc + 1) * 128], gsb[:, :], True, True, ["gup", "gsb"], ["psE"])
                ACT(gT[:, cc, :], psE[:, 2 * TB:3 * TB], AF.Copy, ["psE"], [("gT", cc)])
                P.add("dve", lambda e: e.tensor_tensor_scan(out=cs[:, :], data0=m01[:, :], data1=lw[:, :], initial=0.0, op0=ALU.mult, op1=ALU.add),
                      ["m01", "lw"], ["cs"])
                TT("dve", csm[:, :], cs[:, :], lw[:, :], ALU.subtract, ["cs", "lw"], ["csm"])
                ACT(Epos[:, :], cs[:, :], AF.Exp, ["cs"], ["Epos"])
                ACT(Eneg[:, :], cs[:, :], AF.Exp, ["cs"], ["Eneg"], scale=-1.0)
                ACT(Eprev[:, :], csm[:, :], AF.Exp, ["csm"], ["Eprev"])
                ACT(WCs[:, cc, :].unsqueeze(2), v3(Epos)[:, :, 63:64], AF.Copy, ["Epos"], [("WCs", cc)])
                ACT(sqb[:, :], tK[:, :], AF.Square, ["tK", "cst"], ["sqb"], scale=c_("kk", cc))
                psn, pkn = nextB()
                MM(psn[:, 0:TB], blk1[:, :], sqb[:, :], True, True, ["blk1", "sqb"], [pkn])
                TS("dve", rn[:, :], psn[:, 0:TB], 1e-24, ALU.max, [pkn], ["rn"])
                POW(rn[:, :], rn[:, :], ["rn"], ["rn"], TB)
                STT(kkt[:, :], tK[:, :], c_("kk", cc), rn[:, :], ALU.mult, ALU.mult, ["tK", "cst", "rn"], ["kkt"])
                TS("dve", tmp1[:, :], aa[:, :], c_("ka", cc), ALU.mult, ["aa", "cst"], ["tmp1"], s2=c_("ka", cc), op1=ALU.subtract)
                STT(k2[:, :], tmp1[:, :], 1.0, tK[:, :], ALU.add, ALU.mult, ["tmp1", "tK"], ["k2"])
                STT(rkb[:, :], tR[:, :], c_("rk", cc), k2[:, :], ALU.mult, ALU.mult, ["tR", "cst", "k2"], ["rkb"])
                psb_, pkb = nextB()
                MM(psb_[:, 0:TB], blk1[:, :], rkb[:, :], True, True, ["blk1", "rkb"], [pkb])
                TT("dve", bon[:, cc, :], psb_[:, 0:TB], tV[:, :], ALU.mult, [pkb, "tV"], [("bon", cc)])
                TT("dve", tmp1[:, :], kkt[:, :], aa[:, :], ALU.mult, ["kkt", "aa"], ["tmp1"])
                TT("dve", tmp1[:, :], tmp1[:, :], Eneg[:, :], ALU.mult, ["tmp1", "Eneg"], ["tmp1"])
                TT("dve", tmp2[:, :], k2[:, :], Eneg[:, :], ALU.mult, ["k2", "Eneg"], ["tmp2"])
                E3 = v3(Epos)
                for h in range(2):
                    sl = slice(h * 64, (h + 1) * 64)
                    cl = slice(h * 64, (h + 1) * 64)
                    wc = E3[sl, :, 63:64].broadcast_to([64, NCH, 64])
                    STT(X["ATx"][sl, pc, cl], v3(kkt)[sl], -1.0, v3(Eprev)[sl], ALU.mult, ALU.mult, ["kkt", "Eprev"], [("ATx", cc)])
                    ACT(X["BTx"][sl, pc, cl], v3(tmp1)[sl], AF.Copy, ["tmp1"], [("BTx", cc)])
                    TT("dve", X["BhTx"][sl, pc, cl], v3(tmp1)[sl], wc, ALU.mult, ["tmp1", "Epos"], [("BhTx", cc)])
                    ACT(X["KTx"][sl, pc, cl], v3(tmp2)[sl], AF.Copy, ["tmp2"], [("KTx", cc)])
                    TT("dve", X["KhTx"][sl, pc, cl], v3(tmp2)[sl], wc, ALU.mult, ["tmp2", "Epos"], [("KhTx", cc)])
                    TT("dve", X["RTx"][sl, pc, cl], v3(tR)[sl], v3(Epos)[sl], ALU.mult, ["tR", "Epos"], [("RTx", cc)])
                    ACT(X["vTx"][sl, pc, cl], v3(tV)[sl], AF.Copy, ["tV"], [("vTx", cc)])

            for cc in range(4):
                stage1(cc)
                conv_chunk(cc)
            conv_ln()
            if STOP <= 4:
                return

            def allk(n):
                return [(n, cc) for cc in range(4)]

            def lock_mm(lt, lkey, rt, rkey, ncol=128):
                outs = []
                per_bank = 512 // ncol
                nbank = (NP * ncol + 511) // 512
                for bi in range(nbank):
                    ps, pk = nextB()
                    for q in range(per_bank):
                        pc_ = bi * per_bank + q
                        cc_ = pc_ // NCH
                        MM(ps[:, q * ncol:(q + 1) * ncol], lt[:, pc_, :], rt if rkey is None else rt[:, pc_, :], True, True,
                           [(lkey, cc_)] + ([] if rkey is None else [(rkey, cc_)]), [pk])
                    outs.append((ps, pk, slice(bi * per_bank, (bi + 1) * per_bank)))
                return outs

            def intra(lname, rname, mask, dst, dkey):
                for ps, pk, rng in lock_mm(X[lname], lname, X[rname], rname):
                    TT("dve", dst[:, rng, :], ps[:, :].rearrange("p (c t) -> p c t", t=128),
                       mask[:, :].unsqueeze(1).broadcast_to([128, rng.stop - rng.start, 128]), ALU.mult,
                       [pk, "m_su", "m_ue", "m_sl"], allk(dkey))
            intra("BTx", "ATx", m_su, Pm[0], "Pm0")
            intra("ATx", "BTx", m_sl, Ptm[0], "Ptm0")
            intra("BTx", "RTx", m_ue, MrbT, "MrbT")
            intra("KTx", "ATx", m_su, LakT, "LakT")
            intra("KTx", "RTx", m_ue, MrkT, "MrkT")
            TT("dve", TTf[:, :, :], Pm[0][:, :, :], identf[:, :].unsqueeze(1).broadcast_to([128, NP, 128]), ALU.add,
               allk("Pm0") + ["identf"], allk("TTf"))
            ACT(TTb[:, :, :], TTf[:, :, :], AF.Copy, allk("TTf"), allk("TTb"))
            cur = 0
            for lev in range(5):
                nxt = 1 - cur
                pck, ptk, pnk, ptnk = PMK[cur], PTK[cur], PMK[nxt], PTK[nxt]
                if lev < 4:
                    for ps, pk, rng in lock_mm(Ptm[cur], ptk, Pm[cur], pck):
                        ACT(Pm[nxt][:, rng, :], ps[:, :].rearrange("p (c t) -> p c t", t=128), AF.Copy, [pk] + allk(ptk), allk(pnk))
                for ps, pk, rng in lock_mm(Pm[cur], pck, Ptm[cur], ptk):
                    ACT(Ptm[nxt][:, rng, :], ps[:, :].rearrange("p (c t) -> p c t", t=128), AF.Copy, [pk], allk(ptnk))
                for ps, pk, rng in lock_mm(Ptm[nxt], ptnk, TTb, "TTb"):
                    TT("dve", TTf[:, rng, :], TTf[:, rng, :], ps[:, :].rearrange("p (c t) -> p c t", t=128), ALU.add,
                       allk("TTf") + [pk], allk("TTf"))
                ACT(TTb[:, :, :], TTf[:, :, :], AF.Copy, allk("TTf"), allk("TTb"))
                cur = nxt
            if STOP <= 5:
                return
            for (src, dst, dkey) in (("KhTx", Khx, "Pm0"), ("BhTx", Bhx, "Ptm0")):
                for ps, pk, rng in lock_mm(X[src], src, identb[:, :], None):
                    ACT(dst[:, rng, :], ps[:, :].rearrange("p (c t) -> p c t", t=128), AF.Copy, [pk], allk(dkey))
            for ps, pk, rng in lock_mm(X["vTx"], "vTx", stackI[:, :], None, ncol=64):
                ACT(V2[:, rng, :], ps[:, :].rearrange("p (c t) -> p c t", t=64), AF.Copy, [pk], allk("V2"))
            for c4 in range(NCH):
                for cc in range(4):
                    pc_ = cc * NCH + c4
                    MM(psC[:, cc * 64:(cc + 1) * 64], X["ATx"][:, pc_, :], STb[:, cc, :], True, False, [("ATx", cc), "STb"], ["psCx"])
                    MM(psC[:, cc * 64:(cc + 1) * 64], LakT[:, pc_, :], V2[:, pc_, :], False, True, [("LakT", cc), ("V2", cc)], ["psCx"])
                ACT(X1b[:, :, :], psC[:, 0:256].rearrange("p (c t) -> p c t", t=64), AF.Copy, ["psCx"], ["X1b"])
                for cc in range(4):
                    pc_ = cc * NCH + c4
                    MM(psC[:, 256 + cc * 64:256 + (cc + 1) * 64], TTb[:, pc_, :], X1b[:, cc, :], True, True, [("TTb", cc), "X1b"], ["psCu"])
                ACT(Ub[:, :, :], psC[:, 256:512].rearrange("p (c t) -> p c t", t=64), AF.Copy, ["psCu"], ["Ub"])
                for cc in range(4):
                    pc_ = cc * NCH + c4
                    yk = ("psDy", 0)
                    MM(psD[:, pc_ * 64:(pc_ + 1) * 64], X["RTx"][:, pc_, :], STb[:, cc, :], True, False, [("RTx", cc), "STb"], [yk])
                    MM(psD[:, pc_ * 64:(pc_ + 1) * 64], MrbT[:, pc_, :], Ub[:, cc, :], False, False, [("MrbT", cc), "Ub"], [yk])
                    MM(psD[:, pc_ * 64:(pc_ + 1) * 64], MrkT[:, pc_, :], V2[:, pc_, :], False, True, [("MrkT", cc), ("V2", cc)], [yk])
                for cc in range(4):
                    pc_ = cc * NCH + c4
                    MM(psC[:, cc * 64:(cc + 1) * 64], Khx[:, pc_, :], V2[:, pc_, :], True, False, [("Pm0", cc), ("V2", cc)], ["psCs"])
                    MM(psC[:, cc * 64:(cc + 1) * 64], Bhx[:, pc_, :], Ub[:, cc, :], False, True, [("Ptm0", cc), "Ub"], ["psCs"])
                TT("dve", STf[:, :, :], STf[:, :, :], WCs[:, :, c4:c4 + 1].broadcast_to([128, 4, 64]), ALU.mult,
                   ["STf"] + allk("WCs"), ["STf"])
                TT("dve", STf[:, :, :], STf[:, :, :], psC[:, 0:256].rearrange("p (c t) -> p c t", t=64), ALU.add, ["STf", "psCs"], ["STf"])
                ACT(STb[:, :, :], STf[:, :, :], AF.Copy, ["STf"], ["STb"])
            yks = [("psDy", 0)]
            y3 = psD[:, 0:NP * 64].rearrange("p (c t) -> p c t", t=64)
            P.add("dve", lambda e: e.tensor_reduce(out=ystat[:, 0:NP], in_=y3, axis=AX.X, op=ALU.add), yks, ["ystat"])
            TS("dve", ystat[:, 0:NP], ystat[:, 0:NP], 1.0 / 64, ALU.mult, ["ystat"], ["ystat"])
            TT("dve", ycen[:, :, :], y3, ystat[:, 0:NP].unsqueeze(2).broadcast_to([128, NP, 64]), ALU.subtract, yks + ["ystat"], ["ycen"])
            ACT(ysq[:, :, :], ycen[:, :, :], AF.Square, ["ycen"], ["ysq"])
            P.add("dve", lambda e: e.tensor_reduce(out=ystat[:, 16:16 + NP], in_=ysq[:, :, :], axis=AX.X, op=ALU.add), ["ysq"], ["ystat2"])
            TS("dve", ystat[:, 16:16 + NP], ystat[:, 16:16 + NP], 1.0 / 64, ALU.mult, ["ystat2"], ["ystat2"], s2=64e-5, op1=ALU.add)
            POW(ystat[:, 16:16 + NP], ystat[:, 16:16 + NP], ["ystat2"], ["ystat2"], NP)
            for h in range(2):
                sl = slice(h * 64, (h + 1) * 64)
                TT("dve", ynx[sl, :, h * 64:(h + 1) * 64], ycen[sl, :, :],
                   ystat[sl, 16:16 + NP].unsqueeze(2).broadcast_to([64, NP, 64]), ALU.mult, ["ycen", "ystat2"], [("ynx_", cc) for cc in range(4)])
            (ps, pk, rng), = lock_mm(ynx, "ynx_", stackI[:, :], None, ncol=64)
            psv_ = ps[:, :].rearrange("p (c t) -> p c t", t=TB)
            gng = cst[:, CO["gng"]:CO["gng"] + 4].unsqueeze(2).broadcast_to([128, 4, TB])
            gnb = cst[:, CO["gnb"]:CO["gnb"] + 4].unsqueeze(2).broadcast_to([128, 4, TB])
            TT("dve", t3[:, :, :], psv_, gng, ALU.mult, [pk, "cst"], ["t3"])
            TT("dve", t3[:, :, :], t3[:, :, :], gnb, ALU.add, ["t3", "cst"], ["t3"])
            TT("dve", t3[:, :, :], t3[:, :, :], bon[:, :, :], ALU.add, ["t3"] + allk("bon"), ["t3"])
            TT("dve", mixT[:, 0:4, :], t3[:, :, :], gT[:, :, :], ALU.mult, ["t3"] + allk("gT"), [("mixT", c) for c in range(4)])
            if STOP <= 6:
                return
            for m in range(8):
                ps, pk = nextA()
                for k in range(8):
                    MM(ps[:, 0:TB], Wout[:, k, m * 128:(m + 1) * 128], mixT[:, k, :], k == 0, k == 7, [("Wout", k), ("mixT", k)], [pk])
                TT("dve", xT[:, m, t0:t0 + TB], xT[:, m, t0:t0 + TB], ps[:, 0:TB], ALU.add, [("xT", m), pk], [("xT", m)])

        def ffn_phase(l):
            for tg in range(4):
                t0 = tg * 512
                rms_stats(t0, 512, 1e-5, rs5[:, :], "rs5")
                POW(rs5[:, :], rs5[:, :], ["rs5"], ["rs5"], 512)
                for k in range(8):
                    STT(h2T[:, k, t0:t0 + 512], xT[:, k, t0:t0 + 512], c_("g2", k), rs5[:, :], ALU.mult, ALU.mult,
                        [("xT", k), "cst", "rs5"], [("h2T", tg)])
            w1v = w1_d[l].rearrange("(k p) n -> p k n", p=128)
            w2v = w2_d[l].rearrange("(k p) n -> p k n", p=128)
            for hc in range(8):
                i = hc % 2
                for k in range(8):
                    P.add("pool", lambda e, k=k, i=i, hc=hc: e.dma_start(out=W1c[i][:, k, :], in_=w1v[:, k, hc * 512:(hc + 1) * 512]),
                          (), [("W1c", i)], dma=("w1", i))
                for k in range(4):
                    P.add("pool", lambda e, k=k, i=i, hc=hc: e.dma_start(out=W2c[i][:, k, :], in_=w2v[:, hc * 4 + k, :]),
                          (), [("W2c", i)], dma=("w2", i))
                for tg in range(4):
                    t0 = tg * 512
                    j = (hc * 4 + tg) % 2
                    for hs in range(4):
                        ps, pk = nextA()
                        for k in range(8):
                            MM(ps[:, :], W1c[i][:, k, hs * 128:(hs + 1) * 128], h2T[:, k, t0:t0 + 512], k == 0, k == 7,
                               [("W1c", i), ("h2T", tg)], [pk])
                        ACT(frl[:, :], ps[:, :], AF.Relu, [pk], ["frl"])
                        ACT(fT[j][:, hs, :], frl[:, :], AF.Square, ["frl"], [("fT", j)])
                    for m in range(8):
                        ps, pk = nextA()
                        for hs in range(4):
                            MM(ps[:, :], W2c[i][:, hs, m * 128:(m + 1) * 128], fT[j][:, hs, :], hs == 0, hs == 3,
                               [("W2c", i), ("fT", j)], [pk])
                        TT("dve", xT[:, m, t0:t0 + 512], xT[:, m, t0:t0 + 512], ps[:, :], ALU.add, [("xT", m), pk], [("xT", m)])

        for s in range(nseq):
            load_seq(s)
            for l in range(nlayers):
                P.barrier()
                load_layer_consts(l)
                load_mixer_weights(l)
                MEMSET("dve", carry[:, :], 0.0, [("carry", c) for c in range(14)])
                MEMSET("dve", ubuf[:, :, 0:30], 0.0, [("ubuf", c) for c in range(4)])
                MEMSET("dve", STf[:, :, :], 0.0, ["STf"])
                MEMSET("dve", STb[:, :, :], 0.0, ["STb"])
                for b in range(NBLK if (STOP >= 8 or _os.environ.get('DEV_ALLBLK')) else 1):
                    mixer_block(l, b, b == 0)
                P.barrier()
                if STOP >= 9:
                    ffn_phase(l)
            P.barrier()
            store_seq(s)
        P.add("sp", None, r=[("y", s, t) for s in range(nseq) for t in range(SEQ // 128)])
        stats = P.finalize(st)
        print("PROG", stats)
        P.emit()
    return nc


def host_consts(inp):
    f = np.float32
    cst = np.zeros((L, 128, NCST), f)

    def put(l, name, vec):
        v = np.asarray(vec, f).reshape(-1, 128).T
        cst[l, :, CO[name]:CO[name] + v.shape[1]] = v
    for l in range(L):
        put(l, "g1", inp["norm1_g"][l]); put(l, "g2", inp["norm2_g"][l]); put(l, "mu", inp["mu_shift"][l])
        put(l, "w0", inp["w0"][l]); put(l, "a0", inp["a0"][l]); put(l, "kk", inp["k_k"][l]); put(l, "ka", inp["k_a"][l])
        put(l, "rk", inp["r_k"][l].reshape(-1)); put(l, "gng", inp["gn_g"][l]); put(l, "gnb", inp["gn_b"][l])
        put(l, "dwb", inp["dw_b"][l]); put(l, "clg", inp["cln_g"][l]); put(l, "clb", inp["cln_b"][l])
        dw = np.asarray(inp["dw_w"][l], f)
        cst[l, :, CO["dww"]:CO["dww"] + 124] = dw.T.reshape(4, 128, 31).transpose(1, 0, 2).reshape(128, 124)
    fg = np.ascontiguousarray(np.asarray(inp["final_g"], f).reshape(8, 128).T)
    waup = np.zeros((L, 2, 128, 512), f)
    waup[:, 0, 0:64, :] = np.asarray(inp["w_up"], f)
    waup[:, 1, 64:128, :] = np.asarray(inp["a_up"], f)
    return cst, fg, waup


_NC_CACHE = {}


def kernel(**inputs):
    inp = {k: np.asarray(v) for k, v in inputs.items()}
    n = 8
    cst, fg, waup = host_consts(inp)
    if "nc" not in _NC_CACHE:
        _NC_CACHE["nc"] = build_nc()
    nc = _NC_CACHE["nc"]
    x = np.ascontiguousarray(inp["x"], np.float32)
    shared = dict(cst=cst, fg=fg, waup=waup, gup=np.ascontiguousarray(inp["g_up"], np.float32),
                  w_in=np.ascontiguousarray(inp["w_in"], np.float32), w_out=np.ascontiguousarray(inp["w_out"], np.float32),
                  w_ff1=np.ascontiguousarray(inp["w_ff1"], np.float32), w_ff2=np.ascontiguousarray(inp["w_ff2"], np.float32))
    in_maps = [dict(shared, x=x[2 * c:2 * c + 2]) for c in range(n)]
    res = run_bass_kernel_spmd(nc, in_maps, core_ids=list(range(n)))
    return np.concatenate([r["y"] for r in res.results], axis=0)
```

```python
import numpy as np
from contextlib import ExitStack
import concourse.bass as bass
import concourse.mybir as mybir
from concourse.bass_utils import run_bass_kernel_spmd

F32 = mybir.dt.float32
BF16 = mybir.dt.bfloat16
AF = mybir.ActivationFunctionType
ALU = mybir.AluOpType
AX = mybir.AxisListType

D = 1024
SEQ = 2048
L = 2
INC = 2816
DFF = 4096
TB = 128
NBLK = SEQ // TB
NCH = TB // 64
C0 = float(np.exp(-0.5))

import os as _os
SAME_ENGINE_SYNC = bool(int(_os.environ.get("SES", "1")))

STOP = float(_os.environ.get('DEV_STOP', '99'))
SEM_ROTATE = 20000

CO = {}
_o = 0
for _n, _w in (("g1", 8), ("g2", 8), ("mu", 14), ("w0", 4), ("a0", 4), ("kk", 4), ("ka", 4), ("rk", 4),
               ("gng", 4), ("gnb", 4), ("dwb", 4), ("clg", 4), ("clb", 4), ("dww", 124)):
    CO[_n] = _o
    _o += _w
NCST = _o


class Prog:
    QUEUES = ("pe", "act", "dve", "pool", "sp")

    def __init__(self, nc):
        self.nc = nc
        self.ops = []
        self.last_w = {}
        self.readers = {}
        self.last_q = {}
        self.last_dma = {}

    @staticmethod
    def _bank(k):
        if isinstance(k, tuple) and k[0] in ("psA", "psB", "psF"):
            return ("bank", k[1])
        if k in ("psE", "psE2"):
            return ("bank", 7)
        if k in ("psCx", "psCu", "psCs"):
            return ("bank", 5)
        if isinstance(k, tuple) and k[0] == "psDy":
            return ("bank", 6)
        return None

    def add(self, eng, fn, r=(), w=(), dma=None, extra=()):
        i = len(self.ops)
        deps = set(extra)
        banks = [self._bank(k) for k in list(r) + list(w)]
        banks = [b for b in banks if b is not None]
        if banks:
            r = [k for k in r if self._bank(k) is None]
            w = [k for k in w if self._bank(k) is None] + sorted(set(banks), key=str)
        for k in r:
            j = self.last_w.get(k)
            if j is not None:
                deps.add(j)
        for k in w:
            j = self.last_w.get(k)
            if j is not None:
                deps.add(j)
            deps.update(self.readers.get(k, ()))
        for k in r:
            self.readers.setdefault(k, []).append(i)
        for k in w:
            self.last_w[k] = i
            self.readers[k] = []
        deps.discard(i)
        self.ops.append(dict(eng=eng, fn=fn, deps=deps, dma=dma, sig=False))
        if dma is not None:
            self.last_dma[dma] = i
        elif fn is not None:
            self.last_q[eng] = i
        return i

    def barrier(self):
        ex = set(self.last_q.values()) | set(self.last_dma.values())
        for q in self.QUEUES:
            self.add(q, None, extra=ex)

    def finalize(self, stack):
        nc = self.nc
        ops = self.ops
        for i, o in enumerate(ops):
            for j in o["deps"]:
                d = ops[j]
                if d["dma"] is not None:
                    continue
                if d["eng"] == o["eng"] and (d["eng"] == "pe" or not SAME_ENGINE_SYNC):
                    continue
                d["sig"] = True
        eng_sem, eng_cnt, dma_sem, dma_cnt = {}, {}, {}, {}
        nsem = [0]

        def new_sem(name):
            nsem[0] += 1
            return stack.enter_context(nc.semaphore(name))

        for i, o in enumerate(ops):
            if o["dma"] is not None:
                key = o["dma"]
                if key not in dma_sem:
                    dma_sem[key] = new_sem("d%d" % len(dma_sem))
                    dma_cnt[key] = 0
                dma_cnt[key] += 16
                o["sem"] = dma_sem[key]
                o["val"] = dma_cnt[key]
            elif o["sig"]:
                e = o["eng"]
                if e not in eng_sem or eng_cnt[e] >= SEM_ROTATE:
                    eng_sem[e] = new_sem("e%s%d" % (e, nsem[0]))
                    eng_cnt[e] = 0
                eng_cnt[e] += 1
                o["sem"] = eng_sem[e]
                o["val"] = eng_cnt[e]
        known = {q: {} for q in self.QUEUES}
        latest_dma = {}
        nwaits = 0
        for i, o in enumerate(ops):
            waits = {}
            for j in o["deps"]:
                d = ops[j]
                if d["dma"] is not None:
                    sem, val = d["sem"], latest_dma[d["dma"]]
                else:
                    if not d["sig"]:
                        continue
                    sem, val = d["sem"], d["val"]
                sid = id(sem)
                if sid not in waits or waits[sid][1] < val:
                    waits[sid] = (sem, val)
            kn = known[o["eng"]]
            wl = []
            for sid, (sem, val) in waits.items():
                if kn.get(sid, 0) >= val:
                    continue
                kn[sid] = val
                wl.append((sem, val))
            o["waits"] = wl
            nwaits += len(wl)
            if o["dma"] is not None:
                latest_dma[o["dma"]] = o["val"]
        self.stats = dict(n_ops=len(ops), n_sems=nsem[0], n_waits=nwaits,
                          per_eng={q: sum(1 for o in ops if o["eng"] == q) for q in self.QUEUES})
        return self.stats

    def emit_queue(self, q, eng):
        for o in self.ops:
            if o["eng"] != q:
                continue
            for sem, val in o["waits"]:
                eng.wait_ge(sem, val)
            if o["fn"] is None:
                continue
            ins = o["fn"](eng)
            if o["dma"] is not None:
                ins.then_inc(o["sem"], 16)
            elif o["sig"]:
                ins.then_inc(o["sem"], 1)

    def emit(self):
        with self.nc.Block() as block:
            @block.tensor
            def _(e):
                self.emit_queue("pe", e)

            @block.scalar
            def _(e):
                self.emit_queue("act", e)

            @block.vector
            def _(e):
                self.emit_queue("dve", e)

            @block.gpsimd
            def _(e):
                self.emit_queue("pool", e)

            @block.sync
            def _(e):
                self.emit_queue("sp", e)


def build_nc(nseq=2, nlayers=L, dbg=None):
    nc = bass.Bass("TRN2", target_bir_lowering=False)
    x_d = nc.dram_tensor("x", [nseq, SEQ, D], F32, kind="ExternalInput").ap()
    cst_d = nc.dram_tensor("cst", [L, 128, NCST], F32, kind="ExternalInput").ap()
    fg_d = nc.dram_tensor("fg", [128, 8], F32, kind="ExternalInput").ap()
    waup_d = nc.dram_tensor("waup", [L, 2, 128, 512], F32, kind="ExternalInput").ap()
    gup_d = nc.dram_tensor("gup", [L, 128, 512], F32, kind="ExternalInput").ap()
    win_d = nc.dram_tensor("w_in", [L, D, INC], F32, kind="ExternalInput").ap()
    wout_d = nc.dram_tensor("w_out", [L, D, D], F32, kind="ExternalInput").ap()
    w1_d = nc.dram_tensor("w_ff1", [L, D, DFF], F32, kind="ExternalInput").ap()
    w2_d = nc.dram_tensor("w_ff2", [L, DFF, D], F32, kind="ExternalInput").ap()
    y_d = nc.dram_tensor("y", [nseq, SEQ, D], F32, kind="ExternalOutput").ap()
    dbg_d = None
    if dbg:
        dbg_d = nc.dram_tensor("dbg", [dbg, 128, SEQ], F32, kind="ExternalOutput").ap()

    st = ExitStack()
    with st:
        def sb(name, shape, dt=F32):
            return st.enter_context(nc.sbuf_tensor(name, shape, dt))

        def psum(name, dt=F32):
            return st.enter_context(nc.psum_tensor(name, [128, 512], dt))

        P = Prog(nc)

        def TT(eng, out, a, b, op, r, w):
            P.add(eng, lambda e: e.tensor_tensor(out=out, in0=a, in1=b, op=op), r, w)

        def TS(eng, out, a, s1, op0, r, w, s2=None, op1=None):
            if op1 is None:
                P.add(eng, lambda e: e.tensor_scalar(out=out, in0=a, scalar1=s1, scalar2=None, op0=op0), r, w)
            else:
                P.add(eng, lambda e: e.tensor_scalar(out=out, in0=a, scalar1=s1, scalar2=s2, op0=op0, op1=op1), r, w)

        def STT(out, a, s, b, op0, op1, r, w):
            P.add("dve", lambda e: e.scalar_tensor_tensor(out=out, in0=a, scalar=s, in1=b, op0=op0, op1=op1), r, w)

        def ACT(out, in_, func, r, w, bias=None, scale=None):
            kw = {}
            if bias is not None:
                kw["bias"] = bias
            if scale is not None:
                kw["scale"] = scale
            P.add("act", lambda e: e.activation(out=out, in_=in_, func=func, **kw), r, w)

        def MM(ps, lhsT, rhs, start, stop, r, w):
            P.add("pe", lambda e: e.matmul(ps, lhsT, rhs, start=start, stop=stop), r, w)

        def MEMSET(eng, ap, val, w):
            P.add(eng, lambda e: e.memset(ap, val), (), w)

        def POW(out, a, r, w, n):
            ACT(out, a, AF.Sqrt, r, w)
            P.add("dve", lambda e: e.reciprocal(out=out, in_=out), w, w)

        xT = sb("xT", [128, 8, SEQ])
        arena = sb("arena", [128, 32768], BF16)
        Win = arena[:, 0:8 * INC].rearrange("p (k n) -> p k n", k=8)
        Wout = arena[:, 8 * INC:8 * INC + 8 * D].rearrange("p (k n) -> p k n", k=8)
        h2T = arena[:, 0:8 * SEQ].rearrange("p (k n) -> p k n", k=8)
        W1c = [arena[:, 8 * SEQ + i * 4096: 8 * SEQ + (i + 1) * 4096].rearrange("p (k n) -> p k n", k=8) for i in range(2)]
        W2c = [arena[:, 8 * SEQ + 8192 + i * 4096: 8 * SEQ + 8192 + (i + 1) * 4096].rearrange("p (k n) -> p k n", k=4)
               for i in range(2)]
        cst = sb("cst_s", [128, NCST])
        cst2 = sb("cst2", [128, 16])
        fg = sb("fgs", [128, 8])
        waup = sb("waup_s", [128, 2, 512], BF16)
        gup = sb("gup_s", [128, 512], BF16)
        identb = sb("identb", [128, 128], BF16)
        identf = sb("identf", [128, 128])
        onesf = sb("onesf", [128, 128])
        onesb = sb("onesb", [128, 128], BF16)
        blk1 = sb("blk1", [128, 128], BF16)
        stackI = sb("stackI", [128, 64], BF16)
        m_su = sb("m_su", [128, 128])
        m_ue = sb("m_ue", [128, 128])
        m_sl = sb("m_sl", [128, 128])
        m01 = sb("m01", [128, TB])
        nhalf = sb("nhalf", [128, 2])

        NF, NB_ = 6640, 22660
        scrF = sb("scrF", [128, NF])
        scrB = sb("scrB", [128, NB_], BF16)
        aoff = {"F": 0, "B": 0}

        def areset():
            aoff["F"] = 0
            aoff["B"] = 0

        def take(shape, dt=F32):
            kind = "F" if dt == F32 else "B"
            t, cap = (scrF, NF) if kind == "F" else (scrB, NB_)
            size = int(np.prod(shape[1:]))
            o = aoff[kind]
            aoff[kind] = o + ((size + 15) // 16) * 16
            assert aoff[kind] <= cap, (kind, aoff[kind], cap)
            a = t[:, o:o + size]
            if len(shape) == 3:
                a = a.rearrange("p (a b) -> p a b", a=shape[1])
            return a

        def c_(name, j=0):
            o = CO[name] + j
            return cst[:, o:o + 1]

        areset()
        hT = take([128, 8, TB], BF16)
        mixT = take([128, 8, TB], BF16)
        gT = take([128, 4, TB], BF16)
        bon = take([128, 4, TB], BF16)
        ubuf = take([128, 4, 30 + TB], BF16)
        big1 = take([128, 4, TB])
        big2 = take([128, 4, TB])
        cvz = take([128, 4, TB])
        cvb = take([128, 4, TB])
        tR = big1[:, 0, :]; tK = big1[:, 1, :]; tV = big1[:, 2, :]; lw = big1[:, 3, :]
        aa = big2[:, 0, :]; cs = big2[:, 1, :]; csm = big2[:, 2, :]; Epos = big2[:, 3, :]
        cvt = take([128, 4, TB], BF16)
        carry = take([128, 16])
        praw = [take([128, TB + 2]) for i in range(2)]
        dtmp = take([128, TB])
        lwin = take([128, TB], BF16)
        gsb = take([128, TB], BF16)
        gdm = take([128, TB])
        sqk = [take([128, TB], BF16) for i in range(2)]
        rstd = take([128, TB])
        mean = take([128, TB])
        diag = [take([128, 8, 128], BF16) for i in range(2)]
        Eneg = take([128, TB]); Eprev = take([128, TB])
        kkt = take([128, TB]); k2 = take([128, TB]); tmp1 = take([128, TB]); tmp2 = take([128, TB])
        rn = take([128, TB])
        sqb = take([128, TB], BF16); rkb = take([128, TB], BF16)
        XN = ("ATx", "BTx", "KTx", "RTx", "KhTx", "BhTx", "vTx")
        X = {n: take([128, 4 * NCH, 128], BF16) for n in XN}
        Pm = [take([128, 4 * NCH, 128], BF16), X["BTx"]]
        Ptm = [take([128, 4 * NCH, 128], BF16), X["KTx"]]
        PMK = ["Pm0", "BTx"]
        PTK = ["Ptm0", "KTx"]
        MrbT = take([128, 4 * NCH, 128], BF16)
        LakT = take([128, 4 * NCH, 128], BF16)
        MrkT = take([128, 4 * NCH, 128], BF16)
        TTf = take([128, 4 * NCH, 128])
        TTb = take([128, 4 * NCH, 128], BF16)
        Khx = Pm[0]
        Bhx = Ptm[0]
        V2 = take([128, 4 * NCH, 64], BF16)
        X1b = take([128, 4, 64], BF16)
        Ub = take([128, 4, 64], BF16)
        STf = take([128, 4, 64])
        STb = take([128, 4, 64], BF16)
        WCs = take([128, 4, NCH])
        ysq = take([128, 4 * NCH, 64])
        ycen = take([128, 4 * NCH, 64])
        ystat = take([128, 32])
        ynx = take([128, 4 * NCH, 128], BF16)
        t3 = take([128, 4, TB])
        print("mixer scratch", dict(aoff))
        areset()
        fsq = take([128, 512], BF16)
        frl = [take([128, 512]) for i in range(2)]
        fT = [take([128, 16, 512], BF16).rearrange("p (a b) n -> p a b n", a=4) for i in range(2)]
        rs5 = take([128, 512])
        areset()
        xin = [take([128, D]) for i in range(2)]
        areset()
        yout = [take([128, D]) for i in range(2)]
        hn = take([128, 8, 128])
        rstd_s = take([128, 128])
        sqk_s = [take([128, 128], BF16) for i in range(2)]

        psA = [psum("psA%d" % i) for i in range(3)]
        psB = [psum("psB%d" % i) for i in range(2)]
        psC = psum("psC")
        psD = psum("psD")
        psE = psum("psE")
        cnt = {"A": 0, "B": 0, "praw": 0, "sqk": 0, "diag": 0, "xin": 0, "yout": 0}

        psP = psA + psB

        def nextA():
            i = cnt["A"] % 5
            cnt["A"] += 1
            return psP[i], ("psA", i)

        nextB = nextA

        MEMSET("dve", onesf[:, :], 1.0, ["onesf"])
        MEMSET("dve", onesb[:, :], 1.0, ["onesb"])
        MEMSET("dve", nhalf[:, :], -0.5, ["nhalf"])
        MEMSET("dve", m01[:, :], 1.0, ["m01"])
        MEMSET("dve", m01[:, :].rearrange("p (c t) -> p c t", t=64)[:, :, 0:1], 0.0, ["m01"])
        MEMSET("dve", blk1[:, :], 0.0, ["blk1"])
        MEMSET("dve", blk1[0:64, 0:64], 1.0, ["blk1"])
        MEMSET("dve", blk1[64:128, 64:128], 1.0, ["blk1"])
        P.add("pool", lambda e: e.affine_select(out=m_su[:, :], in_=onesf[:, :], pattern=[[1, 128]], compare_op=ALU.is_gt, fill=0.0,
                                                base=0, channel_multiplier=-1), ["onesf"], ["m_su"])
        P.add("pool", lambda e: e.affine_select(out=m_ue[:, :], in_=onesf[:, :], pattern=[[1, 128]], compare_op=ALU.is_ge, fill=0.0,
                                                base=0, channel_multiplier=-1), ["onesf"], ["m_ue"])
        P.add("pool", lambda e: e.affine_select(out=m_sl[:, :], in_=onesf[:, :], pattern=[[-1, 128]], compare_op=ALU.is_gt, fill=0.0,
                                                base=0, channel_multiplier=1), ["onesf"], ["m_sl"])
        TT("dve", identf[:, :], m_ue[:, :], m_su[:, :], ALU.subtract, ["m_ue", "m_su"], ["identf"])
        P.add("dve", lambda e: e.tensor_copy(out=identb[:, :], in_=identf[:, :]), ["identf"], ["identb"])
        P.add("dve", lambda e: e.tensor_copy(out=stackI[0:64, :], in_=identf[0:64, 0:64]), ["identf"], ["stackI"])
        P.add("dve", lambda e: e.tensor_copy(out=stackI[64:128, :], in_=identf[64:128, 64:128]), ["identf"], ["stackI"])
        for n in XN:
            MEMSET("dve", X[n][:, :, :], 0.0, [(n, cc) for cc in range(4)])
        MEMSET("dve", ynx[:, :, :], 0.0, [("ynx_", cc) for cc in range(4)])
        P.add("sp", lambda e: e.dma_start(out=fg[:, :], in_=fg_d[:, :]), (), ["fg"], dma="c0")

        def dbg_dump(idx, ap, keys, n):
            if dbg_d is None:
                return
            P.add("sp", lambda e: e.dma_start(out=dbg_d[idx, 0:ap.shape[0], 0:n], in_=ap), keys, [("dbg", idx)], dma="dbg")

        def rms_stats(t0, n, eps_rs, rs_out, rs_key, sq_bufs=None):
            ps = psE
            for k in range(8):
                i = cnt["sqk"] % 2
                cnt["sqk"] += 1
                if sq_bufs is not None:
                    s = sq_bufs[i][:, 0:n]
                    skey = ("sqs", i)
                elif n > TB:
                    s = fsq[:, 0:n]
                    skey = "fsq"
                else:
                    s = sqk[i][:, 0:n]
                    skey = ("sqk", i)
                ACT(s, xT[:, k, t0:t0 + n], AF.Square, [("xT", k)], [skey])
                MM(ps[:, 0:n], onesb[:, :], s, k == 0, k == 7, ["onesb", skey], ["psE"])
            TS("dve", rs_out, ps[:, 0:n], 1.0 / D, ALU.mult, ["psE"], [rs_key], s2=1e-5, op1=ALU.add)

        def load_seq(s):
            for tt_ in range(SEQ // 128):
                i = cnt["xin"] % 2
                cnt["xin"] += 1
                P.add("sp", lambda e, i=i, tt_=tt_: e.dma_start(out=xin[i][:, :], in_=x_d[s, tt_ * 128:(tt_ + 1) * 128, :]),
                      (), [("xin", i)], dma=("xin", i))
                for half in range(2):
                    ps, pk = nextA()
                    for k4 in range(4):
                        k = half * 4 + k4
                        P.add("pe", lambda e, ps=ps, k=k, k4=k4, i=i: e.transpose(ps[:, k4 * 128:(k4 + 1) * 128], xin[i][:, k * 128:(k + 1) * 128], identf[:, :]),
                              [("xin", i), "identf"], [pk])
                    P.add("act", lambda e, ps=ps, half=half, tt_=tt_: e.activation(
                        out=xT[:, half * 4:half * 4 + 4, tt_ * 128:(tt_ + 1) * 128],
                        in_=ps[:, :].rearrange("p (k t) -> p k t", k=4), func=AF.Copy),
                        [pk], [("xT", half * 4 + j) for j in range(4)])

        def store_seq(s):
            for tt_ in range(SEQ // 128):
                t0 = tt_ * 128
                rms_stats(t0, 128, 1e-5, rstd_s[:, :], "rstd_s", sq_bufs=sqk_s)
                POW(rstd_s[:, :], rstd_s[:, :], ["rstd_s"], ["rstd_s"], 128)
                for k in range(8):
                    STT(hn[:, k, :], xT[:, k, t0:t0 + 128], fg[:, k:k + 1], rstd_s[:, :], ALU.mult, ALU.mult,
                        [("xT", k), "fg", "rstd_s"], [("hn", k)])
                i = cnt["yout"] % 2
                cnt["yout"] += 1
                for half in range(2):
                    ps, pk = nextA()
                    for k4 in range(4):
                        k = half * 4 + k4
                        P.add("pe", lambda e, ps=ps, k=k, k4=k4: e.transpose(ps[:, k4 * 128:(k4 + 1) * 128], hn[:, k, :], identf[:, :]),
                              [("hn", k), "identf"], [pk])
                    P.add("act", lambda e, ps=ps, half=half, i=i: e.activation(out=yout[i][:, half * 512:(half + 1) * 512], in_=ps[:, :], func=AF.Copy),
                          [pk], [("yout", i)])
                P.add("sp", lambda e, i=i, tt_=tt_: e.dma_start(out=y_d[s, tt_ * 128:(tt_ + 1) * 128, :], in_=yout[i][:, :]),
                      [("yout", i)], [("y", s, tt_)], dma=("yout", i))

        def load_layer_consts(l):
            P.add("sp", lambda e: e.dma_start(out=cst[:, :], in_=cst_d[l, :, :]), (), ["cst"], dma="c0")
            for j in range(2):
                P.add("pool", lambda e, j=j: e.dma_start(out=waup[:, j, :], in_=waup_d[l, j, :, :]), (), ["waup"], dma="c1")
            P.add("pool", lambda e: e.dma_start(out=gup[:, :], in_=gup_d[l, :, :]), (), ["gup"], dma="c1")
            TS("dve", cst2[:, 0:4], cst[:, CO["w0"]:CO["w0"] + 4], 0.5, ALU.mult, ["cst"], ["cst2"])
            TS("dve", cst2[:, 4:8], cst[:, CO["a0"]:CO["a0"] + 4], 0.5, ALU.mult, ["cst"], ["cst2"])

        def load_mixer_weights(l):
            wv = win_d[l].rearrange("(k p) n -> p k n", p=128)
            for k in range(8):
                for c0 in range(0, INC, 704):
                    P.add("pool", lambda e, k=k, c0=c0: e.dma_start(out=Win[:, k, c0:c0 + 704], in_=wv[:, k, c0:c0 + 704]),
                          (), [("Win", k)], dma="win")
            wo = wout_d[l].rearrange("(k p) n -> p k n", p=128)
            for k in range(8):
                P.add("pool", lambda e, k=k: e.dma_start(out=Wout[:, k, :], in_=wo[:, k, :]), (), [("Wout", k)], dma="wout")

        def mixer_block(l, b, first):
            t0 = b * TB
            NP = 4 * NCH
            rms_stats(t0, TB, 1e-5, rstd[:, :], "rstd")
            POW(rstd[:, :], rstd[:, :], ["rstd"], ["rstd"], TB)
            for k in range(8):
                STT(hT[:, k, :], xT[:, k, t0:t0 + TB], c_("g1", k), rstd[:, :], ALU.mult, ALU.mult,
                    [("xT", k), "cst", "rstd"], [("hT", k)])
            if STOP <= 1:
                return

            def inproj_wave(chunks):
                outs = [nextA() for _ in chunks]
                for k in range(8):
                    for (ps, pk), c in zip(outs, chunks):
                        MM(ps[:, 0:TB], Win[:, k, c * 128:(c + 1) * 128], hT[:, k, :], k == 0, k == 7, [("Win", k), ("hT", k)], [pk])
                return outs

            def shift_mix(c, ps, pk, dst, dkey):
                i = cnt["praw"] % 2
                cnt["praw"] += 1
                pr = praw[i]
                prk = ("praw", i)
                ACT(pr[:, 1:TB + 1], ps[:, 0:TB], AF.Copy, [pk], [prk])
                ACT(pr[:, 0:1], carry[:, c:c + 1], AF.Copy, [("carry", c)], [prk])
                ACT(carry[:, c:c + 1], pr[:, TB:TB + 1], AF.Copy, [prk], [("carry", c)])
                TT("dve", dtmp[:, :], pr[:, 0:TB], pr[:, 1:TB + 1], ALU.subtract, [prk], ["dtmp"])
                STT(dst, dtmp[:, :], c_("mu", c), pr[:, 1:TB + 1], ALU.mult, ALU.add, ["dtmp", "cst", prk], [dkey])

            (ps, pk), (psgd, pkgd) = inproj_wave([12, 13])
            shift_mix(12, ps, pk, tmp1[:, :], "tmp1")
            ACT(lwin[0:64, :], tmp1[0:64, :], AF.Tanh, ["tmp1"], ["lwin"])
            ACT(lwin[64:128, :], tmp1[64:128, :], AF.Copy, ["tmp1"], ["lwin"])
            shift_mix(13, psgd, pkgd, gdm[:, :], "gdm")
            ACT(gdm[:, :], gdm[:, :], AF.Tanh, ["gdm"], ["gdm"], scale=0.5)
            TS("dve", gsb[:, :], gdm[:, :], 0.5, ALU.mult, ["gdm"], ["gsb"], s2=0.5, op1=ALU.add)
            if STOP <= 2:
                return

            def v3(t):
                return t[:, :].rearrange("p (c t) -> p c t", t=64)

            def conv_glu(c, psv, pkv, psg, pkg):
                ACT(rn[:, :], psg[:, 0:TB], AF.Tanh, [pkg], ["rn"], scale=0.5)
                TS("dve", rn[:, :], rn[:, :], 0.5, ALU.mult, ["rn"], ["rn"], s2=0.5, op1=ALU.add)
                TT("dve", ubuf[:, c, 30:30 + TB], psv[:, 0:TB], rn[:, :], ALU.mult, [pkv, "rn"], [("ubuf", c)])

            def conv_mm():
                outs = [nextA() for _ in range(4)]
                dww3 = cst[:, CO["dww"]:CO["dww"] + 124].rearrange("p (c k) -> p c k", k=31)
                for g0 in range(0, 31, 2):
                    G = min(2, 31 - g0)
                    i = cnt["diag"] % 2
                    cnt["diag"] += 1
                    dg = diag[i][:, :, :].rearrange("p (c g) n -> p c g n", g=2)
                    TT("dve", dg[:, :, 0:G, :], identf[:, :].unsqueeze(1).unsqueeze(1).broadcast_to([128, 4, G, 128]),
                       dww3[:, :, g0:g0 + G].unsqueeze(3).broadcast_to([128, 4, G, 128]), ALU.mult, ["identf", "cst"], [("diag", i)])
                    for j in range(G):
                        kt = g0 + j
                        for c in range(4):
                            ps, pk = outs[c]
                            MM(ps[:, 0:TB], dg[:, c, j, :], ubuf[:, c, kt:kt + TB], kt == 0, kt == 30, [("diag", i), ("ubuf", c)], [pk])
                for c in range(4):
                    ps, pk = outs[c]
                    ACT(cvb[:, c, :], ps[:, 0:TB], AF.Identity, [pk, "cst"], [("cvb", c)], bias=c_("dwb", c))
                    ACT(cvt[:, c, :], cvb[:, c, :], AF.Copy, [("cvb", c)], [("cvt", c)])
                    ACT(ubuf[:, c, 0:30], ubuf[:, c, TB:TB + 30], AF.Copy, [("ubuf", c)], [("ubuf", c)])

            def conv_ln():
                cvk = [("cvb", c) for c in range(4)]
                for c in range(4):
                    MM(psE[:, 0:TB], onesb[:, :], cvt[:, c, :], c == 0, c == 3, ["onesb", ("cvt", c)], ["psE"])
                TS("dve", mean[:, :], psE[:, 0:TB], 1.0 / 512, ALU.mult, ["psE"], ["mean"])
                TT("dve", cvb[:, :, :], cvb[:, :, :], mean[:, :].unsqueeze(1).broadcast_to([128, 4, TB]), ALU.subtract, cvk + ["mean"], cvk)
                ACT(cvt[:, :, :], cvb[:, :, :], AF.Square, cvk, [("cvt", c) for c in range(4)])
                for c in range(4):
                    MM(psE[:, 0:TB], onesb[:, :], cvt[:, c, :], c == 0, c == 3, ["onesb", ("cvt", c)], ["psE"])
                TS("dve", mean[:, :], psE[:, 0:TB], 1.0 / 512, ALU.mult, ["psE"], ["mean"], s2=1e-5, op1=ALU.add)
                POW(mean[:, :], mean[:, :], ["mean"], ["mean"], TB)
                TT("dve", cvb[:, :, :], cvb[:, :, :], mean[:, :].unsqueeze(1).broadcast_to([128, 4, TB]), ALU.mult, cvk + ["mean"], cvk)
                for c in range(4):
                    TS("dve", cvb[:, c, :], cvb[:, c, :], c_("clg", c), ALU.mult, [("cvb", c), "cst"], [("cvb", c)], s2=c_("clb", c), op1=ALU.add)
                ACT(cvz[:, :, :], cvb[:, :, :], AF.Tanh, cvk, ["cvz"], scale=0.5)
                TS("dve", cvz[:, :, :], cvz[:, :, :], 0.5, ALU.mult, ["cvz"], ["cvz"], s2=0.5, op1=ALU.add)
                TT("dve", mixT[:, 4:8, :], cvb[:, :, :], cvz[:, :, :], ALU.mult, cvk + ["cvz"], [("mixT", 4 + c) for c in range(4)])

            def stage1(cc):
                pc = slice(cc * NCH, (cc + 1) * NCH)
                w5 = inproj_wave([cc, 4 + cc, 8 + cc, 14 + cc, 18 + cc])
                shift_mix(cc, w5[0][0], w5[0][1], tR[:, :], "tR")
                shift_mix(4 + cc, w5[1][0], w5[1][1], tK[:, :], "tK")
                shift_mix(8 + cc, w5[2][0], w5[2][1], tV[:, :], "tV")
                conv_glu(cc, w5[3][0], w5[3][1], w5[4][0], w5[4][1])
                MM(psE[:, 0:TB], waup[:, 0, cc * 128:(cc + 1) * 128], lwin[:, :], True, True, ["waup", "lwin"], ["psE"])
                ACT(lw[:, :], psE[:, 0:TB], AF.Tanh, ["psE", "cst2"], ["lw"], bias=cst2[:, cc:cc + 1], scale=0.5)
                TS("dve", lw[:, :], lw[:, :], -0.5 * C0, ALU.mult, ["lw"], ["lw"], s2=-0.5 * C0, op1=ALU.add)
                MM(psE[:, TB:2 * TB], waup[:, 1, cc * 128:(cc + 1) * 128], lwin[:, :], True, True, ["waup", "lwin"], ["psE2"])
                ACT(aa[:, :], psE[:, TB:2 * TB], AF.Tanh, ["psE2", "cst2"], ["aa"], bias=cst2[:, 4 + cc:5 + cc], scale=0.5)
                TS("dve", aa[:, :], aa[:, :], 0.5, ALU.mult, ["aa"], ["aa"], s2=0.5, op1=ALU.add)
                MM(psE[:, 2 * TB:3 * TB], gup[:, cc * 128:(cc + 1) * 128], gsb[:, :], True, True, ["gup", "gsb"], ["psE"])
                ACT(gT[:, cc, :], psE[:, 2 * TB:3 * TB], AF.Copy, ["psE"], [("gT", cc)])
                P.add("dve", lambda e: e.tensor_tensor_scan(out=cs[:, :], data0=m01[:, :], data1=lw[:, :], initial=0.0, op0=ALU.mult, op1=ALU.add),
                      ["m01", "lw"], ["cs"])
                TT("dve", csm[:, :], cs[:, :], lw[:, :], ALU.subtract, ["cs", "lw"], ["csm"])
                ACT(Epos[:, :], cs[:, :], AF.Exp, ["cs"], ["Epos"])
                ACT(Eneg[:, :], cs[:, :], AF.Exp, ["cs"], ["Eneg"], scale=-1.0)
                ACT(Eprev[:, :], csm[:, :], AF.Exp, ["csm"], ["Eprev"])
                ACT(WCs[:, cc, :].unsqueeze(2), v3(Epos)[:, :, 63:64], AF.Copy, ["Epos"], [("WCs", cc)])
                ACT(sqb[:, :], tK[:, :], AF.Square, ["tK", "cst"], ["sqb"], scale=c_("kk", cc))
                psn, pkn = nextB()
                MM(psn[:, 0:TB], blk1[:, :], sqb[:, :], True, True, ["blk1", "sqb"], [pkn])
                TS("dve", rn[:, :], psn[:, 0:TB], 1e-24, ALU.max, [pkn], ["rn"])
                POW(rn[:, :], rn[:, :], ["rn"], ["rn"], TB)
                STT(kkt[:, :], tK[:, :], c_("kk", cc), rn[:, :], ALU.mult, ALU.mult, ["tK", "cst", "rn"], ["kkt"])
                TS("dve", tmp1[:, :], aa[:, :], c_("ka", cc), ALU.mult, ["aa", "cst"], ["tmp1"], s2=c_("ka", cc), op1=ALU.subtract)
                STT(k2[:, :], tmp1[:, :], 1.0, tK[:, :], ALU.add, ALU.mult, ["tmp1", "tK"], ["k2"])
                STT(rkb[:, :], tR[:, :], c_("rk", cc), k2[:, :], ALU.mult, ALU.mult, ["tR", "cst", "k2"], ["rkb"])
                psb_, pkb = nextB()
                MM(psb_[:, 0:TB], blk1[:, :], rkb[:, :], True, True, ["blk1", "rkb"], [pkb])
                TT("dve", bon[:, cc, :], psb_[:, 0:TB], tV[:, :], ALU.mult, [pkb, "tV"], [("bon", cc)])
                TT("dve", tmp1[:, :], kkt[:, :], aa[:, :], ALU.mult, ["kkt", "aa"], ["tmp1"])
                TT("dve", tmp1[:, :], tmp1[:, :], Eneg[:, :], ALU.mult, ["tmp1", "Eneg"], ["tmp1"])
                TT("dve", tmp2[:, :], k2[:, :], Eneg[:, :], ALU.mult, ["k2", "Eneg"], ["tmp2"])
                E3 = v3(Epos)
                for h in range(2):
                    sl = slice(h * 64, (h + 1) * 64)
                    cl = slice(h * 64, (h + 1) * 64)
                    wc = E3[sl, :, 63:64].broadcast_to([64, NCH, 64])
                    STT(X["ATx"][sl, pc, cl], v3(kkt)[sl], -1.0, v3(Eprev)[sl], ALU.mult, ALU.mult, ["kkt", "Eprev"], [("ATx", cc)])
                    ACT(X["BTx"][sl, pc, cl], v3(tmp1)[sl], AF.Copy, ["tmp1"], [("BTx", cc)])
                    TT("dve", X["BhTx"][sl, pc, cl], v3(tmp1)[sl], wc, ALU.mult, ["tmp1", "Epos"], [("BhTx", cc)])
                    ACT(X["KTx"][sl, pc, cl], v3(tmp2)[sl], AF.Copy, ["tmp2"], [("KTx", cc)])
                    TT("dve", X["KhTx"][sl, pc, cl], v3(tmp2)[sl], wc, ALU.mult, ["tmp2", "Epos"], [("KhTx", cc)])
                    TT("dve", X["RTx"][sl, pc, cl], v3(tR)[sl], v3(Epos)[sl], ALU.mult, ["tR", "Epos"], [("RTx", cc)])
                    ACT(X["vTx"][sl, pc, cl], v3(tV)[sl], AF.Copy, ["tV"], [("vTx", cc)])

            for cc in range(4):
                stage1(cc)
            conv_mm()
            conv_ln()
            if STOP <= 4:
                return

            def allk(n):
                return [(n, cc) for cc in range(4)]

            def lock_mm(lt, lkey, rt, rkey, ncol=128):
                outs = []
                if ncol == 64:
                    ps, pk = nextB()
                    for pc_ in range(NP):
                        cc_ = pc_ // NCH
                        MM(ps[:, pc_ * 64:(pc_ + 1) * 64], lt[:, pc_, :], rt if rkey is None else rt[:, pc_, :], True, True,
                           [(lkey, cc_)] + ([] if rkey is None else [(rkey, cc_)]), [pk])
                    return [(ps, pk, lambda d: d[:, :, :])]
                banks = [nextB(), nextB()]
                for pc_ in range(NP):
                    ps, pk = banks[pc_ % 2]
                    q = pc_ // 2
                    cc_ = pc_ // NCH
                    MM(ps[:, q * 128:(q + 1) * 128], lt[:, pc_, :], rt if rkey is None else rt[:, pc_, :], True, True,
                       [(lkey, cc_)] + ([] if rkey is None else [(rkey, cc_)]), [pk])
                for bi in range(2):
                    ps, pk = banks[bi]
                    outs.append((ps, pk, (lambda d, bi=bi: d.rearrange("p (q two) t -> p two q t", two=2)[:, bi, :, :])))
                return outs

            def intra(lname, rname, mask, dst, dkey):
                for ps, pk, sel in lock_mm(X[lname], lname, X[rname], rname):
                    TT("dve", sel(dst), ps[:, :].rearrange("p (c t) -> p c t", t=128),
                       mask[:, :].unsqueeze(1).broadcast_to([128, NP // 2, 128]), ALU.mult,
                       [pk, "m_su", "m_ue", "m_sl"], allk(dkey))
            intra("BTx", "ATx", m_su, Pm[0], "Pm0")
            intra("ATx", "BTx", m_sl, Ptm[0], "Ptm0")
            intra("BTx", "RTx", m_ue, MrbT, "MrbT")
            intra("KTx", "ATx", m_su, LakT, "LakT")
            intra("KTx", "RTx", m_ue, MrkT, "MrkT")
            TT("dve", TTf[:, :, :], Pm[0][:, :, :], identf[:, :].unsqueeze(1).broadcast_to([128, NP, 128]), ALU.add,
               allk("Pm0") + ["identf"], allk("TTf"))
            ACT(TTb[:, :, :], TTf[:, :, :], AF.Copy, allk("TTf"), allk("TTb"))
            cur = 0
            for lev in range(5):
                nxt = 1 - cur
                pck, ptk, pnk, ptnk = PMK[cur], PTK[cur], PMK[nxt], PTK[nxt]
                if lev < 4:
                    for ps, pk, sel in lock_mm(Ptm[cur], ptk, Pm[cur], pck):
                        ACT(sel(Pm[nxt]), ps[:, :].rearrange("p (c t) -> p c t", t=128), AF.Copy, [pk] + allk(ptk), allk(pnk))
                for ps, pk, sel in lock_mm(Pm[cur], pck, Ptm[cur], ptk):
                    ACT(sel(Ptm[nxt]), ps[:, :].rearrange("p (c t) -> p c t", t=128), AF.Copy, [pk], allk(ptnk))
                for ps, pk, sel in lock_mm(Ptm[nxt], ptnk, TTb, "TTb"):
                    TT("dve", sel(TTf), sel(TTf), ps[:, :].rearrange("p (c t) -> p c t", t=128), ALU.add,
                       allk("TTf") + [pk], allk("TTf"))
                ACT(TTb[:, :, :], TTf[:, :, :], AF.Copy, allk("TTf"), allk("TTb"))
                cur = nxt
            if STOP <= 5:
                return
            for (src, dst, dkey) in (("KhTx", Khx, "Pm0"), ("BhTx", Bhx, "Ptm0")):
                for ps, pk, sel in lock_mm(X[src], src, identb[:, :], None):
                    ACT(sel(dst), ps[:, :].rearrange("p (c t) -> p c t", t=128), AF.Copy, [pk], allk(dkey))
            for ps, pk, sel in lock_mm(X["vTx"], "vTx", stackI[:, :], None, ncol=64):
                ACT(V2[:, :, :], ps[:, :].rearrange("p (c t) -> p c t", t=64), AF.Copy, [pk], allk("V2"))
            for c4 in range(NCH):
                for cc in range(4):
                    pc_ = cc * NCH + c4
                    MM(psC[:, cc * 64:(cc + 1) * 64], X["ATx"][:, pc_, :], STb[:, cc, :], True, False, [("ATx", cc), "STb"], ["psCx"])
                    MM(psC[:, cc * 64:(cc + 1) * 64], LakT[:, pc_, :], V2[:, pc_, :], False, True, [("LakT", cc), ("V2", cc)], ["psCx"])
                ACT(X1b[:, :, :], psC[:, 0:256].rearrange("p (c t) -> p c t", t=64), AF.Copy, ["psCx"], ["X1b"])
                for cc in range(4):
                    pc_ = cc * NCH + c4
                    MM(psC[:, 256 + cc * 64:256 + (cc + 1) * 64], TTb[:, pc_, :], X1b[:, cc, :], True, True, [("TTb", cc), "X1b"], ["psCu"])
                ACT(Ub[:, :, :], psC[:, 256:512].rearrange("p (c t) -> p c t", t=64), AF.Copy, ["psCu"], ["Ub"])
                for cc in range(4):
                    pc_ = cc * NCH + c4
                    yk = ("psDy", 0)
                    MM(psD[:, pc_ * 64:(pc_ + 1) * 64], X["RTx"][:, pc_, :], STb[:, cc, :], True, False, [("RTx", cc), "STb"], [yk])
                    MM(psD[:, pc_ * 64:(pc_ + 1) * 64], MrbT[:, pc_, :], Ub[:, cc, :], False, False, [("MrbT", cc), "Ub"], [yk])
                    MM(psD[:, pc_ * 64:(pc_ + 1) * 64], MrkT[:, pc_, :], V2[:, pc_, :], False, True, [("MrkT", cc), ("V2", cc)], [yk])
                for cc in range(4):
                    pc_ = cc * NCH + c4
                    MM(psC[:, cc * 64:(cc + 1) * 64], Khx[:, pc_, :], V2[:, pc_, :], True, False, [("Pm0", cc), ("V2", cc)], ["psCs"])
                    MM(psC[:, cc * 64:(cc + 1) * 64], Bhx[:, pc_, :], Ub[:, cc, :], False, True, [("Ptm0", cc), "Ub"], ["psCs"])
                TT("dve", STf[:, :, :], STf[:, :, :], WCs[:, :, c4:c4 + 1].broadcast_to([128, 4, 64]), ALU.mult,
                   ["STf"] + allk("WCs"), ["STf"])
                TT("dve", STf[:, :, :], STf[:, :, :], psC[:, 0:256].rearrange("p (c t) -> p c t", t=64), ALU.add, ["STf", "psCs"], ["STf"])
                ACT(STb[:, :, :], STf[:, :, :], AF.Copy, ["STf"], ["STb"])
            yks = [("psDy", 0)]
            y3 = psD[:, 0:NP * 64].rearrange("p (c t) -> p c t", t=64)
            P.add("dve", lambda e: e.tensor_reduce(out=ystat[:, 0:NP], in_=y3, axis=AX.X, op=ALU.add), yks, ["ystat"])
            TS("dve", ystat[:, 0:NP], ystat[:, 0:NP], 1.0 / 64, ALU.mult, ["ystat"], ["ystat"])
            TT("dve", ycen[:, :, :], y3, ystat[:, 0:NP].unsqueeze(2).broadcast_to([128, NP, 64]), ALU.subtract, yks + ["ystat"], ["ycen"])
            ACT(ysq[:, :, :], ycen[:, :, :], AF.Square, ["ycen"], ["ysq"])
            P.add("dve", lambda e: e.tensor_reduce(out=ystat[:, 16:16 + NP], in_=ysq[:, :, :], axis=AX.X, op=ALU.add), ["ysq"], ["ystat2"])
            TS("dve", ystat[:, 16:16 + NP], ystat[:, 16:16 + NP], 1.0 / 64, ALU.mult, ["ystat2"], ["ystat2"], s2=64e-5, op1=ALU.add)
            POW(ystat[:, 16:16 + NP], ystat[:, 16:16 + NP], ["ystat2"], ["ystat2"], NP)
            for h in range(2):
                sl = slice(h * 64, (h + 1) * 64)
                TT("dve", ynx[sl, :, h * 64:(h + 1) * 64], ycen[sl, :, :],
                   ystat[sl, 16:16 + NP].unsqueeze(2).broadcast_to([64, NP, 64]), ALU.mult, ["ycen", "ystat2"], [("ynx_", cc) for cc in range(4)])
            (ps, pk, sel), = lock_mm(ynx, "ynx_", stackI[:, :], None, ncol=64)
            psv_ = ps[:, :].rearrange("p (c t) -> p c t", t=TB)
            gng = cst[:, CO["gng"]:CO["gng"] + 4].unsqueeze(2).broadcast_to([128, 4, TB])
            gnb = cst[:, CO["gnb"]:CO["gnb"] + 4].unsqueeze(2).broadcast_to([128, 4, TB])
            TT("dve", t3[:, :, :], psv_, gng, ALU.mult, [pk, "cst"], ["t3"])
            TT("dve", t3[:, :, :], t3[:, :, :], gnb, ALU.add, ["t3", "cst"], ["t3"])
            TT("dve", t3[:, :, :], t3[:, :, :], bon[:, :, :], ALU.add, ["t3"] + allk("bon"), ["t3"])
            TT("dve", mixT[:, 0:4, :], t3[:, :, :], gT[:, :, :], ALU.mult, ["t3"] + allk("gT"), [("mixT", c) for c in range(4)])
            if STOP <= 6:
                return
            for m0 in range(0, 8, 4):
                outs = [nextA() for _ in range(4)]
                for k in range(8):
                    for q in range(4):
                        ps, pk = outs[q]
                        m = m0 + q
                        MM(ps[:, 0:TB], Wout[:, k, m * 128:(m + 1) * 128], mixT[:, k, :], k == 0, k == 7, [("Wout", k), ("mixT", k)], [pk])
                for q in range(4):
                    ps, pk = outs[q]
                    m = m0 + q
                    TT("dve", xT[:, m, t0:t0 + TB], xT[:, m, t0:t0 + TB], ps[:, 0:TB], ALU.add, [("xT", m), pk], [("xT", m)])

        def ffn_phase(l):
            for tg in range(4):
                t0 = tg * 512
                rms_stats(t0, 512, 1e-5, rs5[:, :], "rs5")
                POW(rs5[:, :], rs5[:, :], ["rs5"], ["rs5"], 512)
                for k in range(8):
                    STT(h2T[:, k, t0:t0 + 512], xT[:, k, t0:t0 + 512], c_("g2", k), rs5[:, :], ALU.mult, ALU.mult,
                        [("xT", k), "cst", "rs5"], [("h2T", tg)])
            w1v = w1_d[l].rearrange("(k p) n -> p k n", p=128)
            w2v = w2_d[l].rearrange("(k p) n -> p k n", p=128)
            def load_w(hc):
                i = hc % 2
                for k in range(8):
                    P.add("pool", lambda e, k=k, i=i, hc=hc: e.dma_start(out=W1c[i][:, k, :], in_=w1v[:, k, hc * 512:(hc + 1) * 512]),
                          (), [("W1c", i)], dma=("w1", i))
                for k in range(4):
                    P.add("pool", lambda e, k=k, i=i, hc=hc: e.dma_start(out=W2c[i][:, k, :], in_=w2v[:, hc * 4 + k, :]),
                          (), [("W2c", i)], dma=("w2", i))

            allps = psA + psB + [psC, psD, psE]
            fcnt = [0]

            def bankset():
                base = (fcnt[0] % 2) * 4
                fcnt[0] += 1
                return [(allps[base + q], ("psF", base + q)) for q in range(4)]

            def up(hc):
                i = hc % 2
                for hs in range(4):
                    bs = bankset()
                    for k in range(8):
                        for tg in range(4):
                            ps, pk = bs[tg]
                            MM(ps[:, :], W1c[i][:, k, hs * 128:(hs + 1) * 128], h2T[:, k, tg * 512:(tg + 1) * 512], k == 0, k == 7,
                               [("W1c", i), ("h2T", tg)], [pk])
                    for tg in range(4):
                        ps, pk = bs[tg]
                        fr = frl[tg % 2]
                        ACT(fr[:, :], ps[:, :], AF.Relu, [pk], [("frl", tg % 2)])
                        TT("dve", fT[i][:, hs, tg, :], fr[:, :], fr[:, :], ALU.mult, [("frl", tg % 2)], [("fT", i, hs)])

            def down(hc):
                i = hc % 2
                for m in range(8):
                    bs = bankset()
                    for hs in range(4):
                        for tg in range(4):
                            ps, pk = bs[tg]
                            MM(ps[:, :], W2c[i][:, hs, m * 128:(m + 1) * 128], fT[i][:, hs, tg, :], hs == 0, hs == 3,
                               [("W2c", i), ("fT", i, hs)], [pk])
                    for tg in range(4):
                        ps, pk = bs[tg]
                        TT("dve", xT[:, m, tg * 512:(tg + 1) * 512], xT[:, m, tg * 512:(tg + 1) * 512], ps[:, :], ALU.add,
                           [("xT", m), pk], [("xT", m)])

            load_w(0)
            load_w(1)
            up(0)
            for hc in range(8):
                if hc + 1 < 8:
                    up(hc + 1)
                down(hc)
                if hc + 2 < 8:
                    load_w(hc + 2)

        for s in range(nseq):
            load_seq(s)
            for l in range(nlayers):
                P.barrier()
                load_layer_consts(l)
                load_mixer_weights(l)
                for n in XN:
                    MEMSET("dve", X[n][:, :, :], 0.0, [(n, cc) for cc in range(4)])
                MEMSET("dve", ynx[:, :, :], 0.0, [("ynx_", cc) for cc in range(4)])
                MEMSET("dve", carry[:, :], 0.0, [("carry", c) for c in range(14)])
                MEMSET("dve", ubuf[:, :, 0:30], 0.0, [("ubuf", c) for c in range(4)])
                MEMSET("dve", STf[:, :, :], 0.0, ["STf"])
                MEMSET("dve", STb[:, :, :], 0.0, ["STb"])
                for b in range(NBLK if (STOP >= 8 or _os.environ.get('DEV_ALLBLK')) else 1):
                    mixer_block(l, b, b == 0)
                P.barrier()
                if STOP >= 9:
                    ffn_phase(l)
            P.barrier()
            store_seq(s)
        P.add("sp", None, r=[("y", s, t) for s in range(nseq) for t in range(SEQ // 128)])
        stats = P.finalize(st)
        print("PROG", stats)
        P.emit()
    return nc


def host_consts(inp):
    f = np.float32
    cst = np.zeros((L, 128, NCST), f)

    def put(l, name, vec):
        v = np.asarray(vec, f).reshape(-1, 128).T
        cst[l, :, CO[name]:CO[name] + v.shape[1]] = v
    for l in range(L):
        put(l, "g1", inp["norm1_g"][l]); put(l, "g2", inp["norm2_g"][l]); put(l, "mu", inp["mu_shift"][l])
        put(l, "w0", inp["w0"][l]); put(l, "a0", inp["a0"][l]); put(l, "kk", inp["k_k"][l]); put(l, "ka", inp["k_a"][l])
        put(l, "rk", inp["r_k"][l].reshape(-1)); put(l, "gng", inp["gn_g"][l]); put(l, "gnb", inp["gn_b"][l])
        put(l, "dwb", inp["dw_b"][l]); put(l, "clg", inp["cln_g"][l]); put(l, "clb", inp["cln_b"][l])
        dw = np.asarray(inp["dw_w"][l], f)
        cst[l, :, CO["dww"]:CO["dww"] + 124] = dw.T.reshape(4, 128, 31).transpose(1, 0, 2).reshape(128, 124)
    fg = np.ascontiguousarray(np.asarray(inp["final_g"], f).reshape(8, 128).T)
    waup = np.zeros((L, 2, 128, 512), f)
    waup[:, 0, 0:64, :] = np.asarray(inp["w_up"], f)
    waup[:, 1, 64:128, :] = np.asarray(inp["a_up"], f)
    return cst, fg, waup


_NC_CACHE = {}


def kernel(**inputs):
    inp = {k: np.asarray(v) for k, v in inputs.items()}
    n = 8
    cst, fg, waup = host_consts(inp)
    if "nc" not in _NC_CACHE:
        _NC_CACHE["nc"] = build_nc()
    nc = _NC_CACHE["nc"]
    x = np.ascontiguousarray(inp["x"], np.float32)
    shared = dict(cst=cst, fg=fg, waup=waup, gup=np.ascontiguousarray(inp["g_up"], np.float32),
                  w_in=np.ascontiguousarray(inp["w_in"], np.float32), w_out=np.ascontiguousarray(inp["w_out"], np.float32),
                  w_ff1=np.ascontiguousarray(inp["w_ff1"], np.float32), w_ff2=np.ascontiguousarray(inp["w_ff2"], np.float32))
    in_maps = [dict(shared, x=x[2 * c:2 * c + 2]) for c in range(n)]
    res = run_bass_kernel_spmd(nc, in_maps, core_ids=list(range(n)))
    return np.concatenate([r["y"] for r in res.results], axis=0)
```

```python
import numpy as np
from contextlib import ExitStack
import concourse.bass as bass
import concourse.mybir as mybir
from concourse.bass_utils import run_bass_kernel_spmd

F32 = mybir.dt.float32
BF16 = mybir.dt.bfloat16
AF = mybir.ActivationFunctionType
ALU = mybir.AluOpType
AX = mybir.AxisListType

D = 1024
SEQ = 2048
L = 2
INC = 2816
DFF = 4096
TB = 128
NBLK = SEQ // TB
NCH = TB // 64
C0 = float(np.exp(-0.5))

import os as _os
SAME_ENGINE_SYNC = bool(int(_os.environ.get("SES", "1")))

STOP = float(_os.environ.get('DEV_STOP', '99'))
SEM_ROTATE = 20000

CO = {}
_o = 0
for _n, _w in (("g1", 8), ("g2", 8), ("mu", 14), ("w0", 4), ("a0", 4), ("kk", 4), ("ka", 4), ("rk", 4),
               ("gng", 4), ("gnb", 4), ("dwb", 4), ("clg", 4), ("clb", 4), ("dww", 124)):
    CO[_n] = _o
    _o += _w
NCST = _o


class Prog:
    QUEUES = ("pe", "act", "dve", "pool", "sp")

    def __init__(self, nc):
        self.nc = nc
        self.ops = []
        self.last_w = {}
        self.readers = {}
        self.last_q = {}
        self.last_dma = {}

    @staticmethod
    def _bank(k):
        if isinstance(k, tuple) and k[0] in ("psA", "psB", "psF"):
            return ("bank", k[1])
        if k in ("psE", "psE2"):
            return ("bank", 7)
        if k in ("psCx", "psCu", "psCs"):
            return ("bank", 5)
        if isinstance(k, tuple) and k[0] == "psDy":
            return ("bank", 6)
        return None

    def add(self, eng, fn, r=(), w=(), dma=None, extra=()):
        i = len(self.ops)
        deps = set(extra)
        banks = [self._bank(k) for k in list(r) + list(w)]
        banks = [b for b in banks if b is not None]
        if banks:
            r = [k for k in r if self._bank(k) is None]
            w = [k for k in w if self._bank(k) is None] + sorted(set(banks), key=str)
        for k in r:
            j = self.last_w.get(k)
            if j is not None:
                deps.add(j)
        for k in w:
            j = self.last_w.get(k)
            if j is not None:
                deps.add(j)
            deps.update(self.readers.get(k, ()))
        for k in r:
            self.readers.setdefault(k, []).append(i)
        for k in w:
            self.last_w[k] = i
            self.readers[k] = []
        deps.discard(i)
        self.ops.append(dict(eng=eng, fn=fn, deps=deps, dma=dma, sig=False))
        if dma is not None:
            self.last_dma[dma] = i
        elif fn is not None:
            self.last_q[eng] = i
        return i

    def barrier(self):
        ex = set(self.last_q.values()) | set(self.last_dma.values())
        for q in self.QUEUES:
            self.add(q, None, extra=ex)

    def finalize(self, stack):
        nc = self.nc
        ops = self.ops
        for i, o in enumerate(ops):
            for j in o["deps"]:
                d = ops[j]
                if d["dma"] is not None:
                    continue
                if d["eng"] == o["eng"] and (d["eng"] == "pe" or not SAME_ENGINE_SYNC):
                    continue
                d["sig"] = True
        eng_sem, eng_cnt, dma_sem, dma_cnt = {}, {}, {}, {}
        nsem = [0]

        def new_sem(name):
            nsem[0] += 1
            return stack.enter_context(nc.semaphore(name))

        for i, o in enumerate(ops):
            if o["dma"] is not None:
                key = o["dma"]
                if key not in dma_sem:
                    dma_sem[key] = new_sem("d%d" % len(dma_sem))
                    dma_cnt[key] = 0
                dma_cnt[key] += 16
                o["sem"] = dma_sem[key]
                o["val"] = dma_cnt[key]
            elif o["sig"]:
                e = o["eng"]
                if e not in eng_sem or eng_cnt[e] >= SEM_ROTATE:
                    eng_sem[e] = new_sem("e%s%d" % (e, nsem[0]))
                    eng_cnt[e] = 0
                eng_cnt[e] += 1
                o["sem"] = eng_sem[e]
                o["val"] = eng_cnt[e]
        known = {q: {} for q in self.QUEUES}
        latest_dma = {}
        nwaits = 0
        for i, o in enumerate(ops):
            waits = {}
            for j in o["deps"]:
                d = ops[j]
                if d["dma"] is not None:
                    sem, val = d["sem"], latest_dma[d["dma"]]
                else:
                    if not d["sig"]:
                        continue
                    sem, val = d["sem"], d["val"]
                sid = id(sem)
                if sid not in waits or waits[sid][1] < val:
                    waits[sid] = (sem, val)
            kn = known[o["eng"]]
            wl = []
            for sid, (sem, val) in waits.items():
                if kn.get(sid, 0) >= val:
                    continue
                kn[sid] = val
                wl.append((sem, val))
            o["waits"] = wl
            nwaits += len(wl)
            if o["dma"] is not None:
                latest_dma[o["dma"]] = o["val"]
        self.stats = dict(n_ops=len(ops), n_sems=nsem[0], n_waits=nwaits,
                          per_eng={q: sum(1 for o in ops if o["eng"] == q) for q in self.QUEUES})
        return self.stats

    def emit_queue(self, q, eng):
        for o in self.ops:
            if o["eng"] != q:
                continue
            for sem, val in o["waits"]:
                eng.wait_ge(sem, val)
            if o["fn"] is None:
                continue
            ins = o["fn"](eng)
            if o["dma"] is not None:
                ins.then_inc(o["sem"], 16)
            elif o["sig"]:
                ins.then_inc(o["sem"], 1)

    def emit(self):
        with self.nc.Block() as block:
            @block.tensor
            def _(e):
                self.emit_queue("pe", e)

            @block.scalar
            def _(e):
                self.emit_queue("act", e)

            @block.vector
            def _(e):
                self.emit_queue("dve", e)

            @block.gpsimd
            def _(e):
                self.emit_queue("pool", e)

            @block.sync
            def _(e):
                self.emit_queue("sp", e)


def build_nc(nseq=2, nlayers=L, dbg=None):
    nc = bass.Bass("TRN2", target_bir_lowering=False)
    x_d = nc.dram_tensor("x", [nseq, SEQ, D], F32, kind="ExternalInput").ap()
    cst_d = nc.dram_tensor("cst", [L, 128, NCST], F32, kind="ExternalInput").ap()
    fg_d = nc.dram_tensor("fg", [128, 8], F32, kind="ExternalInput").ap()
    waup_d = nc.dram_tensor("waup", [L, 2, 128, 512], F32, kind="ExternalInput").ap()
    gup_d = nc.dram_tensor("gup", [L, 128, 512], F32, kind="ExternalInput").ap()
    win_d = nc.dram_tensor("w_in", [L, D, INC], F32, kind="ExternalInput").ap()
    wout_d = nc.dram_tensor("w_out", [L, D, D], F32, kind="ExternalInput").ap()
    w1_d = nc.dram_tensor("w_ff1", [L, D, DFF], F32, kind="ExternalInput").ap()
    w2_d = nc.dram_tensor("w_ff2", [L, DFF, D], F32, kind="ExternalInput").ap()
    y_d = nc.dram_tensor("y", [nseq, SEQ, D], F32, kind="ExternalOutput").ap()
    dbg_d = None
    if dbg:
        dbg_d = nc.dram_tensor("dbg", [dbg, 128, SEQ], F32, kind="ExternalOutput").ap()

    st = ExitStack()
    with st:
        def sb(name, shape, dt=F32):
            return st.enter_context(nc.sbuf_tensor(name, shape, dt))

        def psum(name, dt=F32):
            return st.enter_context(nc.psum_tensor(name, [128, 512], dt))

        P = Prog(nc)

        def TT(eng, out, a, b, op, r, w):
            P.add(eng, lambda e: e.tensor_tensor(out=out, in0=a, in1=b, op=op), r, w)

        def TS(eng, out, a, s1, op0, r, w, s2=None, op1=None):
            if op1 is None:
                P.add(eng, lambda e: e.tensor_scalar(out=out, in0=a, scalar1=s1, scalar2=None, op0=op0), r, w)
            else:
                P.add(eng, lambda e: e.tensor_scalar(out=out, in0=a, scalar1=s1, scalar2=s2, op0=op0, op1=op1), r, w)

        def STT(out, a, s, b, op0, op1, r, w):
            P.add("dve", lambda e: e.scalar_tensor_tensor(out=out, in0=a, scalar=s, in1=b, op0=op0, op1=op1), r, w)

        def ACT(out, in_, func, r, w, bias=None, scale=None):
            kw = {}
            if bias is not None:
                kw["bias"] = bias
            if scale is not None:
                kw["scale"] = scale
            P.add("act", lambda e: e.activation(out=out, in_=in_, func=func, **kw), r, w)

        def MM(ps, lhsT, rhs, start, stop, r, w):
            P.add("pe", lambda e: e.matmul(ps, lhsT, rhs, start=start, stop=stop), r, w)

        def MEMSET(eng, ap, val, w):
            P.add(eng, lambda e: e.memset(ap, val), (), w)

        def POW(out, a, r, w, n):
            ACT(out, a, AF.Sqrt, r, w)
            P.add("dve", lambda e: e.reciprocal(out=out, in_=out), w, w)

        xT = sb("xT", [128, 8, SEQ])
        arena = sb("arena", [128, 32768], BF16)
        Win = arena[:, 0:8 * INC].rearrange("p (k n) -> p k n", k=8)
        Wout = arena[:, 8 * INC:8 * INC + 8 * D].rearrange("p (k n) -> p k n", k=8)
        h2T = arena[:, 0:8 * SEQ].rearrange("p (k n) -> p k n", k=8)
        W1c = [arena[:, 8 * SEQ + i * 4096: 8 * SEQ + (i + 1) * 4096].rearrange("p (k n) -> p k n", k=8) for i in range(2)]
        W2c = [arena[:, 8 * SEQ + 8192 + i * 4096: 8 * SEQ + 8192 + (i + 1) * 4096].rearrange("p (k n) -> p k n", k=4)
               for i in range(2)]
        cst = sb("cst_s", [128, NCST])
        cst2 = sb("cst2", [128, 16])
        fg = sb("fgs", [128, 8])
        waup = sb("waup_s", [128, 2, 512], BF16)
        gup = sb("gup_s", [128, 512], BF16)
        identb = sb("identb", [128, 128], BF16)
        identf = sb("identf", [128, 128])
        onesf = sb("onesf", [128, 128])
        onesb = sb("onesb", [128, 128], BF16)
        blk1 = sb("blk1", [128, 128], BF16)
        stackI = sb("stackI", [128, 64], BF16)
        m_su = sb("m_su", [128, 128], BF16)
        m_ue = sb("m_ue", [128, 128], BF16)
        m_sl = sb("m_sl", [128, 128], BF16)
        m01 = sb("m01", [128, 4 * TB])

        NF, NB_ = 6624, 22912
        scrF = sb("scrF", [128, NF])
        scrB = sb("scrB", [128, NB_], BF16)
        aoff = {"F": 0, "B": 0}

        def areset():
            aoff["F"] = 0
            aoff["B"] = 0

        def take(shape, dt=F32):
            kind = "F" if dt == F32 else "B"
            t, cap = (scrF, NF) if kind == "F" else (scrB, NB_)
            size = int(np.prod(shape[1:]))
            o = aoff[kind]
            aoff[kind] = o + ((size + 15) // 16) * 16
            assert aoff[kind] <= cap, (kind, aoff[kind], cap)
            a = t[:, o:o + size]
            if len(shape) == 3:
                a = a.rearrange("p (a b) -> p a b", a=shape[1])
            return a

        def c_(name, j=0):
            o = CO[name] + j
            return cst[:, o:o + 1]

        areset()
        hT = take([128, 8, TB], BF16)
        mixT = take([128, 8, TB], BF16)
        gT = take([128, 4, TB], BF16)
        bon = take([128, 4, TB], BF16)
        ubuf = take([128, 4, 30 + TB], BF16)
        cvz = take([128, 4, TB])
        cvb = take([128, 4, TB])
        cvt = take([128, 4, TB], BF16)
        carry = take([128, 16])
        praw4 = take([128, 4, TB + 2])
        lwin = take([128, TB], BF16)
        gsb = take([128, TB], BF16)
        sqk = [take([128, TB], BF16) for i in range(2)]
        rstd = take([128, TB])
        mean = rstd
        diag = [take([128, 8, 128], BF16) for i in range(2)]
        csm4 = take([128, 4, TB]); Eneg4 = take([128, 4, TB]); rnq = take([128, 4, TB]); t_a = take([128, 4, TB])
        sqb4 = take([128, 4, TB], BF16)
        tmp1 = t_a[:, 0, :]
        gdm = t_a[:, 1, :]
        XN = ("ATx", "BTx", "KTx", "RTx", "KhTx", "BhTx", "vTx")
        X = {n: take([128, 4 * NCH, 128], BF16) for n in XN}
        Pm = [take([128, 4 * NCH, 128], BF16), X["BTx"]]
        Ptm = [take([128, 4 * NCH, 128], BF16), X["KTx"]]
        PMK = ["Pm0", "BTx"]
        PTK = ["Ptm0", "KTx"]
        MrbT = take([128, 4 * NCH, 128], BF16)
        LakT = take([128, 4 * NCH, 128], BF16)
        MrkT = take([128, 4 * NCH, 128], BF16)
        TTf = take([128, 4 * NCH, 128])
        TTb = take([128, 4 * NCH, 128], BF16)
        Khx = Pm[0]
        Bhx = Ptm[0]
        V2 = take([128, 4 * NCH, 64], BF16)
        X1b = take([128, 4, 64], BF16)
        Ub = take([128, 4, 64], BF16)
        STf = take([128, 4, 64])
        STb = take([128, 4, 64], BF16)
        WCs = take([128, 4, NCH])
        ysq = take([128, 4 * NCH, 64])
        ycen = take([128, 4 * NCH, 64])
        ystat = take([128, 32])
        ynx = take([128, 4 * NCH, 128], BF16)
        t3 = take([128, 4, TB])
        r4 = TTf[:, 0:4, :]
        k4 = TTf[:, 4:8, :]
        v4 = ysq.rearrange("p (a b) t -> p a (b t)", a=4)
        lw4 = ycen.rearrange("p (a b) t -> p a (b t)", a=4)
        aa4 = t3
        cs4 = cvz
        K_R = [("TTf", 0), ("TTf", 1)]
        K_K = [("TTf", 2), ("TTf", 3)]
        K_V = ["ysq"]
        K_LW = ["ycen"]
        K_AA = ["t3"]
        K_CS = ["cvz"]
        print("mixer scratch", dict(aoff))
        areset()
        fsq = take([128, 512], BF16)
        frl = [take([128, 512]) for i in range(2)]
        fT = [take([128, 16, 512], BF16).rearrange("p (a b) n -> p a b n", a=4) for i in range(2)]
        rs5 = take([128, 512])
        areset()
        xin = [take([128, D]) for i in range(2)]
        areset()
        yout = [take([128, D]) for i in range(2)]
        hn = take([128, 8, 128])
        rstd_s = take([128, 128])
        sqk_s = [take([128, 128], BF16) for i in range(2)]

        psA = [psum("psA%d" % i) for i in range(3)]
        psB = [psum("psB%d" % i) for i in range(2)]
        psC = psum("psC")
        psD = psum("psD")
        psE = psum("psE")
        cnt = {"A": 0, "B": 0, "praw": 0, "sqk": 0, "diag": 0, "xin": 0, "yout": 0}

        psP = psA + psB

        def nextA():
            i = cnt["A"] % 5
            cnt["A"] += 1
            return psP[i], ("psA", i)

        nextB = nextA

        MEMSET("dve", onesf[:, :], 1.0, ["onesf"])
        MEMSET("dve", onesb[:, :], 1.0, ["onesb"])
        MEMSET("dve", m01[:, :], 1.0, ["m01"])
        MEMSET("dve", m01[:, :].rearrange("p (c t) -> p c t", t=64)[:, :, 0:1], 0.0, ["m01"])
        MEMSET("dve", blk1[:, :], 0.0, ["blk1"])
        MEMSET("dve", blk1[0:64, 0:64], 1.0, ["blk1"])
        MEMSET("dve", blk1[64:128, 64:128], 1.0, ["blk1"])
        P.add("pool", lambda e: e.affine_select(out=m_su[:, :], in_=onesf[:, :], pattern=[[1, 128]], compare_op=ALU.is_gt, fill=0.0,
                                                base=0, channel_multiplier=-1), ["onesf"], ["m_su"])
        P.add("pool", lambda e: e.affine_select(out=m_ue[:, :], in_=onesf[:, :], pattern=[[1, 128]], compare_op=ALU.is_ge, fill=0.0,
                                                base=0, channel_multiplier=-1), ["onesf"], ["m_ue"])
        P.add("pool", lambda e: e.affine_select(out=m_sl[:, :], in_=onesf[:, :], pattern=[[-1, 128]], compare_op=ALU.is_gt, fill=0.0,
                                                base=0, channel_multiplier=1), ["onesf"], ["m_sl"])
        TT("dve", identf[:, :], m_ue[:, :], m_su[:, :], ALU.subtract, ["m_ue", "m_su"], ["identf"])
        P.add("dve", lambda e: e.tensor_copy(out=identb[:, :], in_=identf[:, :]), ["identf"], ["identb"])
        P.add("dve", lambda e: e.tensor_copy(out=stackI[0:64, :], in_=identf[0:64, 0:64]), ["identf"], ["stackI"])
        P.add("dve", lambda e: e.tensor_copy(out=stackI[64:128, :], in_=identf[64:128, 64:128]), ["identf"], ["stackI"])
        for n in XN:
            MEMSET("dve", X[n][:, :, :], 0.0, [(n, cc) for cc in range(4)])
        MEMSET("dve", ynx[:, :, :], 0.0, [("ynx_", cc) for cc in range(4)])
        P.add("sp", lambda e: e.dma_start(out=fg[:, :], in_=fg_d[:, :]), (), ["fg"], dma="c0")

        def dbg_dump(idx, ap, keys, n):
            if dbg_d is None:
                return
            P.add("sp", lambda e: e.dma_start(out=dbg_d[idx, 0:ap.shape[0], 0:n], in_=ap), keys, [("dbg", idx)], dma="dbg")

        def rms_stats(t0, n, eps_rs, rs_out, rs_key, sq_bufs=None):
            ps = psE
            for k in range(8):
                i = cnt["sqk"] % 2
                cnt["sqk"] += 1
                if sq_bufs is not None:
                    s = sq_bufs[i][:, 0:n]
                    skey = ("sqs", i)
                elif n > TB:
                    s = fsq[:, 0:n]
                    skey = "fsq"
                else:
                    s = sqk[i][:, 0:n]
                    skey = ("sqk", i)
                ACT(s, xT[:, k, t0:t0 + n], AF.Square, [("xT", k)], [skey])
                MM(ps[:, 0:n], onesb[:, :], s, k == 0, k == 7, ["onesb", skey], ["psE"])
            TS("dve", rs_out, ps[:, 0:n], 1.0 / D, ALU.mult, ["psE"], [rs_key], s2=1e-5, op1=ALU.add)

        def load_seq(s):
            for tt_ in range(SEQ // 128):
                i = cnt["xin"] % 2
                cnt["xin"] += 1
                P.add("sp", lambda e, i=i, tt_=tt_: e.dma_start(out=xin[i][:, :], in_=x_d[s, tt_ * 128:(tt_ + 1) * 128, :]),
                      (), [("xin", i)], dma=("xin", i))
                for half in range(2):
                    ps, pk = nextA()
                    for k4 in range(4):
                        k = half * 4 + k4
                        P.add("pe", lambda e, ps=ps, k=k, k4=k4, i=i: e.transpose(ps[:, k4 * 128:(k4 + 1) * 128], xin[i][:, k * 128:(k + 1) * 128], identf[:, :]),
                              [("xin", i), "identf"], [pk])
                    P.add("act", lambda e, ps=ps, half=half, tt_=tt_: e.activation(
                        out=xT[:, half * 4:half * 4 + 4, tt_ * 128:(tt_ + 1) * 128],
                        in_=ps[:, :].rearrange("p (k t) -> p k t", k=4), func=AF.Copy),
                        [pk], [("xT", half * 4 + j) for j in range(4)])

        def store_seq(s):
            for tt_ in range(SEQ // 128):
                t0 = tt_ * 128
                rms_stats(t0, 128, 1e-5, rstd_s[:, :], "rstd_s", sq_bufs=sqk_s)
                POW(rstd_s[:, :], rstd_s[:, :], ["rstd_s"], ["rstd_s"], 128)
                for k in range(8):
                    STT(hn[:, k, :], xT[:, k, t0:t0 + 128], fg[:, k:k + 1], rstd_s[:, :], ALU.mult, ALU.mult,
                        [("xT", k), "fg", "rstd_s"], [("hn", k)])
                i = cnt["yout"] % 2
                cnt["yout"] += 1
                for half in range(2):
                    ps, pk = nextA()
                    for k4 in range(4):
                        k = half * 4 + k4
                        P.add("pe", lambda e, ps=ps, k=k, k4=k4: e.transpose(ps[:, k4 * 128:(k4 + 1) * 128], hn[:, k, :], identf[:, :]),
                              [("hn", k), "identf"], [pk])
                    P.add("act", lambda e, ps=ps, half=half, i=i: e.activation(out=yout[i][:, half * 512:(half + 1) * 512], in_=ps[:, :], func=AF.Copy),
                          [pk], [("yout", i)])
                P.add("sp", lambda e, i=i, tt_=tt_: e.dma_start(out=y_d[s, tt_ * 128:(tt_ + 1) * 128, :], in_=yout[i][:, :]),
                      [("yout", i)], [("y", s, tt_)], dma=("yout", i))

        def load_layer_consts(l):
            P.add("sp", lambda e: e.dma_start(out=cst[:, :], in_=cst_d[l, :, :]), (), ["cst"], dma="c0")
            for j in range(2):
                P.add("pool", lambda e, j=j: e.dma_start(out=waup[:, j, :], in_=waup_d[l, j, :, :]), (), ["waup"], dma="c1")
            P.add("pool", lambda e: e.dma_start(out=gup[:, :], in_=gup_d[l, :, :]), (), ["gup"], dma="c1")
            TS("dve", cst2[:, 0:4], cst[:, CO["w0"]:CO["w0"] + 4], 0.5, ALU.mult, ["cst"], ["cst2"])
            TS("dve", cst2[:, 4:8], cst[:, CO["a0"]:CO["a0"] + 4], 0.5, ALU.mult, ["cst"], ["cst2"])

        def load_mixer_weights(l):
            wv = win_d[l].rearrange("(k p) n -> p k n", p=128)
            for k in range(8):
                for c0 in range(0, INC, 704):
                    P.add("pool", lambda e, k=k, c0=c0: e.dma_start(out=Win[:, k, c0:c0 + 704], in_=wv[:, k, c0:c0 + 704]),
                          (), [("Win", k)], dma="win")
            wo = wout_d[l].rearrange("(k p) n -> p k n", p=128)
            for k in range(8):
                P.add("pool", lambda e, k=k: e.dma_start(out=Wout[:, k, :], in_=wo[:, k, :]), (), [("Wout", k)], dma="wout")

        def mixer_block(l, b, first):
            t0 = b * TB
            NP = 4 * NCH
            rms_stats(t0, TB, 1e-5, rstd[:, :], "rstd")
            POW(rstd[:, :], rstd[:, :], ["rstd"], ["rstd"], TB)
            for k in range(8):
                STT(hT[:, k, :], xT[:, k, t0:t0 + TB], c_("g1", k), rstd[:, :], ALU.mult, ALU.mult,
                    [("xT", k), "cst", "rstd"], [("hT", k)])
            if STOP <= 1:
                return

            def inproj_wave(chunks):
                outs = [nextA() for _ in chunks]
                for k in range(8):
                    for (ps, pk), c in zip(outs, chunks):
                        MM(ps[:, 0:TB], Win[:, k, c * 128:(c + 1) * 128], hT[:, k, :], k == 0, k == 7, [("Win", k), ("hT", k)], [pk])
                return outs

            def shift_mix1(c, ps, pk, dst, dkey):
                pr = praw4[:, 0, :]
                prk = "praw4"
                ACT(pr[:, 1:TB + 1], ps[:, 0:TB], AF.Copy, [pk], [prk])
                ACT(pr[:, 0:1], carry[:, c:c + 1], AF.Copy, [("carry", c)], [prk])
                ACT(carry[:, c:c + 1], pr[:, TB:TB + 1], AF.Copy, [prk], [("carry", c)])
                TT("dve", praw4[:, 1, 0:TB], pr[:, 0:TB], pr[:, 1:TB + 1], ALU.subtract, [prk], [prk])
                STT(dst, praw4[:, 1, 0:TB], c_("mu", c), pr[:, 1:TB + 1], ALU.mult, ALU.add, ["cst", prk], [dkey])

            def bc4(name):
                return cst[:, CO[name]:CO[name] + 4].unsqueeze(2).broadcast_to([128, 4, TB])

            def shift_mix4(base, outs, dst, dkeys):
                prk = "praw4"
                for cc in range(4):
                    ps, pk = outs[cc]
                    ACT(praw4[:, cc, 1:TB + 1], ps[:, 0:TB], AF.Copy, [pk], [prk])
                ck = [("carry", base + cc) for cc in range(4)]
                ACT(praw4[:, :, 0:1], carry[:, base:base + 4].unsqueeze(2), AF.Copy, ck, [prk])
                ACT(carry[:, base:base + 4].unsqueeze(2), praw4[:, :, TB:TB + 1], AF.Copy, [prk], ck)
                TT("dve", t_a[:, :, :], praw4[:, :, 0:TB], praw4[:, :, 1:TB + 1], ALU.subtract, [prk], ["t_a"])
                TT("dve", t_a[:, :, :], t_a[:, :, :], cst[:, CO["mu"] + base:CO["mu"] + base + 4].unsqueeze(2).broadcast_to([128, 4, TB]),
                   ALU.mult, ["t_a", "cst"], ["t_a"])
                TT("dve", dst, t_a[:, :, :], praw4[:, :, 1:TB + 1], ALU.add, ["t_a", prk], dkeys)

            (ps, pk), (psgd, pkgd) = inproj_wave([12, 13])
            shift_mix1(12, ps, pk, tmp1, "t_a")
            ACT(lwin[0:64, :], tmp1[0:64, :], AF.Tanh, ["t_a"], ["lwin"])
            ACT(lwin[64:128, :], tmp1[64:128, :], AF.Copy, ["t_a"], ["lwin"])
            shift_mix1(13, psgd, pkgd, gdm, "t_a")
            ACT(gdm, gdm, AF.Tanh, ["t_a"], ["t_a"], scale=0.5)
            TS("dve", gsb[:, :], gdm, 0.5, ALU.mult, ["t_a"], ["gsb"], s2=0.5, op1=ALU.add)
            if STOP <= 2:
                return

            def v3(t):
                return t[:, :].rearrange("p (c t) -> p c t", t=64)

            def conv_glu4():
                for c0 in (0, 2):
                    w = inproj_wave([14 + c0, 15 + c0, 18 + c0, 19 + c0])
                    vals, gates = w[0:2], w[2:4]
                    for j in range(2):
                        ACT(t_a[:, c0 + j, :], gates[j][0][:, 0:TB], AF.Tanh, [gates[j][1]], ["t_a"], scale=0.5)
                    TS("dve", t_a[:, c0:c0 + 2, :], t_a[:, c0:c0 + 2, :], 0.5, ALU.mult, ["t_a"], ["t_a"], s2=0.5, op1=ALU.add)
                    for j in range(2):
                        c = c0 + j
                        TT("dve", ubuf[:, c, 30:30 + TB], vals[j][0][:, 0:TB], t_a[:, c, :], ALU.mult, [vals[j][1], "t_a"], [("ubuf", c)])

            def conv_mm():
                outs = [nextA() for _ in range(4)]
                dww3 = cst[:, CO["dww"]:CO["dww"] + 124].rearrange("p (c k) -> p c k", k=31)
                for g0 in range(0, 31, 2):
                    G = min(2, 31 - g0)
                    i = cnt["diag"] % 2
                    cnt["diag"] += 1
                    dg = diag[i][:, :, :].rearrange("p (c g) n -> p c g n", g=2)
                    TT("dve", dg[:, :, 0:G, :], identf[:, :].unsqueeze(1).unsqueeze(1).broadcast_to([128, 4, G, 128]),
                       dww3[:, :, g0:g0 + G].unsqueeze(3).broadcast_to([128, 4, G, 128]), ALU.mult, ["identf", "cst"], [("diag", i)])
                    for j in range(G):
                        kt = g0 + j
                        for c in range(4):
                            ps, pk = outs[c]
                            MM(ps[:, 0:TB], dg[:, c, j, :], ubuf[:, c, kt:kt + TB], kt == 0, kt == 30, [("diag", i), ("ubuf", c)], [pk])
                for c in range(4):
                    ps, pk = outs[c]
                    ACT(cvb[:, c, :], ps[:, 0:TB], AF.Identity, [pk, "cst"], [("cvb", c)], bias=c_("dwb", c))
                    ACT(cvt[:, c, :], cvb[:, c, :], AF.Copy, [("cvb", c)], [("cvt", c)])
                    ACT(ubuf[:, c, 0:30], ubuf[:, c, TB:TB + 30], AF.Copy, [("ubuf", c)], [("ubuf", c)])

            def conv_ln():
                cvk = [("cvb", c) for c in range(4)]
                for c in range(4):
                    MM(psE[:, 0:TB], onesb[:, :], cvt[:, c, :], c == 0, c == 3, ["onesb", ("cvt", c)], ["psE"])
                TS("dve", mean[:, :], psE[:, 0:TB], 1.0 / 512, ALU.mult, ["psE"], ["rstd"])
                TT("dve", cvb[:, :, :], cvb[:, :, :], mean[:, :].unsqueeze(1).broadcast_to([128, 4, TB]), ALU.subtract, cvk + ["rstd"], cvk)
                ACT(cvt[:, :, :], cvb[:, :, :], AF.Square, cvk, [("cvt", c) for c in range(4)])
                for c in range(4):
                    MM(psE[:, 0:TB], onesb[:, :], cvt[:, c, :], c == 0, c == 3, ["onesb", ("cvt", c)], ["psE"])
                TS("dve", mean[:, :], psE[:, 0:TB], 1.0 / 512, ALU.mult, ["psE"], ["rstd"], s2=1e-5, op1=ALU.add)
                POW(mean[:, :], mean[:, :], ["rstd"], ["rstd"], TB)
                TT("dve", cvb[:, :, :], cvb[:, :, :], mean[:, :].unsqueeze(1).broadcast_to([128, 4, TB]), ALU.mult, cvk + ["rstd"], cvk)
                for c in range(4):
                    TS("dve", cvb[:, c, :], cvb[:, c, :], c_("clg", c), ALU.mult, [("cvb", c), "cst"], [("cvb", c)], s2=c_("clb", c), op1=ALU.add)
                ACT(cvz[:, :, :], cvb[:, :, :], AF.Tanh, cvk, ["cvz"], scale=0.5)
                TS("dve", cvz[:, :, :], cvz[:, :, :], 0.5, ALU.mult, ["cvz"], ["cvz"], s2=0.5, op1=ALU.add)
                TT("dve", mixT[:, 4:8, :], cvb[:, :, :], cvz[:, :, :], ALU.mult, cvk + ["cvz"], [("mixT", 4 + c) for c in range(4)])

            def stage1_all():
                def flat(t):
                    return t.rearrange("p a t -> p (a t)")

                def v8(t):
                    return t.rearrange("p a (c t) -> p (a c) t", t=64)
                shift_mix4(0, inproj_wave([0, 1, 2, 3]), r4, K_R)
                shift_mix4(4, inproj_wave([4, 5, 6, 7]), k4, K_K)
                shift_mix4(8, inproj_wave([8, 9, 10, 11]), v4, K_V)
                ps, pk = nextB()
                for cc in range(4):
                    MM(ps[:, cc * TB:(cc + 1) * TB], waup[:, 0, cc * 128:(cc + 1) * 128], lwin[:, :], True, True, ["waup", "lwin"], [pk])
                ps3 = ps[:, :].rearrange("p (a t) -> p a t", a=4)
                TT("dve", lw4, ps3, bc4("w0"), ALU.add, [pk, "cst"], K_LW)
                if dbg_d is not None and b == 0 and l == 0:
                    dbg_dump(11, lw4.rearrange("p a t -> p (a t)"), K_LW, 512)
                ACT(lw4, lw4, AF.Tanh, K_LW, K_LW, scale=0.5)
                TS("dve", lw4, lw4, -0.5 * C0, ALU.mult, K_LW, K_LW, s2=-0.5 * C0, op1=ALU.add)
                ps, pk = nextB()
                for cc in range(4):
                    MM(ps[:, cc * TB:(cc + 1) * TB], waup[:, 1, cc * 128:(cc + 1) * 128], lwin[:, :], True, True, ["waup", "lwin"], [pk])
                ps3 = ps[:, :].rearrange("p (a t) -> p a t", a=4)
                TT("dve", aa4, ps3, bc4("a0"), ALU.add, [pk, "cst"], K_AA)
                ACT(aa4, aa4, AF.Tanh, K_AA, K_AA, scale=0.5)
                TS("dve", aa4, aa4, 0.5, ALU.mult, K_AA, K_AA, s2=0.5, op1=ALU.add)
                ps, pk = nextB()
                for cc in range(4):
                    MM(ps[:, cc * TB:(cc + 1) * TB], gup[:, cc * 128:(cc + 1) * 128], gsb[:, :], True, True, ["gup", "gsb"], [pk])
                ACT(gT[:, :, :], ps[:, :].rearrange("p (a t) -> p a t", a=4), AF.Copy, [pk], allk("gT"))
                P.add("dve", lambda e: e.tensor_tensor_scan(out=flat(cs4), data0=m01[:, :], data1=flat(lw4), initial=0.0,
                                                            op0=ALU.mult, op1=ALU.add), ["m01"] + K_LW, K_CS)
                TT("dve", csm4, cs4, lw4, ALU.subtract, K_CS + K_LW, ["csm4"])
                ACT(Eneg4, cs4, AF.Exp, K_CS, ["Eneg4"], scale=-1.0)
                ACT(cs4, cs4, AF.Exp, K_CS, K_CS)
                ACT(csm4, csm4, AF.Exp, ["csm4"], ["csm4"])
                Epos4, Eprev4 = cs4, csm4
                ACT(WCs.rearrange("p a c -> p (a c)").unsqueeze(2), v8(Epos4)[:, :, 63:64], AF.Copy, K_CS, allk("WCs"))
                TT("dve", rnq, k4, bc4("kk"), ALU.mult, K_K + ["cst"], ["rnq"])
                ACT(sqb4, rnq, AF.Square, ["rnq"], ["sqb4"])
                ps, pk = nextB()
                for cc in range(4):
                    MM(ps[:, cc * TB:(cc + 1) * TB], blk1[:, :], sqb4[:, cc, :], True, True, ["blk1", "sqb4"], [pk])
                TS("dve", t_a, ps[:, :].rearrange("p (a t) -> p a t", a=4), 1e-24, ALU.max, [pk], ["t_a"])
                POW(t_a, t_a, ["t_a"], ["t_a"], 4 * TB)
                TT("dve", rnq, rnq, t_a, ALU.mult, ["rnq", "t_a"], ["rnq"])
                TT("dve", t_a, aa4, bc4("ka"), ALU.mult, K_AA + ["cst"], ["t_a"])
                TT("dve", t_a, t_a, bc4("ka"), ALU.subtract, ["t_a", "cst"], ["t_a"])
                STT(k4, t_a, 1.0, k4, ALU.add, ALU.mult, ["t_a"] + K_K, K_K)
                TT("dve", t_a, r4, bc4("rk"), ALU.mult, K_R + ["cst"], ["t_a"])
                TT("dve", sqb4, t_a, k4, ALU.mult, ["t_a"] + K_K, ["sqb4"])
                ps, pk = nextB()
                for cc in range(4):
                    MM(ps[:, cc * TB:(cc + 1) * TB], blk1[:, :], sqb4[:, cc, :], True, True, ["blk1", "sqb4"], [pk])
                TT("dve", bon[:, :, :], ps[:, :].rearrange("p (a t) -> p a t", a=4), v4, ALU.mult, [pk] + K_V, allk("bon"))
                TT("dve", t_a, rnq, aa4, ALU.mult, ["rnq"] + K_AA, ["t_a"])
                TT("dve", t_a, t_a, Eneg4, ALU.mult, ["t_a", "Eneg4"], ["t_a"])
                TT("dve", k4, k4, Eneg4, ALU.mult, K_K + ["Eneg4"], K_K)
                for h in range(2):
                    sl = slice(h * 64, (h + 1) * 64)
                    cl = slice(h * 64, (h + 1) * 64)
                    wc = v8(Epos4)[sl, :, 63:64].broadcast_to([64, NP, 64])
                    STT(X["ATx"][sl, :, cl], v8(rnq)[sl], -1.0, v8(Eprev4)[sl], ALU.mult, ALU.mult, ["rnq", "csm4"], allk("ATx"))
                    ACT(X["BTx"][sl, :, cl], v8(t_a)[sl], AF.Copy, ["t_a"], allk("BTx"))
                    TT("dve", X["BhTx"][sl, :, cl], v8(t_a)[sl], wc, ALU.mult, ["t_a"] + K_CS, allk("BhTx"))
                    ACT(X["KTx"][sl, :, cl], v8(k4)[sl], AF.Copy, K_K, allk("KTx"))
                    TT("dve", X["KhTx"][sl, :, cl], v8(k4)[sl], wc, ALU.mult, K_K + K_CS, allk("KhTx"))
                    TT("dve", X["RTx"][sl, :, cl], v8(r4)[sl], v8(Epos4)[sl], ALU.mult, K_R + K_CS, allk("RTx"))
                    ACT(X["vTx"][sl, :, cl], v8(v4)[sl], AF.Copy, K_V, allk("vTx"))

            def allk(n):
                return [(n, cc) for cc in range(4)]

            stage1_all()
            if dbg_d is not None and b == 0 and l == 0:
                fl = lambda t: t.rearrange("p a t -> p (a t)")
                for i, (t, kk_) in enumerate([(r4, K_R), (k4, K_K), (v4, K_V), (lw4, K_LW), (aa4, K_AA), (cs4, K_CS),
                                               (csm4, ["csm4"]), (Eneg4, ["Eneg4"]), (rnq, ["rnq"]), (t_a, ["t_a"])]):
                    dbg_dump(i, fl(t), kk_, 512)
            conv_glu4()
            conv_mm()
            conv_ln()
            if STOP <= 4:
                return

            def allk(n):
                return [(n, cc) for cc in range(4)]

            def lock_mm(lt, lkey, rt, rkey, ncol=128):
                outs = []
                if ncol == 64:
                    ps, pk = nextB()
                    for pc_ in range(NP):
                        cc_ = pc_ // NCH
                        MM(ps[:, pc_ * 64:(pc_ + 1) * 64], lt[:, pc_, :], rt if rkey is None else rt[:, pc_, :], True, True,
                           [(lkey, cc_)] + ([] if rkey is None else [(rkey, cc_)]), [pk])
                    return [(ps, pk, lambda d: d[:, :, :])]
                banks = [nextB(), nextB()]
                for pc_ in range(NP):
                    ps, pk = banks[pc_ % 2]
                    q = pc_ // 2
                    cc_ = pc_ // NCH
                    MM(ps[:, q * 128:(q + 1) * 128], lt[:, pc_, :], rt if rkey is None else rt[:, pc_, :], True, True,
                       [(lkey, cc_)] + ([] if rkey is None else [(rkey, cc_)]), [pk])
                for bi in range(2):
                    ps, pk = banks[bi]
                    outs.append((ps, pk, (lambda d, bi=bi: d.rearrange("p (q two) t -> p two q t", two=2)[:, bi, :, :])))
                return outs

            def intra(lname, rname, mask, dst, dkey):
                for ps, pk, sel in lock_mm(X[lname], lname, X[rname], rname):
                    TT("dve", sel(dst), ps[:, :].rearrange("p (c t) -> p c t", t=128),
                       mask[:, :].unsqueeze(1).broadcast_to([128, NP // 2, 128]), ALU.mult,
                       [pk, "m_su", "m_ue", "m_sl"], allk(dkey))
            intra("BTx", "ATx", m_su, Pm[0], "Pm0")
            intra("ATx", "BTx", m_sl, Ptm[0], "Ptm0")
            intra("BTx", "RTx", m_ue, MrbT, "MrbT")
            intra("KTx", "ATx", m_su, LakT, "LakT")
            intra("KTx", "RTx", m_ue, MrkT, "MrkT")
            TT("dve", TTf[:, :, :], Pm[0][:, :, :], identf[:, :].unsqueeze(1).broadcast_to([128, NP, 128]), ALU.add,
               allk("Pm0") + ["identf"], allk("TTf"))
            ACT(TTb[:, :, :], TTf[:, :, :], AF.Copy, allk("TTf"), allk("TTb"))
            cur = 0
            for lev in range(5):
                nxt = 1 - cur
                pck, ptk, pnk, ptnk = PMK[cur], PTK[cur], PMK[nxt], PTK[nxt]
                if lev < 4:
                    for ps, pk, sel in lock_mm(Ptm[cur], ptk, Pm[cur], pck):
                        ACT(sel(Pm[nxt]), ps[:, :].rearrange("p (c t) -> p c t", t=128), AF.Copy, [pk] + allk(ptk), allk(pnk))
                for ps, pk, sel in lock_mm(Pm[cur], pck, Ptm[cur], ptk):
                    ACT(sel(Ptm[nxt]), ps[:, :].rearrange("p (c t) -> p c t", t=128), AF.Copy, [pk], allk(ptnk))
                for ps, pk, sel in lock_mm(Ptm[nxt], ptnk, TTb, "TTb"):
                    TT("dve", sel(TTf), sel(TTf), ps[:, :].rearrange("p (c t) -> p c t", t=128), ALU.add,
                       allk("TTf") + [pk], allk("TTf"))
                ACT(TTb[:, :, :], TTf[:, :, :], AF.Copy, allk("TTf"), allk("TTb"))
                cur = nxt
            if STOP <= 5:
                return
            for (src, dst, dkey) in (("KhTx", Khx, "Pm0"), ("BhTx", Bhx, "Ptm0")):
                for ps, pk, sel in lock_mm(X[src], src, identb[:, :], None):
                    ACT(sel(dst), ps[:, :].rearrange("p (c t) -> p c t", t=128), AF.Copy, [pk], allk(dkey))
            for ps, pk, sel in lock_mm(X["vTx"], "vTx", stackI[:, :], None, ncol=64):
                ACT(V2[:, :, :], ps[:, :].rearrange("p (c t) -> p c t", t=64), AF.Copy, [pk], allk("V2"))
            for c4 in range(NCH):
                for cc in range(4):
                    pc_ = cc * NCH + c4
                    MM(psC[:, cc * 64:(cc + 1) * 64], X["ATx"][:, pc_, :], STb[:, cc, :], True, False, [("ATx", cc), "STb"], ["psCx"])
                    MM(psC[:, cc * 64:(cc + 1) * 64], LakT[:, pc_, :], V2[:, pc_, :], False, True, [("LakT", cc), ("V2", cc)], ["psCx"])
                ACT(X1b[:, :, :], psC[:, 0:256].rearrange("p (c t) -> p c t", t=64), AF.Copy, ["psCx"], ["X1b"])
                for cc in range(4):
                    pc_ = cc * NCH + c4
                    MM(psC[:, 256 + cc * 64:256 + (cc + 1) * 64], TTb[:, pc_, :], X1b[:, cc, :], True, True, [("TTb", cc), "X1b"], ["psCu"])
                ACT(Ub[:, :, :], psC[:, 256:512].rearrange("p (c t) -> p c t", t=64), AF.Copy, ["psCu"], ["Ub"])
                for cc in range(4):
                    pc_ = cc * NCH + c4
                    yk = ("psDy", 0)
                    MM(psD[:, pc_ * 64:(pc_ + 1) * 64], X["RTx"][:, pc_, :], STb[:, cc, :], True, False, [("RTx", cc), "STb"], [yk])
                    MM(psD[:, pc_ * 64:(pc_ + 1) * 64], MrbT[:, pc_, :], Ub[:, cc, :], False, False, [("MrbT", cc), "Ub"], [yk])
                    MM(psD[:, pc_ * 64:(pc_ + 1) * 64], MrkT[:, pc_, :], V2[:, pc_, :], False, True, [("MrkT", cc), ("V2", cc)], [yk])
                for cc in range(4):
                    pc_ = cc * NCH + c4
                    MM(psC[:, cc * 64:(cc + 1) * 64], Khx[:, pc_, :], V2[:, pc_, :], True, False, [("Pm0", cc), ("V2", cc)], ["psCs"])
                    MM(psC[:, cc * 64:(cc + 1) * 64], Bhx[:, pc_, :], Ub[:, cc, :], False, True, [("Ptm0", cc), "Ub"], ["psCs"])
                TT("dve", STf[:, :, :], STf[:, :, :], WCs[:, :, c4:c4 + 1].broadcast_to([128, 4, 64]), ALU.mult,
                   ["STf"] + allk("WCs"), ["STf"])
                TT("dve", STf[:, :, :], STf[:, :, :], psC[:, 0:256].rearrange("p (c t) -> p c t", t=64), ALU.add, ["STf", "psCs"], ["STf"])
                ACT(STb[:, :, :], STf[:, :, :], AF.Copy, ["STf"], ["STb"])
            yks = [("psDy", 0)]
            y3 = psD[:, 0:NP * 64].rearrange("p (c t) -> p c t", t=64)
            P.add("dve", lambda e: e.tensor_reduce(out=ystat[:, 0:NP], in_=y3, axis=AX.X, op=ALU.add), yks, ["ystat"])
            TS("dve", ystat[:, 0:NP], ystat[:, 0:NP], 1.0 / 64, ALU.mult, ["ystat"], ["ystat"])
            TT("dve", ycen[:, :, :], y3, ystat[:, 0:NP].unsqueeze(2).broadcast_to([128, NP, 64]), ALU.subtract, yks + ["ystat"], ["ycen"])
            ACT(ysq[:, :, :], ycen[:, :, :], AF.Square, ["ycen"], ["ysq"])
            P.add("dve", lambda e: e.tensor_reduce(out=ystat[:, 16:16 + NP], in_=ysq[:, :, :], axis=AX.X, op=ALU.add), ["ysq"], ["ystat2"])
            TS("dve", ystat[:, 16:16 + NP], ystat[:, 16:16 + NP], 1.0 / 64, ALU.mult, ["ystat2"], ["ystat2"], s2=64e-5, op1=ALU.add)
            POW(ystat[:, 16:16 + NP], ystat[:, 16:16 + NP], ["ystat2"], ["ystat2"], NP)
            for h in range(2):
                sl = slice(h * 64, (h + 1) * 64)
                TT("dve", ynx[sl, :, h * 64:(h + 1) * 64], ycen[sl, :, :],
                   ystat[sl, 16:16 + NP].unsqueeze(2).broadcast_to([64, NP, 64]), ALU.mult, ["ycen", "ystat2"], [("ynx_", cc) for cc in range(4)])
            (ps, pk, sel), = lock_mm(ynx, "ynx_", stackI[:, :], None, ncol=64)
            psv_ = ps[:, :].rearrange("p (c t) -> p c t", t=TB)
            gng = cst[:, CO["gng"]:CO["gng"] + 4].unsqueeze(2).broadcast_to([128, 4, TB])
            gnb = cst[:, CO["gnb"]:CO["gnb"] + 4].unsqueeze(2).broadcast_to([128, 4, TB])
            TT("dve", t3[:, :, :], psv_, gng, ALU.mult, [pk, "cst"], ["t3"])
            TT("dve", t3[:, :, :], t3[:, :, :], gnb, ALU.add, ["t3", "cst"], ["t3"])
            TT("dve", t3[:, :, :], t3[:, :, :], bon[:, :, :], ALU.add, ["t3"] + allk("bon"), ["t3"])
            TT("dve", mixT[:, 0:4, :], t3[:, :, :], gT[:, :, :], ALU.mult, ["t3"] + allk("gT"), [("mixT", c) for c in range(4)])
            if STOP <= 6:
                return
            for m0 in range(0, 8, 4):
                outs = [nextA() for _ in range(4)]
                for k in range(8):
                    for q in range(4):
                        ps, pk = outs[q]
                        m = m0 + q
                        MM(ps[:, 0:TB], Wout[:, k, m * 128:(m + 1) * 128], mixT[:, k, :], k == 0, k == 7, [("Wout", k), ("mixT", k)], [pk])
                for q in range(4):
                    ps, pk = outs[q]
                    m = m0 + q
                    TT("dve", xT[:, m, t0:t0 + TB], xT[:, m, t0:t0 + TB], ps[:, 0:TB], ALU.add, [("xT", m), pk], [("xT", m)])

        def ffn_phase(l):
            for tg in range(4):
                t0 = tg * 512
                rms_stats(t0, 512, 1e-5, rs5[:, :], "rs5")
                POW(rs5[:, :], rs5[:, :], ["rs5"], ["rs5"], 512)
                for k in range(8):
                    STT(h2T[:, k, t0:t0 + 512], xT[:, k, t0:t0 + 512], c_("g2", k), rs5[:, :], ALU.mult, ALU.mult,
                        [("xT", k), "cst", "rs5"], [("h2T", tg)])
            w1v = w1_d[l].rearrange("(k p) n -> p k n", p=128)
            w2v = w2_d[l].rearrange("(k p) n -> p k n", p=128)
            def load_w(hc):
                i = hc % 2
                for k in range(8):
                    P.add("pool", lambda e, k=k, i=i, hc=hc: e.dma_start(out=W1c[i][:, k, :], in_=w1v[:, k, hc * 512:(hc + 1) * 512]),
                          (), [("W1c", i)], dma=("w1", i))
                for k in range(4):
                    P.add("pool", lambda e, k=k, i=i, hc=hc: e.dma_start(out=W2c[i][:, k, :], in_=w2v[:, hc * 4 + k, :]),
                          (), [("W2c", i)], dma=("w2", i))

            allps = psA + psB + [psC, psD, psE]
            fcnt = [0]

            def bankset():
                base = (fcnt[0] % 2) * 4
                fcnt[0] += 1
                return [(allps[base + q], ("psF", base + q)) for q in range(4)]

            def up(hc):
                i = hc % 2
                for hs in range(4):
                    bs = bankset()
                    for k in range(8):
                        for tg in range(4):
                            ps, pk = bs[tg]
                            MM(ps[:, :], W1c[i][:, k, hs * 128:(hs + 1) * 128], h2T[:, k, tg * 512:(tg + 1) * 512], k == 0, k == 7,
                               [("W1c", i), ("h2T", tg)], [pk])
                    for tg in range(4):
                        ps, pk = bs[tg]
                        fr = frl[tg % 2]
                        ACT(fr[:, :], ps[:, :], AF.Relu, [pk], [("frl", tg % 2)])
                        TT("dve", fT[i][:, hs, tg, :], fr[:, :], fr[:, :], ALU.mult, [("frl", tg % 2)], [("fT", i, hs)])

            def down(hc):
                i = hc % 2
                for m in range(8):
                    bs = bankset()
                    for hs in range(4):
                        for tg in range(4):
                            ps, pk = bs[tg]
                            MM(ps[:, :], W2c[i][:, hs, m * 128:(m + 1) * 128], fT[i][:, hs, tg, :], hs == 0, hs == 3,
                               [("W2c", i), ("fT", i, hs)], [pk])
                    for tg in range(4):
                        ps, pk = bs[tg]
                        TT("dve", xT[:, m, tg * 512:(tg + 1) * 512], xT[:, m, tg * 512:(tg + 1) * 512], ps[:, :], ALU.add,
                           [("xT", m), pk], [("xT", m)])

            load_w(0)
            load_w(1)
            up(0)
            for hc in range(8):
                if hc + 1 < 8:
                    up(hc + 1)
                down(hc)
                if hc + 2 < 8:
                    load_w(hc + 2)

        for s in range(nseq):
            load_seq(s)
            for l in range(nlayers):
                P.barrier()
                load_layer_consts(l)
                load_mixer_weights(l)
                for n in XN:
                    MEMSET("dve", X[n][:, :, :], 0.0, [(n, cc) for cc in range(4)])
                MEMSET("dve", ynx[:, :, :], 0.0, [("ynx_", cc) for cc in range(4)])
                MEMSET("dve", carry[:, :], 0.0, [("carry", c) for c in range(14)])
                MEMSET("dve", ubuf[:, :, 0:30], 0.0, [("ubuf", c) for c in range(4)])
                MEMSET("dve", STf[:, :, :], 0.0, ["STf"])
                MEMSET("dve", STb[:, :, :], 0.0, ["STb"])
                for b in range(NBLK if (STOP >= 8 or _os.environ.get('DEV_ALLBLK')) else 1):
                    mixer_block(l, b, b == 0)
                P.barrier()
                if STOP >= 9:
                    ffn_phase(l)
            P.barrier()
            store_seq(s)
        P.add("sp", None, r=[("y", s, t) for s in range(nseq) for t in range(SEQ // 128)])
        stats = P.finalize(st)
        print("PROG", stats)
        P.emit()
    return nc


def host_consts(inp):
    f = np.float32
    cst = np.zeros((L, 128, NCST), f)

    def put(l, name, vec):
        v = np.asarray(vec, f).reshape(-1, 128).T
        cst[l, :, CO[name]:CO[name] + v.shape[1]] = v
    for l in range(L):
        put(l, "g1", inp["norm1_g"][l]); put(l, "g2", inp["norm2_g"][l]); put(l, "mu", inp["mu_shift"][l])
        put(l, "w0", inp["w0"][l]); put(l, "a0", inp["a0"][l]); put(l, "kk", inp["k_k"][l]); put(l, "ka", inp["k_a"][l])
        put(l, "rk", inp["r_k"][l].reshape(-1)); put(l, "gng", inp["gn_g"][l]); put(l, "gnb", inp["gn_b"][l])
        put(l, "dwb", inp["dw_b"][l]); put(l, "clg", inp["cln_g"][l]); put(l, "clb", inp["cln_b"][l])
        dw = np.asarray(inp["dw_w"][l], f)
        cst[l, :, CO["dww"]:CO["dww"] + 124] = dw.T.reshape(4, 128, 31).transpose(1, 0, 2).reshape(128, 124)
    fg = np.ascontiguousarray(np.asarray(inp["final_g"], f).reshape(8, 128).T)
    waup = np.zeros((L, 2, 128, 512), f)
    waup[:, 0, 0:64, :] = np.asarray(inp["w_up"], f)
    waup[:, 1, 64:128, :] = np.asarray(inp["a_up"], f)
    return cst, fg, waup


_NC_CACHE = {}


def kernel(**inputs):
    inp = {k: np.asarray(v) for k, v in inputs.items()}
    n = 8
    cst, fg, waup = host_consts(inp)
    if "nc" not in _NC_CACHE:
        _NC_CACHE["nc"] = build_nc()
    nc = _NC_CACHE["nc"]
    x = np.ascontiguousarray(inp["x"], np.float32)
    shared = dict(cst=cst, fg=fg, waup=waup, gup=np.ascontiguousarray(inp["g_up"], np.float32),
                  w_in=np.ascontiguousarray(inp["w_in"], np.float32), w_out=np.ascontiguousarray(inp["w_out"], np.float32),
                  w_ff1=np.ascontiguousarray(inp["w_ff1"], np.float32), w_ff2=np.ascontiguousarray(inp["w_ff2"], np.float32))
    in_maps = [dict(shared, x=x[2 * c:2 * c + 2]) for c in range(n)]
    res = run_bass_kernel_spmd(nc, in_maps, core_ids=list(range(n)))
    return np.concatenate([r["y"] for r in res.results], axis=0)
```

```python
import numpy as np
from contextlib import ExitStack
import concourse.bass as bass
import concourse.mybir as mybir
from concourse.bass_utils import run_bass_kernel_spmd

F32 = mybir.dt.float32
BF16 = mybir.dt.bfloat16
AF = mybir.ActivationFunctionType
ALU = mybir.AluOpType
AX = mybir.AxisListType

D = 1024
SEQ = 2048
L = 2
INC = 2816
DFF = 4096
TB = 128
NBLK = SEQ // TB
NCH = TB // 64
C0 = float(np.exp(-0.5))

import os as _os
SAME_ENGINE_SYNC = bool(int(_os.environ.get("SES", "1")))

STOP = float(_os.environ.get('DEV_STOP', '99'))
SEM_ROTATE = 20000

CO = {}
_o = 0
for _n, _w in (("g1", 8), ("g2", 8), ("mu", 14), ("w0", 4), ("a0", 4), ("kk", 4), ("ka", 4), ("rk", 4),
               ("gng", 4), ("gnb", 4), ("dwb", 4), ("clg", 4), ("clb", 4), ("dww", 124)):
    CO[_n] = _o
    _o += _w
NCST = _o


class Prog:
    QUEUES = ("pe", "act", "dve", "pool", "sp")

    def __init__(self, nc):
        self.nc = nc
        self.ops = []
        self.last_w = {}
        self.readers = {}
        self.last_q = {}
        self.last_dma = {}

    @staticmethod
    def _bank(k):
        if isinstance(k, tuple) and k[0] in ("psA", "psB", "psF"):
            return ("bank", k[1])
        if k in ("psE", "psE2"):
            return ("bank", 7)
        if k in ("psCx", "psCu", "psCs"):
            return ("bank", 5)
        if isinstance(k, tuple) and k[0] == "psDy":
            return ("bank", 6)
        return None

    def add(self, eng, fn, r=(), w=(), dma=None, extra=()):
        i = len(self.ops)
        deps = set(extra)
        banks = [self._bank(k) for k in list(r) + list(w)]
        banks = [b for b in banks if b is not None]
        if banks:
            r = [k for k in r if self._bank(k) is None]
            w = [k for k in w if self._bank(k) is None] + sorted(set(banks), key=str)
        for k in r:
            j = self.last_w.get(k)
            if j is not None:
                deps.add(j)
        for k in w:
            j = self.last_w.get(k)
            if j is not None:
                deps.add(j)
            deps.update(self.readers.get(k, ()))
        for k in r:
            self.readers.setdefault(k, []).append(i)
        for k in w:
            self.last_w[k] = i
            self.readers[k] = []
        deps.discard(i)
        self.ops.append(dict(eng=eng, fn=fn, deps=deps, dma=dma, sig=False))
        if dma is not None:
            self.last_dma[dma] = i
        elif fn is not None:
            self.last_q[eng] = i
        return i

    def barrier(self):
        ex = set(self.last_q.values()) | set(self.last_dma.values())
        for q in self.QUEUES:
            self.add(q, None, extra=ex)

    def finalize(self, stack):
        nc = self.nc
        ops = self.ops
        for i, o in enumerate(ops):
            for j in o["deps"]:
                d = ops[j]
                if d["dma"] is not None:
                    continue
                if d["eng"] == o["eng"] and (d["eng"] == "pe" or not SAME_ENGINE_SYNC):
                    continue
                d["sig"] = True
        eng_sem, eng_cnt, dma_sem, dma_cnt = {}, {}, {}, {}
        nsem = [0]

        def new_sem(name):
            nsem[0] += 1
            return stack.enter_context(nc.semaphore(name))

        for i, o in enumerate(ops):
            if o["dma"] is not None:
                key = o["dma"]
                if key not in dma_sem:
                    dma_sem[key] = new_sem("d%d" % len(dma_sem))
                    dma_cnt[key] = 0
                dma_cnt[key] += 16
                o["sem"] = dma_sem[key]
                o["val"] = dma_cnt[key]
            elif o["sig"]:
                e = o["eng"]
                if e not in eng_sem or eng_cnt[e] >= SEM_ROTATE:
                    eng_sem[e] = new_sem("e%s%d" % (e, nsem[0]))
                    eng_cnt[e] = 0
                eng_cnt[e] += 1
                o["sem"] = eng_sem[e]
                o["val"] = eng_cnt[e]
        known = {q: {} for q in self.QUEUES}
        latest_dma = {}
        nwaits = 0
        for i, o in enumerate(ops):
            waits = {}
            for j in o["deps"]:
                d = ops[j]
                if d["dma"] is not None:
                    sem, val = d["sem"], latest_dma[d["dma"]]
                else:
                    if not d["sig"]:
                        continue
                    sem, val = d["sem"], d["val"]
                sid = id(sem)
                if sid not in waits or waits[sid][1] < val:
                    waits[sid] = (sem, val)
            kn = known[o["eng"]]
            wl = []
            for sid, (sem, val) in waits.items():
                if kn.get(sid, 0) >= val:
                    continue
                kn[sid] = val
                wl.append((sem, val))
            o["waits"] = wl
            nwaits += len(wl)
            if o["dma"] is not None:
                latest_dma[o["dma"]] = o["val"]
        self.stats = dict(n_ops=len(ops), n_sems=nsem[0], n_waits=nwaits,
                          per_eng={q: sum(1 for o in ops if o["eng"] == q) for q in self.QUEUES})
        return self.stats

    def emit_queue(self, q, eng):
        for o in self.ops:
            if o["eng"] != q:
                continue
            for sem, val in o["waits"]:
                eng.wait_ge(sem, val)
            if o["fn"] is None:
                continue
            ins = o["fn"](eng)
            if o["dma"] is not None:
                ins.then_inc(o["sem"], 16)
            elif o["sig"]:
                ins.then_inc(o["sem"], 1)

    def emit(self):
        with self.nc.Block() as block:
            @block.tensor
            def _(e):
                self.emit_queue("pe", e)

            @block.scalar
            def _(e):
                self.emit_queue("act", e)

            @block.vector
            def _(e):
                self.emit_queue("dve", e)

            @block.gpsimd
            def _(e):
                self.emit_queue("pool", e)

            @block.sync
            def _(e):
                self.emit_queue("sp", e)


def build_nc(nseq=2, nlayers=L, dbg=None):
    nc = bass.Bass("TRN2", target_bir_lowering=False)
    x_d = nc.dram_tensor("x", [nseq, SEQ, D], F32, kind="ExternalInput").ap()
    cst_d = nc.dram_tensor("cst", [L, 128, NCST], F32, kind="ExternalInput").ap()
    fg_d = nc.dram_tensor("fg", [128, 8], F32, kind="ExternalInput").ap()
    waup_d = nc.dram_tensor("waup", [L, 2, 128, 512], F32, kind="ExternalInput").ap()
    gup_d = nc.dram_tensor("gup", [L, 128, 512], F32, kind="ExternalInput").ap()
    win_d = nc.dram_tensor("w_in", [L, D, INC], F32, kind="ExternalInput").ap()
    wout_d = nc.dram_tensor("w_out", [L, D, D], F32, kind="ExternalInput").ap()
    w1_d = nc.dram_tensor("w_ff1", [L, D, DFF], F32, kind="ExternalInput").ap()
    w2_d = nc.dram_tensor("w_ff2", [L, DFF, D], F32, kind="ExternalInput").ap()
    y_d = nc.dram_tensor("y", [nseq, SEQ, D], F32, kind="ExternalOutput").ap()
    dbg_d = None
    if dbg:
        dbg_d = nc.dram_tensor("dbg", [dbg, 128, SEQ], F32, kind="ExternalOutput").ap()

    st = ExitStack()
    with st:
        def sb(name, shape, dt=F32):
            return st.enter_context(nc.sbuf_tensor(name, shape, dt))

        def psum(name, dt=F32):
            return st.enter_context(nc.psum_tensor(name, [128, 512], dt))

        P = Prog(nc)

        def TT(eng, out, a, b, op, r, w):
            P.add(eng, lambda e: e.tensor_tensor(out=out, in0=a, in1=b, op=op), r, w)

        def TS(eng, out, a, s1, op0, r, w, s2=None, op1=None):
            if op1 is None:
                P.add(eng, lambda e: e.tensor_scalar(out=out, in0=a, scalar1=s1, scalar2=None, op0=op0), r, w)
            else:
                P.add(eng, lambda e: e.tensor_scalar(out=out, in0=a, scalar1=s1, scalar2=s2, op0=op0, op1=op1), r, w)

        def STT(out, a, s, b, op0, op1, r, w):
            P.add("dve", lambda e: e.scalar_tensor_tensor(out=out, in0=a, scalar=s, in1=b, op0=op0, op1=op1), r, w)

        def ACT(out, in_, func, r, w, bias=None, scale=None):
            kw = {}
            if bias is not None:
                kw["bias"] = bias
            if scale is not None:
                kw["scale"] = scale
            P.add("act", lambda e: e.activation(out=out, in_=in_, func=func, **kw), r, w)

        def MM(ps, lhsT, rhs, start, stop, r, w):
            P.add("pe", lambda e: e.matmul(ps, lhsT, rhs, start=start, stop=stop), r, w)

        def MEMSET(eng, ap, val, w):
            P.add(eng, lambda e: e.memset(ap, val), (), w)

        def POW(out, a, r, w, n):
            ACT(out, a, AF.Sqrt, r, w)
            P.add("dve", lambda e: e.reciprocal(out=out, in_=out), w, w)

        xT = sb("xT", [128, 8, SEQ])
        arena = sb("arena", [128, 32768], BF16)
        Win = arena[:, 0:8 * INC].rearrange("p (k n) -> p k n", k=8)
        Wout = arena[:, 8 * INC:8 * INC + 8 * D].rearrange("p (k n) -> p k n", k=8)
        h2T = arena[:, 0:8 * SEQ].rearrange("p (k n) -> p k n", k=8)
        W1c = [arena[:, 8 * SEQ + i * 4096: 8 * SEQ + (i + 1) * 4096].rearrange("p (k n) -> p k n", k=8) for i in range(2)]
        W2c = [arena[:, 8 * SEQ + 8192 + i * 4096: 8 * SEQ + 8192 + (i + 1) * 4096].rearrange("p (k n) -> p k n", k=4)
               for i in range(2)]
        cst = sb("cst_s", [128, NCST])
        cst2 = sb("cst2", [128, 16])
        fg = sb("fgs", [128, 8])
        waup = sb("waup_s", [128, 2, 512], BF16)
        gup = sb("gup_s", [128, 512], BF16)
        identb = sb("identb", [128, 128], BF16)
        identf = sb("identf", [128, 128])
        onesf = sb("onesf", [128, 128])
        onesb = sb("onesb", [128, 128], BF16)
        blk1 = sb("blk1", [128, 128], BF16)
        stackI = sb("stackI", [128, 64], BF16)
        m_su = sb("m_su", [128, 128], BF16)
        m_ue = sb("m_ue", [128, 128], BF16)
        m_sl = sb("m_sl", [128, 128], BF16)
        m01 = sb("m01", [128, 4 * TB])

        NF, NB_ = 6624, 22912
        scrF = sb("scrF", [128, NF])
        scrB = sb("scrB", [128, NB_], BF16)
        aoff = {"F": 0, "B": 0}

        def areset():
            aoff["F"] = 0
            aoff["B"] = 0

        def take(shape, dt=F32):
            kind = "F" if dt == F32 else "B"
            t, cap = (scrF, NF) if kind == "F" else (scrB, NB_)
            size = int(np.prod(shape[1:]))
            o = aoff[kind]
            aoff[kind] = o + ((size + 15) // 16) * 16
            assert aoff[kind] <= cap, (kind, aoff[kind], cap)
            a = t[:, o:o + size]
            if len(shape) == 3:
                a = a.rearrange("p (a b) -> p a b", a=shape[1])
            return a

        def c_(name, j=0):
            o = CO[name] + j
            return cst[:, o:o + 1]

        areset()
        hT = take([128, 8, TB], BF16)
        mixT = take([128, 8, TB], BF16)
        gT = take([128, 4, TB], BF16)
        bon = take([128, 4, TB], BF16)
        ubuf = take([128, 4, 30 + TB], BF16)
        cvz = take([128, 4, TB])
        cvb = take([128, 4, TB])
        cvt = take([128, 4, TB], BF16)
        carry = take([128, 16])
        praw4 = take([128, 4, TB + 2])
        lwin = take([128, TB], BF16)
        gsb = take([128, TB], BF16)
        sqk = [take([128, TB], BF16) for i in range(2)]
        rstd = take([128, TB])
        mean = rstd
        diag = [take([128, 8, 128], BF16) for i in range(2)]
        csm4 = take([128, 4, TB]); Eneg4 = take([128, 4, TB]); rnq = take([128, 4, TB]); t_a = take([128, 4, TB])
        sqb4 = take([128, 4, TB], BF16)
        tmp1 = t_a[:, 0, :]
        gdm = t_a[:, 1, :]
        XN = ("ATx", "BTx", "KTx", "RTx", "KhTx", "BhTx", "vTx")
        X = {n: take([128, 4 * NCH, 128], BF16) for n in XN}
        Pm = [take([128, 4 * NCH, 128], BF16), X["BTx"]]
        Ptm = [take([128, 4 * NCH, 128], BF16), X["KTx"]]
        PMK = ["Pm0", "BTx"]
        PTK = ["Ptm0", "KTx"]
        MrbT = take([128, 4 * NCH, 128], BF16)
        LakT = take([128, 4 * NCH, 128], BF16)
        MrkT = take([128, 4 * NCH, 128], BF16)
        TTf = take([128, 4 * NCH, 128])
        TTb = take([128, 4 * NCH, 128], BF16)
        Khx = Pm[0]
        Bhx = Ptm[0]
        V2 = take([128, 4 * NCH, 64], BF16)
        X1b = take([128, 4, 64], BF16)
        Ub = take([128, 4, 64], BF16)
        STf = take([128, 4, 64])
        STb = take([128, 4, 64], BF16)
        WCs = take([128, 4, NCH])
        ysq = take([128, 4 * NCH, 64])
        ycen = take([128, 4 * NCH, 64])
        ystat = take([128, 32])
        ynx = take([128, 4 * NCH, 128], BF16)
        t3 = take([128, 4, TB])
        r4 = TTf[:, 0:4, :]
        k4 = TTf[:, 4:8, :]
        v4 = ysq.rearrange("p (a b) t -> p a (b t)", a=4)
        lw4 = ycen.rearrange("p (a b) t -> p a (b t)", a=4)
        aa4 = t3
        cs4 = cvz
        K_R = [("TTf", 0), ("TTf", 1)]
        K_K = [("TTf", 2), ("TTf", 3)]
        K_V = ["ysq"]
        K_LW = ["ycen"]
        K_AA = ["t3"]
        K_CS = ["cvz"]
        print("mixer scratch", dict(aoff))
        areset()
        fsq = take([128, 512], BF16)
        frl = [take([128, 512]) for i in range(2)]
        fT = [take([128, 16, 512], BF16).rearrange("p (a b) n -> p a b n", a=4) for i in range(2)]
        rs5 = take([128, 512])
        areset()
        xin = [take([128, D]) for i in range(2)]
        areset()
        yout = [take([128, D]) for i in range(2)]
        hn = take([128, 8, 128])
        rstd_s = take([128, 128])
        sqk_s = [take([128, 128], BF16) for i in range(2)]

        psA = [psum("psA%d" % i) for i in range(3)]
        psB = [psum("psB%d" % i) for i in range(2)]
        psC = psum("psC")
        psD = psum("psD")
        psE = psum("psE")
        cnt = {"A": 0, "B": 0, "praw": 0, "sqk": 0, "diag": 0, "xin": 0, "yout": 0}

        psP = psA + psB

        allbanks = psA + psB + [psC, psD, psE]

        def nextA():
            i = cnt["A"] % 4
            cnt["A"] += 1
            return allbanks[i], ("psA", i)

        def nextB():
            i = 4 + cnt["B"] % 3
            cnt["B"] += 1
            return allbanks[i], ("psA", i)

        MEMSET("dve", onesf[:, :], 1.0, ["onesf"])
        MEMSET("dve", onesb[:, :], 1.0, ["onesb"])
        MEMSET("dve", m01[:, :], 1.0, ["m01"])
        MEMSET("dve", m01[:, :].rearrange("p (c t) -> p c t", t=64)[:, :, 0:1], 0.0, ["m01"])
        MEMSET("dve", blk1[:, :], 0.0, ["blk1"])
        MEMSET("dve", blk1[0:64, 0:64], 1.0, ["blk1"])
        MEMSET("dve", blk1[64:128, 64:128], 1.0, ["blk1"])
        P.add("pool", lambda e: e.affine_select(out=m_su[:, :], in_=onesf[:, :], pattern=[[1, 128]], compare_op=ALU.is_gt, fill=0.0,
                                                base=0, channel_multiplier=-1), ["onesf"], ["m_su"])
        P.add("pool", lambda e: e.affine_select(out=m_ue[:, :], in_=onesf[:, :], pattern=[[1, 128]], compare_op=ALU.is_ge, fill=0.0,
                                                base=0, channel_multiplier=-1), ["onesf"], ["m_ue"])
        P.add("pool", lambda e: e.affine_select(out=m_sl[:, :], in_=onesf[:, :], pattern=[[-1, 128]], compare_op=ALU.is_gt, fill=0.0,
                                                base=0, channel_multiplier=1), ["onesf"], ["m_sl"])
        TT("dve", identf[:, :], m_ue[:, :], m_su[:, :], ALU.subtract, ["m_ue", "m_su"], ["identf"])
        P.add("dve", lambda e: e.tensor_copy(out=identb[:, :], in_=identf[:, :]), ["identf"], ["identb"])
        P.add("dve", lambda e: e.tensor_copy(out=stackI[0:64, :], in_=identf[0:64, 0:64]), ["identf"], ["stackI"])
        P.add("dve", lambda e: e.tensor_copy(out=stackI[64:128, :], in_=identf[64:128, 64:128]), ["identf"], ["stackI"])
        for n in XN:
            MEMSET("dve", X[n][:, :, :], 0.0, [(n, cc) for cc in range(4)])
        MEMSET("dve", ynx[:, :, :], 0.0, [("ynx_", cc) for cc in range(4)])
        P.add("sp", lambda e: e.dma_start(out=fg[:, :], in_=fg_d[:, :]), (), ["fg"], dma="c0")

        def dbg_dump(idx, ap, keys, n):
            if dbg_d is None:
                return
            P.add("sp", lambda e: e.dma_start(out=dbg_d[idx, 0:ap.shape[0], 0:n], in_=ap), keys, [("dbg", idx)], dma="dbg")

        def rms_stats(t0, n, eps_rs, rs_out, rs_key, sq_bufs=None):
            ps = psE
            for k in range(8):
                i = cnt["sqk"] % 2
                cnt["sqk"] += 1
                if sq_bufs is not None:
                    s = sq_bufs[i][:, 0:n]
                    skey = ("sqs", i)
                elif n > TB:
                    s = fsq[:, 0:n]
                    skey = "fsq"
                else:
                    s = sqk[i][:, 0:n]
                    skey = ("sqk", i)
                ACT(s, xT[:, k, t0:t0 + n], AF.Square, [("xT", k)], [skey])
                MM(ps[:, 0:n], onesb[:, :], s, k == 0, k == 7, ["onesb", skey], ["psE"])
            TS("dve", rs_out, ps[:, 0:n], 1.0 / D, ALU.mult, ["psE"], [rs_key], s2=1e-5, op1=ALU.add)

        def load_seq(s):
            for tt_ in range(SEQ // 128):
                i = cnt["xin"] % 2
                cnt["xin"] += 1
                P.add("sp", lambda e, i=i, tt_=tt_: e.dma_start(out=xin[i][:, :], in_=x_d[s, tt_ * 128:(tt_ + 1) * 128, :]),
                      (), [("xin", i)], dma=("xin", i))
                for half in range(2):
                    ps, pk = nextA()
                    for k4 in range(4):
                        k = half * 4 + k4
                        P.add("pe", lambda e, ps=ps, k=k, k4=k4, i=i: e.transpose(ps[:, k4 * 128:(k4 + 1) * 128], xin[i][:, k * 128:(k + 1) * 128], identf[:, :]),
                              [("xin", i), "identf"], [pk])
                    P.add("act", lambda e, ps=ps, half=half, tt_=tt_: e.activation(
                        out=xT[:, half * 4:half * 4 + 4, tt_ * 128:(tt_ + 1) * 128],
                        in_=ps[:, :].rearrange("p (k t) -> p k t", k=4), func=AF.Copy),
                        [pk], [("xT", half * 4 + j) for j in range(4)])

        def store_seq(s):
            for tt_ in range(SEQ // 128):
                t0 = tt_ * 128
                rms_stats(t0, 128, 1e-5, rstd_s[:, :], "rstd_s", sq_bufs=sqk_s)
                POW(rstd_s[:, :], rstd_s[:, :], ["rstd_s"], ["rstd_s"], 128)
                for k in range(8):
                    STT(hn[:, k, :], xT[:, k, t0:t0 + 128], fg[:, k:k + 1], rstd_s[:, :], ALU.mult, ALU.mult,
                        [("xT", k), "fg", "rstd_s"], [("hn", k)])
                i = cnt["yout"] % 2
                cnt["yout"] += 1
                for half in range(2):
                    ps, pk = nextA()
                    for k4 in range(4):
                        k = half * 4 + k4
                        P.add("pe", lambda e, ps=ps, k=k, k4=k4: e.transpose(ps[:, k4 * 128:(k4 + 1) * 128], hn[:, k, :], identf[:, :]),
                              [("hn", k), "identf"], [pk])
                    P.add("act", lambda e, ps=ps, half=half, i=i: e.activation(out=yout[i][:, half * 512:(half + 1) * 512], in_=ps[:, :], func=AF.Copy),
                          [pk], [("yout", i)])
                P.add("sp", lambda e, i=i, tt_=tt_: e.dma_start(out=y_d[s, tt_ * 128:(tt_ + 1) * 128, :], in_=yout[i][:, :]),
                      [("yout", i)], [("y", s, tt_)], dma=("yout", i))

        def load_layer_consts(l):
            P.add("sp", lambda e: e.dma_start(out=cst[:, :], in_=cst_d[l, :, :]), (), ["cst"], dma="c0")
            for j in range(2):
                P.add("pool", lambda e, j=j: e.dma_start(out=waup[:, j, :], in_=waup_d[l, j, :, :]), (), ["waup"], dma="c1")
            P.add("pool", lambda e: e.dma_start(out=gup[:, :], in_=gup_d[l, :, :]), (), ["gup"], dma="c1")
            TS("dve", cst2[:, 0:4], cst[:, CO["w0"]:CO["w0"] + 4], 0.5, ALU.mult, ["cst"], ["cst2"])
            TS("dve", cst2[:, 4:8], cst[:, CO["a0"]:CO["a0"] + 4], 0.5, ALU.mult, ["cst"], ["cst2"])

        def load_mixer_weights(l):
            wv = win_d[l].rearrange("(k p) n -> p k n", p=128)
            for k in range(8):
                for c0 in range(0, INC, 704):
                    P.add("pool", lambda e, k=k, c0=c0: e.dma_start(out=Win[:, k, c0:c0 + 704], in_=wv[:, k, c0:c0 + 704]),
                          (), [("Win", k)], dma="win")
            wo = wout_d[l].rearrange("(k p) n -> p k n", p=128)
            for k in range(8):
                P.add("pool", lambda e, k=k: e.dma_start(out=Wout[:, k, :], in_=wo[:, k, :]), (), [("Wout", k)], dma="wout")

        def mixer_block(l, b, first):
            t0 = b * TB
            NP = 4 * NCH
            rms_stats(t0, TB, 1e-5, rstd[:, :], "rstd")
            POW(rstd[:, :], rstd[:, :], ["rstd"], ["rstd"], TB)
            for k in range(8):
                STT(hT[:, k, :], xT[:, k, t0:t0 + TB], c_("g1", k), rstd[:, :], ALU.mult, ALU.mult,
                    [("xT", k), "cst", "rstd"], [("hT", k)])
            if STOP <= 1:
                return

            def inproj_wave(chunks):
                outs = [nextA() for _ in chunks]
                for k in range(8):
                    for (ps, pk), c in zip(outs, chunks):
                        MM(ps[:, 0:TB], Win[:, k, c * 128:(c + 1) * 128], hT[:, k, :], k == 0, k == 7, [("Win", k), ("hT", k)], [pk])
                return outs

            def shift_mix1(c, ps, pk, dst, dkey):
                pr = praw4[:, 0, :]
                prk = "praw4"
                ACT(pr[:, 1:TB + 1], ps[:, 0:TB], AF.Copy, [pk], [prk])
                ACT(pr[:, 0:1], carry[:, c:c + 1], AF.Copy, [("carry", c)], [prk])
                ACT(carry[:, c:c + 1], pr[:, TB:TB + 1], AF.Copy, [prk], [("carry", c)])
                TT("dve", praw4[:, 1, 0:TB], pr[:, 0:TB], pr[:, 1:TB + 1], ALU.subtract, [prk], [prk])
                STT(dst, praw4[:, 1, 0:TB], c_("mu", c), pr[:, 1:TB + 1], ALU.mult, ALU.add, ["cst", prk], [dkey])

            def bc4(name):
                return cst[:, CO[name]:CO[name] + 4].unsqueeze(2).broadcast_to([128, 4, TB])

            def shift_mix4(base, outs, dst, dkeys):
                prk = "praw4"
                for cc in range(4):
                    ps, pk = outs[cc]
                    ACT(praw4[:, cc, 1:TB + 1], ps[:, 0:TB], AF.Copy, [pk], [prk])
                ck = [("carry", base + cc) for cc in range(4)]
                ACT(praw4[:, :, 0:1], carry[:, base:base + 4].unsqueeze(2), AF.Copy, ck, [prk])
                ACT(carry[:, base:base + 4].unsqueeze(2), praw4[:, :, TB:TB + 1], AF.Copy, [prk], ck)
                TT("dve", t_a[:, :, :], praw4[:, :, 0:TB], praw4[:, :, 1:TB + 1], ALU.subtract, [prk], ["t_a"])
                TT("dve", t_a[:, :, :], t_a[:, :, :], cst[:, CO["mu"] + base:CO["mu"] + base + 4].unsqueeze(2).broadcast_to([128, 4, TB]),
                   ALU.mult, ["t_a", "cst"], ["t_a"])
                TT("dve", dst, t_a[:, :, :], praw4[:, :, 1:TB + 1], ALU.add, ["t_a", prk], dkeys)

            (ps, pk), (psgd, pkgd) = inproj_wave([12, 13])
            shift_mix1(12, ps, pk, tmp1, "t_a")
            ACT(lwin[0:64, :], tmp1[0:64, :], AF.Tanh, ["t_a"], ["lwin"])
            ACT(lwin[64:128, :], tmp1[64:128, :], AF.Copy, ["t_a"], ["lwin"])
            shift_mix1(13, psgd, pkgd, gdm, "t_a")
            ACT(gdm, gdm, AF.Tanh, ["t_a"], ["t_a"], scale=0.5)
            TS("dve", gsb[:, :], gdm, 0.5, ALU.mult, ["t_a"], ["gsb"], s2=0.5, op1=ALU.add)
            if STOP <= 2:
                return

            def v3(t):
                return t[:, :].rearrange("p (c t) -> p c t", t=64)

            def conv_stream():
                for c0 in (0, 2):
                    w = inproj_wave([14 + c0, 15 + c0, 18 + c0, 19 + c0])
                    vals, gates = w[0:2], w[2:4]
                    for j in range(2):
                        ACT(t_a[:, c0 + j, :], gates[j][0][:, 0:TB], AF.Tanh, [gates[j][1]], ["t_a"], scale=0.5)
                    TS("dve", t_a[:, c0:c0 + 2, :], t_a[:, c0:c0 + 2, :], 0.5, ALU.mult, ["t_a"], ["t_a"], s2=0.5, op1=ALU.add)
                    for j in range(2):
                        c = c0 + j
                        TT("dve", ubuf[:, c, 30:30 + TB], vals[j][0][:, 0:TB], t_a[:, c, :], ALU.mult, [vals[j][1], "t_a"], [("ubuf", c)])
                    yield
                outs = [nextA() for _ in range(4)]
                dww3 = cst[:, CO["dww"]:CO["dww"] + 124].rearrange("p (c k) -> p c k", k=31)
                for g0 in range(0, 31, 2):
                    G = min(2, 31 - g0)
                    i = cnt["diag"] % 2
                    cnt["diag"] += 1
                    dg = diag[i][:, :, :].rearrange("p (c g) n -> p c g n", g=2)
                    TT("dve", dg[:, :, 0:G, :], identf[:, :].unsqueeze(1).unsqueeze(1).broadcast_to([128, 4, G, 128]),
                       dww3[:, :, g0:g0 + G].unsqueeze(3).broadcast_to([128, 4, G, 128]), ALU.mult, ["identf", "cst"], [("diag", i)])
                    for j in range(G):
                        kt = g0 + j
                        for c in range(4):
                            ps, pk = outs[c]
                            MM(ps[:, 0:TB], dg[:, c, j, :], ubuf[:, c, kt:kt + TB], kt == 0, kt == 30, [("diag", i), ("ubuf", c)], [pk])
                    yield
                for c in range(4):
                    ps, pk = outs[c]
                    ACT(cvb[:, c, :], ps[:, 0:TB], AF.Identity, [pk, "cst"], [("cvb", c)], bias=c_("dwb", c))
                    ACT(cvt[:, c, :], cvb[:, c, :], AF.Copy, [("cvb", c)], [("cvt", c)])
                    ACT(ubuf[:, c, 0:30], ubuf[:, c, TB:TB + 30], AF.Copy, [("ubuf", c)], [("ubuf", c)])
                yield
                cvk = [("cvb", c) for c in range(4)]
                for c in range(4):
                    MM(psE[:, 0:TB], onesb[:, :], cvt[:, c, :], c == 0, c == 3, ["onesb", ("cvt", c)], ["psE"])
                TS("dve", mean[:, :], psE[:, 0:TB], 1.0 / 512, ALU.mult, ["psE"], ["rstd"])
                TT("dve", cvb[:, :, :], cvb[:, :, :], mean[:, :].unsqueeze(1).broadcast_to([128, 4, TB]), ALU.subtract, cvk + ["rstd"], cvk)
                yield
                ACT(cvt[:, :, :], cvb[:, :, :], AF.Square, cvk, [("cvt", c) for c in range(4)])
                for c in range(4):
                    MM(psE[:, 0:TB], onesb[:, :], cvt[:, c, :], c == 0, c == 3, ["onesb", ("cvt", c)], ["psE"])
                TS("dve", mean[:, :], psE[:, 0:TB], 1.0 / 512, ALU.mult, ["psE"], ["rstd"], s2=1e-5, op1=ALU.add)
                POW(mean[:, :], mean[:, :], ["rstd"], ["rstd"], TB)
                yield
                TT("dve", cvb[:, :, :], cvb[:, :, :], mean[:, :].unsqueeze(1).broadcast_to([128, 4, TB]), ALU.mult, cvk + ["rstd"], cvk)
                for c in range(4):
                    TS("dve", cvb[:, c, :], cvb[:, c, :], c_("clg", c), ALU.mult, [("cvb", c), "cst"], [("cvb", c)], s2=c_("clb", c), op1=ALU.add)
                ACT(cvz[:, :, :], cvb[:, :, :], AF.Tanh, cvk, ["cvz"], scale=0.5)
                TS("dve", cvz[:, :, :], cvz[:, :, :], 0.5, ALU.mult, ["cvz"], ["cvz"], s2=0.5, op1=ALU.add)
                TT("dve", mixT[:, 4:8, :], cvb[:, :, :], cvz[:, :, :], ALU.mult, cvk + ["cvz"], [("mixT", 4 + c) for c in range(4)])

            def stage1_all():
                def flat(t):
                    return t.rearrange("p a t -> p (a t)")

                def v8(t):
                    return t.rearrange("p a (c t) -> p (a c) t", t=64)
                shift_mix4(0, inproj_wave([0, 1, 2, 3]), r4, K_R)
                shift_mix4(4, inproj_wave([4, 5, 6, 7]), k4, K_K)
                shift_mix4(8, inproj_wave([8, 9, 10, 11]), v4, K_V)
                ps, pk = nextB()
                for cc in range(4):
                    MM(ps[:, cc * TB:(cc + 1) * TB], waup[:, 0, cc * 128:(cc + 1) * 128], lwin[:, :], True, True, ["waup", "lwin"], [pk])
                ps3 = ps[:, :].rearrange("p (a t) -> p a t", a=4)
                TT("dve", lw4, ps3, bc4("w0"), ALU.add, [pk, "cst"], K_LW)
                if dbg_d is not None and b == 0 and l == 0:
                    dbg_dump(11, lw4.rearrange("p a t -> p (a t)"), K_LW, 512)
                ACT(lw4, lw4, AF.Tanh, K_LW, K_LW, scale=0.5)
                TS("dve", lw4, lw4, -0.5 * C0, ALU.mult, K_LW, K_LW, s2=-0.5 * C0, op1=ALU.add)
                ps, pk = nextB()
                for cc in range(4):
                    MM(ps[:, cc * TB:(cc + 1) * TB], waup[:, 1, cc * 128:(cc + 1) * 128], lwin[:, :], True, True, ["waup", "lwin"], [pk])
                ps3 = ps[:, :].rearrange("p (a t) -> p a t", a=4)
                TT("dve", aa4, ps3, bc4("a0"), ALU.add, [pk, "cst"], K_AA)
                ACT(aa4, aa4, AF.Tanh, K_AA, K_AA, scale=0.5)
                TS("dve", aa4, aa4, 0.5, ALU.mult, K_AA, K_AA, s2=0.5, op1=ALU.add)
                ps, pk = nextB()
                for cc in range(4):
                    MM(ps[:, cc * TB:(cc + 1) * TB], gup[:, cc * 128:(cc + 1) * 128], gsb[:, :], True, True, ["gup", "gsb"], [pk])
                ACT(gT[:, :, :], ps[:, :].rearrange("p (a t) -> p a t", a=4), AF.Copy, [pk], allk("gT"))
                P.add("dve", lambda e: e.tensor_tensor_scan(out=flat(cs4), data0=m01[:, :], data1=flat(lw4), initial=0.0,
                                                            op0=ALU.mult, op1=ALU.add), ["m01"] + K_LW, K_CS)
                TT("dve", csm4, cs4, lw4, ALU.subtract, K_CS + K_LW, ["csm4"])
                ACT(Eneg4, cs4, AF.Exp, K_CS, ["Eneg4"], scale=-1.0)
                ACT(cs4, cs4, AF.Exp, K_CS, K_CS)
                ACT(csm4, csm4, AF.Exp, ["csm4"], ["csm4"])
                Epos4, Eprev4 = cs4, csm4
                ACT(WCs.rearrange("p a c -> p (a c)").unsqueeze(2), v8(Epos4)[:, :, 63:64], AF.Copy, K_CS, allk("WCs"))
                TT("dve", rnq, k4, bc4("kk"), ALU.mult, K_K + ["cst"], ["rnq"])
                ACT(sqb4, rnq, AF.Square, ["rnq"], ["sqb4"])
                ps, pk = nextB()
                for cc in range(4):
                    MM(ps[:, cc * TB:(cc + 1) * TB], blk1[:, :], sqb4[:, cc, :], True, True, ["blk1", "sqb4"], [pk])
                TS("dve", t_a, ps[:, :].rearrange("p (a t) -> p a t", a=4), 1e-24, ALU.max, [pk], ["t_a"])
                POW(t_a, t_a, ["t_a"], ["t_a"], 4 * TB)
                TT("dve", rnq, rnq, t_a, ALU.mult, ["rnq", "t_a"], ["rnq"])
                TT("dve", t_a, aa4, bc4("ka"), ALU.mult, K_AA + ["cst"], ["t_a"])
                TT("dve", t_a, t_a, bc4("ka"), ALU.subtract, ["t_a", "cst"], ["t_a"])
                STT(k4, t_a, 1.0, k4, ALU.add, ALU.mult, ["t_a"] + K_K, K_K)
                TT("dve", t_a, r4, bc4("rk"), ALU.mult, K_R + ["cst"], ["t_a"])
                TT("dve", sqb4, t_a, k4, ALU.mult, ["t_a"] + K_K, ["sqb4"])
                ps, pk = nextB()
                for cc in range(4):
                    MM(ps[:, cc * TB:(cc + 1) * TB], blk1[:, :], sqb4[:, cc, :], True, True, ["blk1", "sqb4"], [pk])
                TT("dve", bon[:, :, :], ps[:, :].rearrange("p (a t) -> p a t", a=4), v4, ALU.mult, [pk] + K_V, allk("bon"))
                TT("dve", t_a, rnq, aa4, ALU.mult, ["rnq"] + K_AA, ["t_a"])
                TT("dve", t_a, t_a, Eneg4, ALU.mult, ["t_a", "Eneg4"], ["t_a"])
                TT("dve", k4, k4, Eneg4, ALU.mult, K_K + ["Eneg4"], K_K)
                for h in range(2):
                    sl = slice(h * 64, (h + 1) * 64)
                    cl = slice(h * 64, (h + 1) * 64)
                    wc = v8(Epos4)[sl, :, 63:64].broadcast_to([64, NP, 64])
                    STT(X["ATx"][sl, :, cl], v8(rnq)[sl], -1.0, v8(Eprev4)[sl], ALU.mult, ALU.mult, ["rnq", "csm4"], allk("ATx"))
                    ACT(X["BTx"][sl, :, cl], v8(t_a)[sl], AF.Copy, ["t_a"], allk("BTx"))
                    TT("dve", X["BhTx"][sl, :, cl], v8(t_a)[sl], wc, ALU.mult, ["t_a"] + K_CS, allk("BhTx"))
                    ACT(X["KTx"][sl, :, cl], v8(k4)[sl], AF.Copy, K_K, allk("KTx"))
                    TT("dve", X["KhTx"][sl, :, cl], v8(k4)[sl], wc, ALU.mult, K_K + K_CS, allk("KhTx"))
                    TT("dve", X["RTx"][sl, :, cl], v8(r4)[sl], v8(Epos4)[sl], ALU.mult, K_R + K_CS, allk("RTx"))
                    ACT(X["vTx"][sl, :, cl], v8(v4)[sl], AF.Copy, K_V, allk("vTx"))

            def allk(n):
                return [(n, cc) for cc in range(4)]

            stage1_all()
            if dbg_d is not None and b == 0 and l == 0:
                fl = lambda t: t.rearrange("p a t -> p (a t)")
                for i, (t, kk_) in enumerate([(r4, K_R), (k4, K_K), (v4, K_V), (lw4, K_LW), (aa4, K_AA), (cs4, K_CS),
                                               (csm4, ["csm4"]), (Eneg4, ["Eneg4"]), (rnq, ["rnq"]), (t_a, ["t_a"])]):
                    dbg_dump(i, fl(t), kk_, 512)
            if STOP <= 4:
                return

            def stage2_stream(g):
                prs = (2 * g, 2 * g + 1)
                lo = 2 * g * NCH
                NS = 2 * NCH
                rs = slice(lo, lo + NS)
                ps2 = slice(2 * g, 2 * g + 2)
                lb = [(allbanks[2 * g], ("psA", 2 * g)), (allbanks[2 * g + 1], ("psA", 2 * g + 1))]
                pC, kC = allbanks[4 + 2 * g], ("psA", 4 + 2 * g)
                pD, kD = allbanks[5 + 2 * g], ("psA", 5 + 2 * g)

                def ak(n):
                    return [(n, cc) for cc in prs]

                def lock(lt, lkey, rt, rkey):
                    for idx in range(NS):
                        pc_ = lo + idx
                        ps, pk = lb[idx % 2]
                        q = idx // 2
                        MM(ps[:, q * 128:(q + 1) * 128], lt[:, pc_, :], rt if rkey is None else rt[:, pc_, :], True, True,
                           [(lkey, pc_ // NCH)] + ([] if rkey is None else [(rkey, pc_ // NCH)]), [pk])
                    outs = []
                    for bi in range(2):
                        ps, pk = lb[bi]
                        outs.append((ps[:, 0:(NS // 2) * 128].rearrange("p (c t) -> p c t", t=128), pk,
                                     (lambda d, bi=bi: d[:, rs, :].rearrange("p (q two) t -> p two q t", two=2)[:, bi, :, :])))
                    return outs

                def intra(lname, rname, mask, dst, dkey):
                    for pv, pk, sel in lock(X[lname], lname, X[rname], rname):
                        TT("dve", sel(dst), pv, mask[:, :].unsqueeze(1).broadcast_to([128, NS // 2, 128]), ALU.mult,
                           [pk, "m_su", "m_ue", "m_sl"], ak(dkey))
                intra("BTx", "ATx", m_su, Pm[0], "Pm0")
                yield
                intra("ATx", "BTx", m_sl, Ptm[0], "Ptm0")
                yield
                intra("BTx", "RTx", m_ue, MrbT, "MrbT")
                yield
                intra("KTx", "ATx", m_su, LakT, "LakT")
                yield
                intra("KTx", "RTx", m_ue, MrkT, "MrkT")
                yield
                TT("dve", TTf[:, rs, :], Pm[0][:, rs, :], identf[:, :].unsqueeze(1).broadcast_to([128, NS, 128]), ALU.add,
                   ak("Pm0") + ["identf"], ak("TTf"))
                ACT(TTb[:, rs, :], TTf[:, rs, :], AF.Copy, ak("TTf"), ak("TTb"))
                cur = 0
                for lev in range(5):
                    nxt = 1 - cur
                    pck, ptk, pnk, ptnk = PMK[cur], PTK[cur], PMK[nxt], PTK[nxt]
                    if lev < 4:
                        for pv, pk, sel in lock(Ptm[cur], ptk, Pm[cur], pck):
                            ACT(sel(Pm[nxt]), pv, AF.Copy, [pk] + ak(ptk), ak(pnk))
                        yield
                    for pv, pk, sel in lock(Pm[cur], pck, Ptm[cur], ptk):
                        ACT(sel(Ptm[nxt]), pv, AF.Copy, [pk], ak(ptnk))
                    yield
                    for pv, pk, sel in lock(Ptm[nxt], ptnk, TTb, "TTb"):
                        TT("dve", sel(TTf), sel(TTf), pv, ALU.add, ak("TTf") + [pk], ak("TTf"))
                    ACT(TTb[:, rs, :], TTf[:, rs, :], AF.Copy, ak("TTf"), ak("TTb"))
                    yield
                    cur = nxt
                for (src, dst, dkey) in (("KhTx", Khx, "Pm0"), ("BhTx", Bhx, "Ptm0")):
                    for pv, pk, sel in lock(X[src], src, identb[:, :], None):
                        ACT(sel(dst), pv, AF.Copy, [pk], ak(dkey))
                    yield
                ps, pk = lb[0]
                for idx in range(NS):
                    pc_ = lo + idx
                    MM(ps[:, idx * 64:(idx + 1) * 64], X["vTx"][:, pc_, :], stackI[:, :], True, True, [("vTx", pc_ // NCH)], [pk])
                ACT(V2[:, rs, :], ps[:, 0:NS * 64].rearrange("p (c t) -> p c t", t=64), AF.Copy, [pk], ak("V2"))
                yield
                kST, kX1, kUb = ("STb", g), ("X1b", g), ("Ub", g)
                for c4 in range(NCH):
                    for j, cc in enumerate(prs):
                        pc_ = cc * NCH + c4
                        MM(pC[:, j * 64:(j + 1) * 64], X["ATx"][:, pc_, :], STb[:, cc, :], True, False, [("ATx", cc), kST], [kC])
                        MM(pC[:, j * 64:(j + 1) * 64], LakT[:, pc_, :], V2[:, pc_, :], False, True, [("LakT", cc), ("V2", cc)], [kC])
                    ACT(X1b[:, ps2, :], pC[:, 0:128].rearrange("p (c t) -> p c t", t=64), AF.Copy, [kC], [kX1])
                    yield
                    for j, cc in enumerate(prs):
                        pc_ = cc * NCH + c4
                        MM(pC[:, 128 + j * 64:128 + (j + 1) * 64], TTb[:, pc_, :], X1b[:, cc, :], True, True, [("TTb", cc), kX1], [kC])
                    ACT(Ub[:, ps2, :], pC[:, 128:256].rearrange("p (c t) -> p c t", t=64), AF.Copy, [kC], [kUb])
                    yield
                    for j, cc in enumerate(prs):
                        pc_ = cc * NCH + c4
                        o = (pc_ - lo) * 64
                        MM(pD[:, o:o + 64], X["RTx"][:, pc_, :], STb[:, cc, :], True, False, [("RTx", cc), kST], [kD])
                        MM(pD[:, o:o + 64], MrbT[:, pc_, :], Ub[:, cc, :], False, False, [("MrbT", cc), kUb], [kD])
                        MM(pD[:, o:o + 64], MrkT[:, pc_, :], V2[:, pc_, :], False, True, [("MrkT", cc), ("V2", cc)], [kD])
                    for j, cc in enumerate(prs):
                        pc_ = cc * NCH + c4
                        MM(pC[:, 256 + j * 64:256 + (j + 1) * 64], Khx[:, pc_, :], V2[:, pc_, :], True, False, [("Pm0", cc), ("V2", cc)], [kC])
                        MM(pC[:, 256 + j * 64:256 + (j + 1) * 64], Bhx[:, pc_, :], Ub[:, cc, :], False, True, [("Ptm0", cc), kUb], [kC])
                    kSf = ("STf", g)
                    TT("dve", STf[:, ps2, :], STf[:, ps2, :], WCs[:, ps2, c4:c4 + 1].broadcast_to([128, 2, 64]), ALU.mult,
                       [kSf] + ak("WCs"), [kSf])
                    TT("dve", STf[:, ps2, :], STf[:, ps2, :], pC[:, 256:384].rearrange("p (c t) -> p c t", t=64), ALU.add, [kSf, kC], [kSf])
                    ACT(STb[:, ps2, :], STf[:, ps2, :], AF.Copy, [kSf], [kST])
                    yield
                y3 = pD[:, 0:NS * 64].rearrange("p (c t) -> p c t", t=64)
                ks1, ks2, kyc, kys = ("ystat", g), ("ystat2", g), ("ycen_", g), ("ysq_", g)
                P.add("dve", lambda e: e.tensor_reduce(out=ystat[:, lo:lo + NS], in_=y3, axis=AX.X, op=ALU.add), [kD], [ks1])
                TS("dve", ystat[:, lo:lo + NS], ystat[:, lo:lo + NS], 1.0 / 64, ALU.mult, [ks1], [ks1])
                TT("dve", ycen[:, rs, :], y3, ystat[:, lo:lo + NS].unsqueeze(2).broadcast_to([128, NS, 64]), ALU.subtract,
                   [kD, ks1], [kyc, "ycen"])
                ACT(ysq[:, rs, :], ycen[:, rs, :], AF.Square, [kyc, "ycen"], [kys, "ysq"])
                P.add("dve", lambda e: e.tensor_reduce(out=ystat[:, 16 + lo:16 + lo + NS], in_=ysq[:, rs, :], axis=AX.X, op=ALU.add),
                      [kys, "ysq"], [ks2])
                TS("dve", ystat[:, 16 + lo:16 + lo + NS], ystat[:, 16 + lo:16 + lo + NS], 1.0 / 64, ALU.mult, [ks2], [ks2],
                   s2=64e-5, op1=ALU.add)
                POW(ystat[:, 16 + lo:16 + lo + NS], ystat[:, 16 + lo:16 + lo + NS], [ks2], [ks2], NS)
                for h in range(2):
                    sl = slice(h * 64, (h + 1) * 64)
                    TT("dve", ynx[sl, rs, h * 64:(h + 1) * 64], ycen[sl, rs, :],
                       ystat[sl, 16 + lo:16 + lo + NS].unsqueeze(2).broadcast_to([64, NS, 64]), ALU.mult, [kyc, "ycen", ks2], ak("ynx_"))
                yield
                ps, pk = lb[0]
                for idx in range(NS):
                    pc_ = lo + idx
                    MM(ps[:, idx * 64:(idx + 1) * 64], ynx[:, pc_, :], stackI[:, :], True, True, [("ynx_", pc_ // NCH)], [pk])
                psv_ = ps[:, 0:NS * 64].rearrange("p (c t) -> p c t", t=TB)
                gng = cst[:, CO["gng"] + 2 * g:CO["gng"] + 2 * g + 2].unsqueeze(2).broadcast_to([128, 2, TB])
                gnb = cst[:, CO["gnb"] + 2 * g:CO["gnb"] + 2 * g + 2].unsqueeze(2).broadcast_to([128, 2, TB])
                kt3 = ("t3_", g)
                TT("dve", t3[:, ps2, :], psv_, gng, ALU.mult, [pk, "cst"], [kt3, "t3"])
                TT("dve", t3[:, ps2, :], t3[:, ps2, :], gnb, ALU.add, [kt3, "cst"], [kt3])
                TT("dve", t3[:, ps2, :], t3[:, ps2, :], bon[:, ps2, :], ALU.add, [kt3] + ak("bon"), [kt3])
                TT("dve", mixT[:, ps2, :], t3[:, ps2, :], gT[:, ps2, :], ALU.mult, [kt3, "t3"] + ak("gT"), [("mixT", c) for c in prs])

            def merge(*gens):
                gens = list(gens)
                while gens:
                    for g in list(gens):
                        try:
                            next(g)
                        except StopIteration:
                            gens.remove(g)

            if STOP <= 5:
                for _ in conv_stream():
                    pass
                return
            for _ in conv_stream():
                pass
            if _os.environ.get("NOMERGE"):
                for g in range(2):
                    for _ in stage2_stream(g):
                        pass
            else:
                merge(stage2_stream(0), stage2_stream(1))
            if STOP <= 6:
                return
            for m0 in range(0, 8, 4):
                outs = [nextA() for _ in range(4)]
                for k in range(8):
                    for q in range(4):
                        ps, pk = outs[q]
                        m = m0 + q
                        MM(ps[:, 0:TB], Wout[:, k, m * 128:(m + 1) * 128], mixT[:, k, :], k == 0, k == 7, [("Wout", k), ("mixT", k)], [pk])
                for q in range(4):
                    ps, pk = outs[q]
                    m = m0 + q
                    TT("dve", xT[:, m, t0:t0 + TB], xT[:, m, t0:t0 + TB], ps[:, 0:TB], ALU.add, [("xT", m), pk], [("xT", m)])

        def ffn_phase(l):
            for tg in range(4):
                t0 = tg * 512
                rms_stats(t0, 512, 1e-5, rs5[:, :], "rs5")
                POW(rs5[:, :], rs5[:, :], ["rs5"], ["rs5"], 512)
                for k in range(8):
                    STT(h2T[:, k, t0:t0 + 512], xT[:, k, t0:t0 + 512], c_("g2", k), rs5[:, :], ALU.mult, ALU.mult,
                        [("xT", k), "cst", "rs5"], [("h2T", tg)])
            w1v = w1_d[l].rearrange("(k p) n -> p k n", p=128)
            w2v = w2_d[l].rearrange("(k p) n -> p k n", p=128)
            def load_w(hc):
                i = hc % 2
                for k in range(8):
                    P.add("pool", lambda e, k=k, i=i, hc=hc: e.dma_start(out=W1c[i][:, k, :], in_=w1v[:, k, hc * 512:(hc + 1) * 512]),
                          (), [("W1c", i)], dma=("w1", i))
                for k in range(4):
                    P.add("pool", lambda e, k=k, i=i, hc=hc: e.dma_start(out=W2c[i][:, k, :], in_=w2v[:, hc * 4 + k, :]),
                          (), [("W2c", i)], dma=("w2", i))

            allps = psA + psB + [psC, psD, psE]
            fcnt = [0]

            def bankset():
                base = (fcnt[0] % 2) * 4
                fcnt[0] += 1
                return [(allps[base + q], ("psF", base + q)) for q in range(4)]

            def up(hc):
                i = hc % 2
                for hs in range(4):
                    bs = bankset()
                    for k in range(8):
                        for tg in range(4):
                            ps, pk = bs[tg]
                            MM(ps[:, :], W1c[i][:, k, hs * 128:(hs + 1) * 128], h2T[:, k, tg * 512:(tg + 1) * 512], k == 0, k == 7,
                               [("W1c", i), ("h2T", tg)], [pk])
                    for tg in range(4):
                        ps, pk = bs[tg]
                        fr = frl[tg % 2]
                        ACT(fr[:, :], ps[:, :], AF.Relu, [pk], [("frl", tg % 2)])
                        TT("dve", fT[i][:, hs, tg, :], fr[:, :], fr[:, :], ALU.mult, [("frl", tg % 2)], [("fT", i, hs)])

            def down(hc):
                i = hc % 2
                for m in range(8):
                    bs = bankset()
                    for hs in range(4):
                        for tg in range(4):
                            ps, pk = bs[tg]
                            MM(ps[:, :], W2c[i][:, hs, m * 128:(m + 1) * 128], fT[i][:, hs, tg, :], hs == 0, hs == 3,
                               [("W2c", i), ("fT", i, hs)], [pk])
                    for tg in range(4):
                        ps, pk = bs[tg]
                        TT("dve", xT[:, m, tg * 512:(tg + 1) * 512], xT[:, m, tg * 512:(tg + 1) * 512], ps[:, :], ALU.add,
                           [("xT", m), pk], [("xT", m)])

            load_w(0)
            load_w(1)
            up(0)
            for hc in range(8):
                if hc + 1 < 8:
                    up(hc + 1)
                down(hc)
                if hc + 2 < 8:
                    load_w(hc + 2)

        for s in range(nseq):
            load_seq(s)
            for l in range(nlayers):
                P.barrier()
                load_layer_consts(l)
                load_mixer_weights(l)
                for n in XN:
                    MEMSET("dve", X[n][:, :, :], 0.0, [(n, cc) for cc in range(4)])
                MEMSET("dve", ynx[:, :, :], 0.0, [("ynx_", cc) for cc in range(4)])
                MEMSET("dve", carry[:, :], 0.0, [("carry", c) for c in range(14)])
                MEMSET("dve", ubuf[:, :, 0:30], 0.0, [("ubuf", c) for c in range(4)])
                MEMSET("dve", STf[:, :, :], 0.0, [("STf", 0), ("STf", 1)])
                MEMSET("dve", STb[:, :, :], 0.0, [("STb", 0), ("STb", 1)])
                for b in range(NBLK if (STOP >= 8 or _os.environ.get('DEV_ALLBLK')) else 1):
                    mixer_block(l, b, b == 0)
                P.barrier()
                if STOP >= 9:
                    ffn_phase(l)
            P.barrier()
            store_seq(s)
        P.add("sp", None, r=[("y", s, t) for s in range(nseq) for t in range(SEQ // 128)])
        stats = P.finalize(st)
        print("PROG", stats)
        P.emit()
    return nc


def host_consts(inp):
    f = np.float32
    cst = np.zeros((L, 128, NCST), f)

    def put(l, name, vec):
        v = np.asarray(vec, f).reshape(-1, 128).T
        cst[l, :, CO[name]:CO[name] + v.shape[1]] = v
    for l in range(L):
        put(l, "g1", inp["norm1_g"][l]); put(l, "g2", inp["norm2_g"][l]); put(l, "mu", inp["mu_shift"][l])
        put(l, "w0", inp["w0"][l]); put(l, "a0", inp["a0"][l]); put(l, "kk", inp["k_k"][l]); put(l, "ka", inp["k_a"][l])
        put(l, "rk", inp["r_k"][l].reshape(-1)); put(l, "gng", inp["gn_g"][l]); put(l, "gnb", inp["gn_b"][l])
        put(l, "dwb", inp["dw_b"][l]); put(l, "clg", inp["cln_g"][l]); put(l, "clb", inp["cln_b"][l])
        dw = np.asarray(inp["dw_w"][l], f)
        cst[l, :, CO["dww"]:CO["dww"] + 124] = dw.T.reshape(4, 128, 31).transpose(1, 0, 2).reshape(128, 124)
    fg = np.ascontiguousarray(np.asarray(inp["final_g"], f).reshape(8, 128).T)
    waup = np.zeros((L, 2, 128, 512), f)
    waup[:, 0, 0:64, :] = np.asarray(inp["w_up"], f)
    waup[:, 1, 64:128, :] = np.asarray(inp["a_up"], f)
    return cst, fg, waup


_NC_CACHE = {}


def kernel(**inputs):
    inp = {k: np.asarray(v) for k, v in inputs.items()}
    n = 8
    cst, fg, waup = host_consts(inp)
    if "nc" not in _NC_CACHE:
        _NC_CACHE["nc"] = build_nc()
    nc = _NC_CACHE["nc"]
    x = np.ascontiguousarray(inp["x"], np.float32)
    shared = dict(cst=cst, fg=fg, waup=waup, gup=np.ascontiguousarray(inp["g_up"], np.float32),
                  w_in=np.ascontiguousarray(inp["w_in"], np.float32), w_out=np.ascontiguousarray(inp["w_out"], np.float32),
                  w_ff1=np.ascontiguousarray(inp["w_ff1"], np.float32), w_ff2=np.ascontiguousarray(inp["w_ff2"], np.float32))
    in_maps = [dict(shared, x=x[2 * c:2 * c + 2]) for c in range(n)]
    res = run_bass_kernel_spmd(nc, in_maps, core_ids=list(range(n)))
    return np.concatenate([r["y"] for r in res.results], axis=0)
```

```python
import numpy as np
from contextlib import ExitStack
import concourse.bass as bass
import concourse.mybir as mybir
from concourse.bass_utils import run_bass_kernel_spmd

F32 = mybir.dt.float32
BF16 = mybir.dt.bfloat16
AF = mybir.ActivationFunctionType
ALU = mybir.AluOpType
AX = mybir.AxisListType

D = 1024
SEQ = 2048
L = 2
INC = 2816
DFF = 4096
TB = 128
NBLK = SEQ // TB
NCH = TB // 64
C0 = float(np.exp(-0.5))

import os as _os
SAME_ENGINE_SYNC = bool(int(_os.environ.get("SES", "1")))

STOP = float(_os.environ.get('DEV_STOP', '99'))
SEM_ROTATE = 20000

CO = {}
_o = 0
for _n, _w in (("g1", 8), ("g2", 8), ("mu", 14), ("w0", 4), ("a0", 4), ("kk", 4), ("ka", 4), ("rk", 4),
               ("gng", 4), ("gnb", 4), ("dwb", 4), ("clg", 4), ("clb", 4), ("dww", 124)):
    CO[_n] = _o
    _o += _w
NCST = _o


class Prog:
    QUEUES = ("pe", "act", "dve", "pool", "sp")

    def __init__(self, nc):
        self.nc = nc
        self.ops = []
        self.last_w = {}
        self.readers = {}
        self.last_q = {}
        self.last_dma = {}

    @staticmethod
    def _bank(k):
        if isinstance(k, tuple) and k[0] in ("psA", "psB", "psF"):
            return ("bank", k[1])
        if k in ("psE", "psE2"):
            return ("bank", 7)
        if k in ("psCx", "psCu", "psCs"):
            return ("bank", 5)
        if isinstance(k, tuple) and k[0] == "psDy":
            return ("bank", 6)
        return None

    def add(self, eng, fn, r=(), w=(), dma=None, extra=()):
        i = len(self.ops)
        deps = set(extra)
        banks = [self._bank(k) for k in list(r) + list(w)]
        banks = [b for b in banks if b is not None]
        if banks:
            r = [k for k in r if self._bank(k) is None]
            w = [k for k in w if self._bank(k) is None] + sorted(set(banks), key=str)
        for k in r:
            j = self.last_w.get(k)
            if j is not None:
                deps.add(j)
        for k in w:
            j = self.last_w.get(k)
            if j is not None:
                deps.add(j)
            deps.update(self.readers.get(k, ()))
        for k in r:
            self.readers.setdefault(k, []).append(i)
        for k in w:
            self.last_w[k] = i
            self.readers[k] = []
        deps.discard(i)
        self.ops.append(dict(eng=eng, fn=fn, deps=deps, dma=dma, sig=False))
        if dma is not None:
            self.last_dma[dma] = i
        elif fn is not None:
            self.last_q[eng] = i
        return i

    def barrier(self):
        ex = set(self.last_q.values()) | set(self.last_dma.values())
        for q in self.QUEUES:
            self.add(q, None, extra=ex)

    def finalize(self, stack):
        nc = self.nc
        ops = self.ops
        for i, o in enumerate(ops):
            for j in o["deps"]:
                d = ops[j]
                if d["dma"] is not None:
                    continue
                if d["eng"] == o["eng"] and (d["eng"] == "pe" or not SAME_ENGINE_SYNC):
                    continue
                d["sig"] = True
        eng_sem, eng_cnt, dma_sem, dma_cnt = {}, {}, {}, {}
        nsem = [0]

        def new_sem(name):
            nsem[0] += 1
            return stack.enter_context(nc.semaphore(name))

        for i, o in enumerate(ops):
            if o["dma"] is not None:
                key = o["dma"]
                if key not in dma_sem:
                    dma_sem[key] = new_sem("d%d" % len(dma_sem))
                    dma_cnt[key] = 0
                dma_cnt[key] += 16
                o["sem"] = dma_sem[key]
                o["val"] = dma_cnt[key]
            elif o["sig"]:
                e = o["eng"]
                if e not in eng_sem or eng_cnt[e] >= SEM_ROTATE:
                    eng_sem[e] = new_sem("e%s%d" % (e, nsem[0]))
                    eng_cnt[e] = 0
                eng_cnt[e] += 1
                o["sem"] = eng_sem[e]
                o["val"] = eng_cnt[e]
        known = {q: {} for q in self.QUEUES}
        latest_dma = {}
        nwaits = 0
        for i, o in enumerate(ops):
            waits = {}
            for j in o["deps"]:
                d = ops[j]
                if d["dma"] is not None:
                    sem, val = d["sem"], latest_dma[d["dma"]]
                else:
                    if not d["sig"]:
                        continue
                    sem, val = d["sem"], d["val"]
                sid = id(sem)
                if sid not in waits or waits[sid][1] < val:
                    waits[sid] = (sem, val)
            kn = known[o["eng"]]
            wl = []
            for sid, (sem, val) in waits.items():
                if kn.get(sid, 0) >= val:
                    continue
                kn[sid] = val
                wl.append((sem, val))
            o["waits"] = wl
            nwaits += len(wl)
            if o["dma"] is not None:
                latest_dma[o["dma"]] = o["val"]
        self.stats = dict(n_ops=len(ops), n_sems=nsem[0], n_waits=nwaits,
                          per_eng={q: sum(1 for o in ops if o["eng"] == q) for q in self.QUEUES})
        return self.stats

    def emit_queue(self, q, eng):
        for o in self.ops:
            if o["eng"] != q:
                continue
            for sem, val in o["waits"]:
                eng.wait_ge(sem, val)
            if o["fn"] is None:
                continue
            ins = o["fn"](eng)
            if o["dma"] is not None:
                ins.then_inc(o["sem"], 16)
            elif o["sig"]:
                ins.then_inc(o["sem"], 1)

    def emit(self):
        with self.nc.Block() as block:
            @block.tensor
            def _(e):
                self.emit_queue("pe", e)

            @block.scalar
            def _(e):
                self.emit_queue("act", e)

            @block.vector
            def _(e):
                self.emit_queue("dve", e)

            @block.gpsimd
            def _(e):
                self.emit_queue("pool", e)

            @block.sync
            def _(e):
                self.emit_queue("sp", e)


def build_nc(nseq=2, nlayers=L, dbg=None):
    nc = bass.Bass("TRN2", target_bir_lowering=False)
    x_d = nc.dram_tensor("x", [nseq, SEQ, D], F32, kind="ExternalInput").ap()
    cst_d = nc.dram_tensor("cst", [L, 128, NCST], F32, kind="ExternalInput").ap()
    fg_d = nc.dram_tensor("fg", [128, 8], F32, kind="ExternalInput").ap()
    waup_d = nc.dram_tensor("waup", [L, 2, 128, 512], F32, kind="ExternalInput").ap()
    gup_d = nc.dram_tensor("gup", [L, 128, 512], F32, kind="ExternalInput").ap()
    win_d = nc.dram_tensor("w_in", [L, D, INC], F32, kind="ExternalInput").ap()
    wout_d = nc.dram_tensor("w_out", [L, D, D], F32, kind="ExternalInput").ap()
    w1_d = nc.dram_tensor("w_ff1", [L, D, DFF], F32, kind="ExternalInput").ap()
    w2_d = nc.dram_tensor("w_ff2", [L, DFF, D], F32, kind="ExternalInput").ap()
    y_d = nc.dram_tensor("y", [nseq, SEQ, D], F32, kind="ExternalOutput").ap()
    dbg_d = None
    if dbg:
        dbg_d = nc.dram_tensor("dbg", [dbg, 128, SEQ], F32, kind="ExternalOutput").ap()

    st = ExitStack()
    with st:
        def sb(name, shape, dt=F32):
            return st.enter_context(nc.sbuf_tensor(name, shape, dt))

        def psum(name, dt=F32):
            return st.enter_context(nc.psum_tensor(name, [128, 512], dt))

        P = Prog(nc)

        def TT(eng, out, a, b, op, r, w):
            P.add(eng, lambda e: e.tensor_tensor(out=out, in0=a, in1=b, op=op), r, w)

        def TS(eng, out, a, s1, op0, r, w, s2=None, op1=None):
            if op1 is None:
                P.add(eng, lambda e: e.tensor_scalar(out=out, in0=a, scalar1=s1, scalar2=None, op0=op0), r, w)
            else:
                P.add(eng, lambda e: e.tensor_scalar(out=out, in0=a, scalar1=s1, scalar2=s2, op0=op0, op1=op1), r, w)

        def STT(out, a, s, b, op0, op1, r, w):
            P.add("dve", lambda e: e.scalar_tensor_tensor(out=out, in0=a, scalar=s, in1=b, op0=op0, op1=op1), r, w)

        def ACT(out, in_, func, r, w, bias=None, scale=None):
            kw = {}
            if bias is not None:
                kw["bias"] = bias
            if scale is not None:
                kw["scale"] = scale
            P.add("act", lambda e: e.activation(out=out, in_=in_, func=func, **kw), r, w)

        def MM(ps, lhsT, rhs, start, stop, r, w):
            P.add("pe", lambda e: e.matmul(ps, lhsT, rhs, start=start, stop=stop), r, w)

        def MEMSET(eng, ap, val, w):
            P.add(eng, lambda e: e.memset(ap, val), (), w)

        def POW(out, a, r, w, n):
            ACT(out, a, AF.Sqrt, r, w)
            P.add("dve", lambda e: e.reciprocal(out=out, in_=out), w, w)

        xT = sb("xT", [128, 8, SEQ])
        arena = sb("arena", [128, 32768], BF16)
        Win = arena[:, 0:8 * INC].rearrange("p (k n) -> p k n", k=8)
        Wout = arena[:, 8 * INC:8 * INC + 8 * D].rearrange("p (k n) -> p k n", k=8)
        h2T = arena[:, 0:8 * SEQ].rearrange("p (k n) -> p k n", k=8)
        W1c = [arena[:, 8 * SEQ + i * 4096: 8 * SEQ + (i + 1) * 4096].rearrange("p (k n) -> p k n", k=8) for i in range(2)]
        W2c = [arena[:, 8 * SEQ + 8192 + i * 4096: 8 * SEQ + 8192 + (i + 1) * 4096].rearrange("p (k n) -> p k n", k=4)
               for i in range(2)]
        cst = sb("cst_s", [128, NCST])
        cst2 = sb("cst2", [128, 16])
        fg = sb("fgs", [128, 8])
        waup = sb("waup_s", [128, 2, 512], BF16)
        gup = sb("gup_s", [128, 512], BF16)
        identb = sb("identb", [128, 128], BF16)
        identf = sb("identf", [128, 128])
        onesf = sb("onesf", [128, 128])
        onesb = sb("onesb", [128, 128], BF16)
        blk1 = sb("blk1", [128, 128], BF16)
        stackI = sb("stackI", [128, 64], BF16)
        m_su = sb("m_su", [128, 128], BF16)
        m_ue = sb("m_ue", [128, 128], BF16)
        m_sl = sb("m_sl", [128, 128], BF16)
        m01 = sb("m01", [128, 4 * TB])

        NF, NB_ = 6624, 22912
        scrF = sb("scrF", [128, NF])
        scrB = sb("scrB", [128, NB_], BF16)
        aoff = {"F": 0, "B": 0}

        def areset():
            aoff["F"] = 0
            aoff["B"] = 0

        def take(shape, dt=F32):
            kind = "F" if dt == F32 else "B"
            t, cap = (scrF, NF) if kind == "F" else (scrB, NB_)
            size = int(np.prod(shape[1:]))
            o = aoff[kind]
            aoff[kind] = o + ((size + 15) // 16) * 16
            assert aoff[kind] <= cap, (kind, aoff[kind], cap)
            a = t[:, o:o + size]
            if len(shape) == 3:
                a = a.rearrange("p (a b) -> p a b", a=shape[1])
            return a

        def c_(name, j=0):
            o = CO[name] + j
            return cst[:, o:o + 1]

        areset()
        hT = take([128, 8, TB], BF16)
        mixT = take([128, 8, TB], BF16)
        gT = take([128, 4, TB], BF16)
        bon = take([128, 4, TB], BF16)
        ubuf = take([128, 4, 30 + TB], BF16)
        cvz = take([128, 4, TB])
        cvb = take([128, 4, TB])
        cvt = take([128, 4, TB], BF16)
        carry = take([128, 16])
        praw4 = take([128, 4, TB + 2])
        lwin = take([128, TB], BF16)
        gsb = take([128, TB], BF16)
        sqk = [take([128, TB], BF16) for i in range(2)]
        rstd = take([128, TB])
        mean = rstd
        diag = [take([128, 8, 128], BF16) for i in range(2)]
        csm4 = take([128, 4, TB]); Eneg4 = take([128, 4, TB]); rnq = take([128, 4, TB]); t_a = take([128, 4, TB])
        sqb4 = take([128, 4, TB], BF16)
        tmp1 = t_a[:, 0, :]
        gdm = t_a[:, 1, :]
        XN = ("ATx", "BTx", "KTx", "RTx", "KhTx", "BhTx", "vTx")
        X = {n: take([128, 4 * NCH, 128], BF16) for n in XN}
        Pm = [take([128, 4 * NCH, 128], BF16), X["BTx"]]
        Ptm = [take([128, 4 * NCH, 128], BF16), X["KTx"]]
        PMK = ["Pm0", "BTx"]
        PTK = ["Ptm0", "KTx"]
        MrbT = take([128, 4 * NCH, 128], BF16)
        LakT = take([128, 4 * NCH, 128], BF16)
        MrkT = take([128, 4 * NCH, 128], BF16)
        TTf = take([128, 4 * NCH, 128])
        TTb = take([128, 4 * NCH, 128], BF16)
        Khx = Pm[0]
        Bhx = Ptm[0]
        V2 = take([128, 4 * NCH, 64], BF16)
        X1b = take([128, 4, 64], BF16)
        Ub = take([128, 4, 64], BF16)
        STf = take([128, 4, 64])
        STb = take([128, 4, 64], BF16)
        WCs = take([128, 4, NCH])
        ysq = take([128, 4 * NCH, 64])
        ycen = take([128, 4 * NCH, 64])
        ystat = take([128, 32])
        ynx = take([128, 4 * NCH, 128], BF16)
        t3 = take([128, 4, TB])
        r4 = TTf[:, 0:4, :]
        k4 = TTf[:, 4:8, :]
        v4 = ysq.rearrange("p (a b) t -> p a (b t)", a=4)
        lw4 = ycen.rearrange("p (a b) t -> p a (b t)", a=4)
        aa4 = t3
        cs4 = cvz
        K_R = [("TTf", 0), ("TTf", 1)]
        K_K = [("TTf", 2), ("TTf", 3)]
        K_V = ["ysq"]
        K_LW = ["ycen"]
        K_AA = ["t3"]
        K_CS = ["cvz"]
        print("mixer scratch", dict(aoff))
        areset()
        fsq = take([128, 512], BF16)
        frl = [take([128, 512]) for i in range(2)]
        fT = [take([128, 16, 512], BF16).rearrange("p (a b) n -> p a b n", a=4) for i in range(2)]
        rs5 = take([128, 512])
        areset()
        xin = [take([128, D]) for i in range(2)]
        areset()
        yout = [take([128, D]) for i in range(2)]
        hn = take([128, 8, 128])
        rstd_s = take([128, 128])
        sqk_s = [take([128, 128], BF16) for i in range(2)]

        psA = [psum("psA%d" % i) for i in range(3)]
        psB = [psum("psB%d" % i) for i in range(2)]
        psC = psum("psC")
        psD = psum("psD")
        psE = psum("psE")
        cnt = {"A": 0, "B": 0, "praw": 0, "sqk": 0, "diag": 0, "xin": 0, "yout": 0}

        psP = psA + psB

        allbanks = psA + psB + [psC, psD, psE]

        def nextA():
            i = cnt["A"] % 4
            cnt["A"] += 1
            return allbanks[i], ("psA", i)

        def nextB():
            i = 4 + cnt["B"] % 3
            cnt["B"] += 1
            return allbanks[i], ("psA", i)

        MEMSET("dve", onesf[:, :], 1.0, ["onesf"])
        MEMSET("dve", onesb[:, :], 1.0, ["onesb"])
        MEMSET("dve", m01[:, :], 1.0, ["m01"])
        MEMSET("dve", m01[:, :].rearrange("p (c t) -> p c t", t=64)[:, :, 0:1], 0.0, ["m01"])
        MEMSET("dve", blk1[:, :], 0.0, ["blk1"])
        MEMSET("dve", blk1[0:64, 0:64], 1.0, ["blk1"])
        MEMSET("dve", blk1[64:128, 64:128], 1.0, ["blk1"])
        P.add("pool", lambda e: e.affine_select(out=m_su[:, :], in_=onesf[:, :], pattern=[[1, 128]], compare_op=ALU.is_gt, fill=0.0,
                                                base=0, channel_multiplier=-1), ["onesf"], ["m_su"])
        P.add("pool", lambda e: e.affine_select(out=m_ue[:, :], in_=onesf[:, :], pattern=[[1, 128]], compare_op=ALU.is_ge, fill=0.0,
                                                base=0, channel_multiplier=-1), ["onesf"], ["m_ue"])
        P.add("pool", lambda e: e.affine_select(out=m_sl[:, :], in_=onesf[:, :], pattern=[[-1, 128]], compare_op=ALU.is_gt, fill=0.0,
                                                base=0, channel_multiplier=1), ["onesf"], ["m_sl"])
        TT("dve", identf[:, :], m_ue[:, :], m_su[:, :], ALU.subtract, ["m_ue", "m_su"], ["identf"])
        P.add("dve", lambda e: e.tensor_copy(out=identb[:, :], in_=identf[:, :]), ["identf"], ["identb"])
        P.add("dve", lambda e: e.tensor_copy(out=stackI[0:64, :], in_=identf[0:64, 0:64]), ["identf"], ["stackI"])
        P.add("dve", lambda e: e.tensor_copy(out=stackI[64:128, :], in_=identf[64:128, 64:128]), ["identf"], ["stackI"])
        for n in XN:
            MEMSET("dve", X[n][:, :, :], 0.0, [(n, cc) for cc in range(4)])
        MEMSET("dve", ynx[:, :, :], 0.0, [("ynx_", cc) for cc in range(4)])
        P.add("sp", lambda e: e.dma_start(out=fg[:, :], in_=fg_d[:, :]), (), ["fg"], dma="c0")

        def dbg_dump(idx, ap, keys, n):
            if dbg_d is None:
                return
            P.add("sp", lambda e: e.dma_start(out=dbg_d[idx, 0:ap.shape[0], 0:n], in_=ap), keys, [("dbg", idx)], dma="dbg")

        def rms_stats(t0, n, eps_rs, rs_out, rs_key, sq_bufs=None):
            ps = psE
            for k in range(8):
                i = cnt["sqk"] % 2
                cnt["sqk"] += 1
                if sq_bufs is not None:
                    s = sq_bufs[i][:, 0:n]
                    skey = ("sqs", i)
                elif n > TB:
                    s = fsq[:, 0:n]
                    skey = "fsq"
                else:
                    s = sqk[i][:, 0:n]
                    skey = ("sqk", i)
                ACT(s, xT[:, k, t0:t0 + n], AF.Square, [("xT", k)], [skey])
                MM(ps[:, 0:n], onesb[:, :], s, k == 0, k == 7, ["onesb", skey], ["psE"])
            TS("dve", rs_out, ps[:, 0:n], 1.0 / D, ALU.mult, ["psE"], [rs_key], s2=1e-5, op1=ALU.add)

        def load_seq(s):
            for tt_ in range(SEQ // 128):
                i = cnt["xin"] % 2
                cnt["xin"] += 1
                P.add("sp", lambda e, i=i, tt_=tt_: e.dma_start(out=xin[i][:, :], in_=x_d[s, tt_ * 128:(tt_ + 1) * 128, :]),
                      (), [("xin", i)], dma=("xin", i))
                for half in range(2):
                    ps, pk = nextA()
                    for k4 in range(4):
                        k = half * 4 + k4
                        P.add("pe", lambda e, ps=ps, k=k, k4=k4, i=i: e.transpose(ps[:, k4 * 128:(k4 + 1) * 128], xin[i][:, k * 128:(k + 1) * 128], identf[:, :]),
                              [("xin", i), "identf"], [pk])
                    P.add("act", lambda e, ps=ps, half=half, tt_=tt_: e.activation(
                        out=xT[:, half * 4:half * 4 + 4, tt_ * 128:(tt_ + 1) * 128],
                        in_=ps[:, :].rearrange("p (k t) -> p k t", k=4), func=AF.Copy),
                        [pk], [("xT", half * 4 + j) for j in range(4)])

        def store_seq(s):
            for tt_ in range(SEQ // 128):
                t0 = tt_ * 128
                rms_stats(t0, 128, 1e-5, rstd_s[:, :], "rstd_s", sq_bufs=sqk_s)
                POW(rstd_s[:, :], rstd_s[:, :], ["rstd_s"], ["rstd_s"], 128)
                for k in range(8):
                    STT(hn[:, k, :], xT[:, k, t0:t0 + 128], fg[:, k:k + 1], rstd_s[:, :], ALU.mult, ALU.mult,
                        [("xT", k), "fg", "rstd_s"], [("hn", k)])
                i = cnt["yout"] % 2
                cnt["yout"] += 1
                for half in range(2):
                    ps, pk = nextA()
                    for k4 in range(4):
                        k = half * 4 + k4
                        P.add("pe", lambda e, ps=ps, k=k, k4=k4: e.transpose(ps[:, k4 * 128:(k4 + 1) * 128], hn[:, k, :], identf[:, :]),
                              [("hn", k), "identf"], [pk])
                    P.add("act", lambda e, ps=ps, half=half, i=i: e.activation(out=yout[i][:, half * 512:(half + 1) * 512], in_=ps[:, :], func=AF.Copy),
                          [pk], [("yout", i)])
                P.add("sp", lambda e, i=i, tt_=tt_: e.dma_start(out=y_d[s, tt_ * 128:(tt_ + 1) * 128, :], in_=yout[i][:, :]),
                      [("yout", i)], [("y", s, tt_)], dma=("yout", i))

        def load_layer_consts(l):
            P.add("sp", lambda e: e.dma_start(out=cst[:, :], in_=cst_d[l, :, :]), (), ["cst"], dma="c0")
            for j in range(2):
                P.add("pool", lambda e, j=j: e.dma_start(out=waup[:, j, :], in_=waup_d[l, j, :, :]), (), ["waup"], dma="c1")
            P.add("pool", lambda e: e.dma_start(out=gup[:, :], in_=gup_d[l, :, :]), (), ["gup"], dma="c1")
            TS("dve", cst2[:, 0:4], cst[:, CO["w0"]:CO["w0"] + 4], 0.5, ALU.mult, ["cst"], ["cst2"])
            TS("dve", cst2[:, 4:8], cst[:, CO["a0"]:CO["a0"] + 4], 0.5, ALU.mult, ["cst"], ["cst2"])

        def load_mixer_weights(l):
            wv = win_d[l].rearrange("(k p) n -> p k n", p=128)
            for k in range(8):
                for c0 in range(0, INC, 704):
                    P.add("pool", lambda e, k=k, c0=c0: e.dma_start(out=Win[:, k, c0:c0 + 704], in_=wv[:, k, c0:c0 + 704]),
                          (), [("Win", k)], dma="win")
            wo = wout_d[l].rearrange("(k p) n -> p k n", p=128)
            for k in range(8):
                P.add("pool", lambda e, k=k: e.dma_start(out=Wout[:, k, :], in_=wo[:, k, :]), (), [("Wout", k)], dma="wout")

        def mixer_block(l, b, first):
            t0 = b * TB
            NP = 4 * NCH
            rms_stats(t0, TB, 1e-5, rstd[:, :], "rstd")
            POW(rstd[:, :], rstd[:, :], ["rstd"], ["rstd"], TB)
            for k in range(8):
                STT(hT[:, k, :], xT[:, k, t0:t0 + TB], c_("g1", k), rstd[:, :], ALU.mult, ALU.mult,
                    [("xT", k), "cst", "rstd"], [("hT", k)])
            if STOP <= 1:
                return

            def inproj_wave(chunks):
                outs = [nextA() for _ in chunks]
                for k in range(8):
                    for (ps, pk), c in zip(outs, chunks):
                        MM(ps[:, 0:TB], Win[:, k, c * 128:(c + 1) * 128], hT[:, k, :], k == 0, k == 7, [("Win", k), ("hT", k)], [pk])
                return outs

            def shift_mix1(c, ps, pk, dst, dkey):
                pr = praw4[:, 0, :]
                prk = "praw4"
                ACT(pr[:, 1:TB + 1], ps[:, 0:TB], AF.Copy, [pk], [prk])
                ACT(pr[:, 0:1], carry[:, c:c + 1], AF.Copy, [("carry", c)], [prk])
                ACT(carry[:, c:c + 1], pr[:, TB:TB + 1], AF.Copy, [prk], [("carry", c)])
                TT("dve", praw4[:, 1, 0:TB], pr[:, 0:TB], pr[:, 1:TB + 1], ALU.subtract, [prk], [prk])
                STT(dst, praw4[:, 1, 0:TB], c_("mu", c), pr[:, 1:TB + 1], ALU.mult, ALU.add, ["cst", prk], [dkey])

            def bc4(name):
                return cst[:, CO[name]:CO[name] + 4].unsqueeze(2).broadcast_to([128, 4, TB])

            def shift_mix4(base, outs, dst, dkeys):
                prk = "praw4"
                for cc in range(4):
                    ps, pk = outs[cc]
                    ACT(praw4[:, cc, 1:TB + 1], ps[:, 0:TB], AF.Copy, [pk], [prk])
                ck = [("carry", base + cc) for cc in range(4)]
                ACT(praw4[:, :, 0:1], carry[:, base:base + 4].unsqueeze(2), AF.Copy, ck, [prk])
                ACT(carry[:, base:base + 4].unsqueeze(2), praw4[:, :, TB:TB + 1], AF.Copy, [prk], ck)
                TT("dve", t_a[:, :, :], praw4[:, :, 0:TB], praw4[:, :, 1:TB + 1], ALU.subtract, [prk], ["t_a"])
                TT("dve", t_a[:, :, :], t_a[:, :, :], cst[:, CO["mu"] + base:CO["mu"] + base + 4].unsqueeze(2).broadcast_to([128, 4, TB]),
                   ALU.mult, ["t_a", "cst"], ["t_a"])
                TT("dve", dst, t_a[:, :, :], praw4[:, :, 1:TB + 1], ALU.add, ["t_a", prk], dkeys)

            (ps, pk), (psgd, pkgd) = inproj_wave([12, 13])
            shift_mix1(12, ps, pk, tmp1, "t_a")
            ACT(lwin[0:64, :], tmp1[0:64, :], AF.Tanh, ["t_a"], ["lwin"])
            ACT(lwin[64:128, :], tmp1[64:128, :], AF.Copy, ["t_a"], ["lwin"])
            shift_mix1(13, psgd, pkgd, gdm, "t_a")
            ACT(gdm, gdm, AF.Tanh, ["t_a"], ["t_a"], scale=0.5)
            TS("dve", gsb[:, :], gdm, 0.5, ALU.mult, ["t_a"], ["gsb"], s2=0.5, op1=ALU.add)
            if STOP <= 2:
                return

            def v3(t):
                return t[:, :].rearrange("p (c t) -> p c t", t=64)

            def conv_stream():
                for c0 in (0, 2):
                    w = inproj_wave([14 + c0, 15 + c0, 18 + c0, 19 + c0])
                    vals, gates = w[0:2], w[2:4]
                    for j in range(2):
                        ACT(t_a[:, c0 + j, :], gates[j][0][:, 0:TB], AF.Tanh, [gates[j][1]], ["t_a"], scale=0.5)
                    TS("dve", t_a[:, c0:c0 + 2, :], t_a[:, c0:c0 + 2, :], 0.5, ALU.mult, ["t_a"], ["t_a"], s2=0.5, op1=ALU.add)
                    for j in range(2):
                        c = c0 + j
                        TT("dve", ubuf[:, c, 30:30 + TB], vals[j][0][:, 0:TB], t_a[:, c, :], ALU.mult, [vals[j][1], "t_a"], [("ubuf", c)])
                    yield
                outs = [nextA() for _ in range(4)]
                dww3 = cst[:, CO["dww"]:CO["dww"] + 124].rearrange("p (c k) -> p c k", k=31)
                for g0 in range(0, 31, 2):
                    G = min(2, 31 - g0)
                    i = cnt["diag"] % 2
                    cnt["diag"] += 1
                    dg = diag[i][:, :, :].rearrange("p (c g) n -> p c g n", g=2)
                    TT("dve", dg[:, :, 0:G, :], identf[:, :].unsqueeze(1).unsqueeze(1).broadcast_to([128, 4, G, 128]),
                       dww3[:, :, g0:g0 + G].unsqueeze(3).broadcast_to([128, 4, G, 128]), ALU.mult, ["identf", "cst"], [("diag", i)])
                    for j in range(G):
                        kt = g0 + j
                        for c in range(4):
                            ps, pk = outs[c]
                            MM(ps[:, 0:TB], dg[:, c, j, :], ubuf[:, c, kt:kt + TB], kt == 0, kt == 30, [("diag", i), ("ubuf", c)], [pk])
                    yield
                for c in range(4):
                    ps, pk = outs[c]
                    ACT(cvb[:, c, :], ps[:, 0:TB], AF.Identity, [pk, "cst"], [("cvb", c)], bias=c_("dwb", c))
                    ACT(cvt[:, c, :], cvb[:, c, :], AF.Copy, [("cvb", c)], [("cvt", c)])
                    ACT(ubuf[:, c, 0:30], ubuf[:, c, TB:TB + 30], AF.Copy, [("ubuf", c)], [("ubuf", c)])
                yield
                cvk = [("cvb", c) for c in range(4)]
                for c in range(4):
                    MM(psE[:, 0:TB], onesb[:, :], cvt[:, c, :], c == 0, c == 3, ["onesb", ("cvt", c)], ["psE"])
                TS("dve", mean[:, :], psE[:, 0:TB], 1.0 / 512, ALU.mult, ["psE"], ["rstd"])
                TT("dve", cvb[:, :, :], cvb[:, :, :], mean[:, :].unsqueeze(1).broadcast_to([128, 4, TB]), ALU.subtract, cvk + ["rstd"], cvk)
                yield
                ACT(cvt[:, :, :], cvb[:, :, :], AF.Square, cvk, [("cvt", c) for c in range(4)])
                for c in range(4):
                    MM(psE[:, 0:TB], onesb[:, :], cvt[:, c, :], c == 0, c == 3, ["onesb", ("cvt", c)], ["psE"])
                TS("dve", mean[:, :], psE[:, 0:TB], 1.0 / 512, ALU.mult, ["psE"], ["rstd"], s2=1e-5, op1=ALU.add)
                POW(mean[:, :], mean[:, :], ["rstd"], ["rstd"], TB)
                yield
                TT("dve", cvb[:, :, :], cvb[:, :, :], mean[:, :].unsqueeze(1).broadcast_to([128, 4, TB]), ALU.mult, cvk + ["rstd"], cvk)
                for c in range(4):
                    TS("dve", cvb[:, c, :], cvb[:, c, :], c_("clg", c), ALU.mult, [("cvb", c), "cst"], [("cvb", c)], s2=c_("clb", c), op1=ALU.add)
                ACT(cvz[:, :, :], cvb[:, :, :], AF.Tanh, cvk, ["cvz"], scale=0.5)
                TS("dve", cvz[:, :, :], cvz[:, :, :], 0.5, ALU.mult, ["cvz"], ["cvz"], s2=0.5, op1=ALU.add)
                TT("dve", mixT[:, 4:8, :], cvb[:, :, :], cvz[:, :, :], ALU.mult, cvk + ["cvz"], [("mixT", 4 + c) for c in range(4)])

            def stage1_all():
                def flat(t):
                    return t.rearrange("p a t -> p (a t)")

                def v8(t):
                    return t.rearrange("p a (c t) -> p (a c) t", t=64)
                shift_mix4(0, inproj_wave([0, 1, 2, 3]), r4, K_R)
                shift_mix4(4, inproj_wave([4, 5, 6, 7]), k4, K_K)
                shift_mix4(8, inproj_wave([8, 9, 10, 11]), v4, K_V)
                ps, pk = nextB()
                for cc in range(4):
                    MM(ps[:, cc * TB:(cc + 1) * TB], waup[:, 0, cc * 128:(cc + 1) * 128], lwin[:, :], True, True, ["waup", "lwin"], [pk])
                ps3 = ps[:, :].rearrange("p (a t) -> p a t", a=4)
                TT("dve", lw4, ps3, bc4("w0"), ALU.add, [pk, "cst"], K_LW)
                if dbg_d is not None and b == 0 and l == 0:
                    dbg_dump(11, lw4.rearrange("p a t -> p (a t)"), K_LW, 512)
                ACT(lw4, lw4, AF.Tanh, K_LW, K_LW, scale=0.5)
                TS("dve", lw4, lw4, -0.5 * C0, ALU.mult, K_LW, K_LW, s2=-0.5 * C0, op1=ALU.add)
                ps, pk = nextB()
                for cc in range(4):
                    MM(ps[:, cc * TB:(cc + 1) * TB], waup[:, 1, cc * 128:(cc + 1) * 128], lwin[:, :], True, True, ["waup", "lwin"], [pk])
                ps3 = ps[:, :].rearrange("p (a t) -> p a t", a=4)
                TT("dve", aa4, ps3, bc4("a0"), ALU.add, [pk, "cst"], K_AA)
                ACT(aa4, aa4, AF.Tanh, K_AA, K_AA, scale=0.5)
                TS("dve", aa4, aa4, 0.5, ALU.mult, K_AA, K_AA, s2=0.5, op1=ALU.add)
                ps, pk = nextB()
                for cc in range(4):
                    MM(ps[:, cc * TB:(cc + 1) * TB], gup[:, cc * 128:(cc + 1) * 128], gsb[:, :], True, True, ["gup", "gsb"], [pk])
                ACT(gT[:, :, :], ps[:, :].rearrange("p (a t) -> p a t", a=4), AF.Copy, [pk], allk("gT"))
                P.add("dve", lambda e: e.tensor_tensor_scan(out=flat(cs4), data0=m01[:, :], data1=flat(lw4), initial=0.0,
                                                            op0=ALU.mult, op1=ALU.add), ["m01"] + K_LW, K_CS)
                TT("dve", csm4, cs4, lw4, ALU.subtract, K_CS + K_LW, ["csm4"])
                ACT(Eneg4, cs4, AF.Exp, K_CS, ["Eneg4"], scale=-1.0)
                ACT(cs4, cs4, AF.Exp, K_CS, K_CS)
                ACT(csm4, csm4, AF.Exp, ["csm4"], ["csm4"])
                Epos4, Eprev4 = cs4, csm4
                ACT(WCs.rearrange("p a c -> p (a c)").unsqueeze(2), v8(Epos4)[:, :, 63:64], AF.Copy, K_CS, allk("WCs"))
                TT("dve", rnq, k4, bc4("kk"), ALU.mult, K_K + ["cst"], ["rnq"])
                ACT(sqb4, rnq, AF.Square, ["rnq"], ["sqb4"])
                ps, pk = nextB()
                for cc in range(4):
                    MM(ps[:, cc * TB:(cc + 1) * TB], blk1[:, :], sqb4[:, cc, :], True, True, ["blk1", "sqb4"], [pk])
                TS("dve", t_a, ps[:, :].rearrange("p (a t) -> p a t", a=4), 1e-24, ALU.max, [pk], ["t_a"])
                POW(t_a, t_a, ["t_a"], ["t_a"], 4 * TB)
                TT("dve", rnq, rnq, t_a, ALU.mult, ["rnq", "t_a"], ["rnq"])
                TT("dve", t_a, aa4, bc4("ka"), ALU.mult, K_AA + ["cst"], ["t_a"])
                TT("dve", t_a, t_a, bc4("ka"), ALU.subtract, ["t_a", "cst"], ["t_a"])
                STT(k4, t_a, 1.0, k4, ALU.add, ALU.mult, ["t_a"] + K_K, K_K)
                TT("dve", t_a, r4, bc4("rk"), ALU.mult, K_R + ["cst"], ["t_a"])
                TT("dve", sqb4, t_a, k4, ALU.mult, ["t_a"] + K_K, ["sqb4"])
                ps, pk = nextB()
                for cc in range(4):
                    MM(ps[:, cc * TB:(cc + 1) * TB], blk1[:, :], sqb4[:, cc, :], True, True, ["blk1", "sqb4"], [pk])
                TT("dve", bon[:, :, :], ps[:, :].rearrange("p (a t) -> p a t", a=4), v4, ALU.mult, [pk] + K_V, allk("bon"))
                TT("dve", t_a, rnq, aa4, ALU.mult, ["rnq"] + K_AA, ["t_a"])
                TT("dve", t_a, t_a, Eneg4, ALU.mult, ["t_a", "Eneg4"], ["t_a"])
                TT("dve", k4, k4, Eneg4, ALU.mult, K_K + ["Eneg4"], K_K)
                for h in range(2):
                    sl = slice(h * 64, (h + 1) * 64)
                    cl = slice(h * 64, (h + 1) * 64)
                    wc = v8(Epos4)[sl, :, 63:64].broadcast_to([64, NP, 64])
                    STT(X["ATx"][sl, :, cl], v8(rnq)[sl], -1.0, v8(Eprev4)[sl], ALU.mult, ALU.mult, ["rnq", "csm4"], allk("ATx"))
                    ACT(X["BTx"][sl, :, cl], v8(t_a)[sl], AF.Copy, ["t_a"], allk("BTx"))
                    TT("dve", X["BhTx"][sl, :, cl], v8(t_a)[sl], wc, ALU.mult, ["t_a"] + K_CS, allk("BhTx"))
                    ACT(X["KTx"][sl, :, cl], v8(k4)[sl], AF.Copy, K_K, allk("KTx"))
                    TT("dve", X["KhTx"][sl, :, cl], v8(k4)[sl], wc, ALU.mult, K_K + K_CS, allk("KhTx"))
                    TT("dve", X["RTx"][sl, :, cl], v8(r4)[sl], v8(Epos4)[sl], ALU.mult, K_R + K_CS, allk("RTx"))
                    ACT(X["vTx"][sl, :, cl], v8(v4)[sl], AF.Copy, K_V, allk("vTx"))

            def allk(n):
                return [(n, cc) for cc in range(4)]

            stage1_all()
            if dbg_d is not None and b == 0 and l == 0:
                fl = lambda t: t.rearrange("p a t -> p (a t)")
                for i, (t, kk_) in enumerate([(r4, K_R), (k4, K_K), (v4, K_V), (lw4, K_LW), (aa4, K_AA), (cs4, K_CS),
                                               (csm4, ["csm4"]), (Eneg4, ["Eneg4"]), (rnq, ["rnq"]), (t_a, ["t_a"])]):
                    dbg_dump(i, fl(t), kk_, 512)
            if STOP <= 4:
                return

            def stage2_stream(g):
                prs = (2 * g, 2 * g + 1)
                lo = 2 * g * NCH
                NS = 2 * NCH
                rs = slice(lo, lo + NS)
                ps2 = slice(2 * g, 2 * g + 2)
                lb = [(allbanks[2 * g], ("psA", 2 * g)), (allbanks[2 * g + 1], ("psA", 2 * g + 1))]
                pC, kC = allbanks[4 + 2 * g], ("psA", 4 + 2 * g)
                pD, kD = allbanks[5 + 2 * g], ("psA", 5 + 2 * g)

                def ak(n):
                    return [(n, cc) for cc in prs]

                def lock(lt, lkey, rt, rkey):
                    for idx in range(NS):
                        pc_ = lo + idx
                        ps, pk = lb[idx % 2]
                        q = idx // 2
                        MM(ps[:, q * 128:(q + 1) * 128], lt[:, pc_, :], rt if rkey is None else rt[:, pc_, :], True, True,
                           [(lkey, pc_ // NCH)] + ([] if rkey is None else [(rkey, pc_ // NCH)]), [pk])
                    outs = []
                    for bi in range(2):
                        ps, pk = lb[bi]
                        outs.append((ps[:, 0:(NS // 2) * 128].rearrange("p (c t) -> p c t", t=128), pk,
                                     (lambda d, bi=bi: d[:, rs, :].rearrange("p (q two) t -> p two q t", two=2)[:, bi, :, :])))
                    return outs

                def intra(lname, rname, mask, dst, dkey):
                    for pv, pk, sel in lock(X[lname], lname, X[rname], rname):
                        TT("dve", sel(dst), pv, mask[:, :].unsqueeze(1).broadcast_to([128, NS // 2, 128]), ALU.mult,
                           [pk, "m_su", "m_ue", "m_sl"], ak(dkey))
                intra("BTx", "ATx", m_su, Pm[0], "Pm0")
                yield
                intra("ATx", "BTx", m_sl, Ptm[0], "Ptm0")
                yield
                intra("BTx", "RTx", m_ue, MrbT, "MrbT")
                yield
                intra("KTx", "ATx", m_su, LakT, "LakT")
                yield
                intra("KTx", "RTx", m_ue, MrkT, "MrkT")
                yield
                TT("dve", TTf[:, rs, :], Pm[0][:, rs, :], identf[:, :].unsqueeze(1).broadcast_to([128, NS, 128]), ALU.add,
                   ak("Pm0") + ["identf"], ak("TTf"))
                ACT(TTb[:, rs, :], TTf[:, rs, :], AF.Copy, ak("TTf"), ak("TTb"))
                cur = 0
                for lev in range(5):
                    nxt = 1 - cur
                    pck, ptk, pnk, ptnk = PMK[cur], PTK[cur], PMK[nxt], PTK[nxt]
                    if lev < 4:
                        for pv, pk, sel in lock(Ptm[cur], ptk, Pm[cur], pck):
                            ACT(sel(Pm[nxt]), pv, AF.Copy, [pk] + ak(ptk), ak(pnk))
                        yield
                    for pv, pk, sel in lock(Pm[cur], pck, Ptm[cur], ptk):
                        ACT(sel(Ptm[nxt]), pv, AF.Copy, [pk], ak(ptnk))
                    yield
                    for pv, pk, sel in lock(Ptm[nxt], ptnk, TTb, "TTb"):
                        TT("dve", sel(TTf), sel(TTf), pv, ALU.add, ak("TTf") + [pk], ak("TTf"))
                    ACT(TTb[:, rs, :], TTf[:, rs, :], AF.Copy, ak("TTf"), ak("TTb"))
                    yield
                    cur = nxt
                for (src, dst, dkey) in (("KhTx", Khx, "Pm0"), ("BhTx", Bhx, "Ptm0")):
                    for pv, pk, sel in lock(X[src], src, identb[:, :], None):
                        ACT(sel(dst), pv, AF.Copy, [pk], ak(dkey))
                    yield
                ps, pk = lb[0]
                for idx in range(NS):
                    pc_ = lo + idx
                    MM(ps[:, idx * 64:(idx + 1) * 64], X["vTx"][:, pc_, :], stackI[:, :], True, True, [("vTx", pc_ // NCH)], [pk])
                ACT(V2[:, rs, :], ps[:, 0:NS * 64].rearrange("p (c t) -> p c t", t=64), AF.Copy, [pk], ak("V2"))
                yield
                kST, kX1, kUb = ("STb", g), ("X1b", g), ("Ub", g)
                for c4 in range(NCH):
                    for j, cc in enumerate(prs):
                        pc_ = cc * NCH + c4
                        MM(pC[:, j * 64:(j + 1) * 64], X["ATx"][:, pc_, :], STb[:, cc, :], True, False, [("ATx", cc), kST], [kC])
                        MM(pC[:, j * 64:(j + 1) * 64], LakT[:, pc_, :], V2[:, pc_, :], False, True, [("LakT", cc), ("V2", cc)], [kC])
                    ACT(X1b[:, ps2, :], pC[:, 0:128].rearrange("p (c t) -> p c t", t=64), AF.Copy, [kC], [kX1])
                    yield
                    for j, cc in enumerate(prs):
                        pc_ = cc * NCH + c4
                        MM(pC[:, 128 + j * 64:128 + (j + 1) * 64], TTb[:, pc_, :], X1b[:, cc, :], True, True, [("TTb", cc), kX1], [kC])
                    ACT(Ub[:, ps2, :], pC[:, 128:256].rearrange("p (c t) -> p c t", t=64), AF.Copy, [kC], [kUb])
                    yield
                    for j, cc in enumerate(prs):
                        pc_ = cc * NCH + c4
                        o = (pc_ - lo) * 64
                        MM(pD[:, o:o + 64], X["RTx"][:, pc_, :], STb[:, cc, :], True, False, [("RTx", cc), kST], [kD])
                        MM(pD[:, o:o + 64], MrbT[:, pc_, :], Ub[:, cc, :], False, False, [("MrbT", cc), kUb], [kD])
                        MM(pD[:, o:o + 64], MrkT[:, pc_, :], V2[:, pc_, :], False, True, [("MrkT", cc), ("V2", cc)], [kD])
                    for j, cc in enumerate(prs):
                        pc_ = cc * NCH + c4
                        MM(pC[:, 256 + j * 64:256 + (j + 1) * 64], Khx[:, pc_, :], V2[:, pc_, :], True, False, [("Pm0", cc), ("V2", cc)], [kC])
                        MM(pC[:, 256 + j * 64:256 + (j + 1) * 64], Bhx[:, pc_, :], Ub[:, cc, :], False, True, [("Ptm0", cc), kUb], [kC])
                    kSf = ("STf", g)
                    TT("dve", STf[:, ps2, :], STf[:, ps2, :], WCs[:, ps2, c4:c4 + 1].broadcast_to([128, 2, 64]), ALU.mult,
                       [kSf] + ak("WCs"), [kSf])
                    TT("dve", STf[:, ps2, :], STf[:, ps2, :], pC[:, 256:384].rearrange("p (c t) -> p c t", t=64), ALU.add, [kSf, kC], [kSf])
                    ACT(STb[:, ps2, :], STf[:, ps2, :], AF.Copy, [kSf], [kST])
                    yield
                y3 = pD[:, 0:NS * 64].rearrange("p (c t) -> p c t", t=64)
                ks1, ks2, kyc, kys = ("ystat", g), ("ystat2", g), ("ycen_", g), ("ysq_", g)
                P.add("dve", lambda e: e.tensor_reduce(out=ystat[:, lo:lo + NS], in_=y3, axis=AX.X, op=ALU.add), [kD], [ks1])
                TS("dve", ystat[:, lo:lo + NS], ystat[:, lo:lo + NS], 1.0 / 64, ALU.mult, [ks1], [ks1])
                TT("dve", ycen[:, rs, :], y3, ystat[:, lo:lo + NS].unsqueeze(2).broadcast_to([128, NS, 64]), ALU.subtract,
                   [kD, ks1], [kyc, "ycen"])
                ACT(ysq[:, rs, :], ycen[:, rs, :], AF.Square, [kyc, "ycen"], [kys, "ysq"])
                P.add("dve", lambda e: e.tensor_reduce(out=ystat[:, 16 + lo:16 + lo + NS], in_=ysq[:, rs, :], axis=AX.X, op=ALU.add),
                      [kys, "ysq"], [ks2])
                TS("dve", ystat[:, 16 + lo:16 + lo + NS], ystat[:, 16 + lo:16 + lo + NS], 1.0 / 64, ALU.mult, [ks2], [ks2],
                   s2=64e-5, op1=ALU.add)
                POW(ystat[:, 16 + lo:16 + lo + NS], ystat[:, 16 + lo:16 + lo + NS], [ks2], [ks2], NS)
                for h in range(2):
                    sl = slice(h * 64, (h + 1) * 64)
                    TT("dve", ynx[sl, rs, h * 64:(h + 1) * 64], ycen[sl, rs, :],
                       ystat[sl, 16 + lo:16 + lo + NS].unsqueeze(2).broadcast_to([64, NS, 64]), ALU.mult, [kyc, "ycen", ks2], ak("ynx_"))
                yield
                ps, pk = lb[0]
                for idx in range(NS):
                    pc_ = lo + idx
                    MM(ps[:, idx * 64:(idx + 1) * 64], ynx[:, pc_, :], stackI[:, :], True, True, [("ynx_", pc_ // NCH)], [pk])
                psv_ = ps[:, 0:NS * 64].rearrange("p (c t) -> p c t", t=TB)
                gng = cst[:, CO["gng"] + 2 * g:CO["gng"] + 2 * g + 2].unsqueeze(2).broadcast_to([128, 2, TB])
                gnb = cst[:, CO["gnb"] + 2 * g:CO["gnb"] + 2 * g + 2].unsqueeze(2).broadcast_to([128, 2, TB])
                kt3 = ("t3_", g)
                TT("dve", t3[:, ps2, :], psv_, gng, ALU.mult, [pk, "cst"], [kt3, "t3"])
                TT("dve", t3[:, ps2, :], t3[:, ps2, :], gnb, ALU.add, [kt3, "cst"], [kt3])
                TT("dve", t3[:, ps2, :], t3[:, ps2, :], bon[:, ps2, :], ALU.add, [kt3] + ak("bon"), [kt3])
                TT("dve", mixT[:, ps2, :], t3[:, ps2, :], gT[:, ps2, :], ALU.mult, [kt3, "t3"] + ak("gT"), [("mixT", c) for c in prs])

            def merge(*gens):
                gens = list(gens)
                while gens:
                    for g in list(gens):
                        try:
                            next(g)
                        except StopIteration:
                            gens.remove(g)

            if STOP <= 5:
                for _ in conv_stream():
                    pass
                return
            for _ in conv_stream():
                pass
            if _os.environ.get("NOMERGE"):
                for g in range(2):
                    for _ in stage2_stream(g):
                        pass
            else:
                merge(stage2_stream(0), stage2_stream(1))
            if STOP <= 6:
                return
            for m0 in range(0, 8, 4):
                outs = [nextA() for _ in range(4)]
                for k in range(8):
                    for q in range(4):
                        ps, pk = outs[q]
                        m = m0 + q
                        MM(ps[:, 0:TB], Wout[:, k, m * 128:(m + 1) * 128], mixT[:, k, :], k == 0, k == 7, [("Wout", k), ("mixT", k)], [pk])
                for q in range(4):
                    ps, pk = outs[q]
                    m = m0 + q
                    TT("dve", xT[:, m, t0:t0 + TB], xT[:, m, t0:t0 + TB], ps[:, 0:TB], ALU.add, [("xT", m), pk], [("xT", m)])

        def ffn_phase(l):
            for tg in range(4):
                t0 = tg * 512
                rms_stats(t0, 512, 1e-5, rs5[:, :], "rs5")
                POW(rs5[:, :], rs5[:, :], ["rs5"], ["rs5"], 512)
                for k in range(8):
                    STT(h2T[:, k, t0:t0 + 512], xT[:, k, t0:t0 + 512], c_("g2", k), rs5[:, :], ALU.mult, ALU.mult,
                        [("xT", k), "cst", "rs5"], [("h2T", tg)])
            w1v = w1_d[l].rearrange("(k p) n -> p k n", p=128)
            w2v = w2_d[l].rearrange("(k p) n -> p k n", p=128)
            def load_w(hc):
                i = hc % 2
                for k in range(8):
                    P.add("pool", lambda e, k=k, i=i, hc=hc: e.dma_start(out=W1c[i][:, k, :], in_=w1v[:, k, hc * 512:(hc + 1) * 512]),
                          (), [("W1c", i)], dma=("w1", i))
                for k in range(4):
                    P.add("pool", lambda e, k=k, i=i, hc=hc: e.dma_start(out=W2c[i][:, k, :], in_=w2v[:, hc * 4 + k, :]),
                          (), [("W2c", i)], dma=("w2", i))

            allps = psA + psB + [psC, psD, psE]
            fcnt = [0]

            def bankset():
                base = (fcnt[0] % 2) * 4
                fcnt[0] += 1
                return [(allps[base + q], ("psF", base + q)) for q in range(4)]

            def up(hc):
                i = hc % 2
                for hs in range(4):
                    bs = bankset()
                    for k in range(8):
                        for tg in range(4):
                            ps, pk = bs[tg]
                            MM(ps[:, :], W1c[i][:, k, hs * 128:(hs + 1) * 128], h2T[:, k, tg * 512:(tg + 1) * 512], k == 0, k == 7,
                               [("W1c", i), ("h2T", tg)], [pk])
                    for tg in range(4):
                        ps, pk = bs[tg]
                        fr = frl[tg % 2]
                        ACT(fr[:, :], ps[:, :], AF.Relu, [pk], [("frl", tg % 2)])
                        TT("dve", fT[i][:, hs, tg, :], fr[:, :], fr[:, :], ALU.mult, [("frl", tg % 2)], [("fT", i, hs)])

            def down(hc):
                i = hc % 2
                for m in range(8):
                    bs = bankset()
                    for hs in range(4):
                        for tg in range(4):
                            ps, pk = bs[tg]
                            MM(ps[:, :], W2c[i][:, hs, m * 128:(m + 1) * 128], fT[i][:, hs, tg, :], hs == 0, hs == 3,
                               [("W2c", i), ("fT", i, hs)], [pk])
                    for tg in range(4):
                        ps, pk = bs[tg]
                        TT("dve", xT[:, m, tg * 512:(tg + 1) * 512], xT[:, m, tg * 512:(tg + 1) * 512], ps[:, :], ALU.add,
                           [("xT", m), pk], [("xT", m)])

            load_w(0)
            load_w(1)
            up(0)
            for hc in range(8):
                if hc + 1 < 8:
                    up(hc + 1)
                down(hc)
                if hc + 2 < 8:
                    load_w(hc + 2)

        for s in range(nseq):
            if s > 0:
                P.barrier()
            load_seq(s)
            for l in range(nlayers):
                P.barrier()
                load_layer_consts(l)
                load_mixer_weights(l)
                for n in XN:
                    MEMSET("dve", X[n][:, :, :], 0.0, [(n, cc) for cc in range(4)])
                MEMSET("dve", ynx[:, :, :], 0.0, [("ynx_", cc) for cc in range(4)])
                MEMSET("dve", carry[:, :], 0.0, [("carry", c) for c in range(14)])
                MEMSET("dve", ubuf[:, :, 0:30], 0.0, [("ubuf", c) for c in range(4)])
                MEMSET("dve", STf[:, :, :], 0.0, [("STf", 0), ("STf", 1)])
                MEMSET("dve", STb[:, :, :], 0.0, [("STb", 0), ("STb", 1)])
                for b in range(NBLK if (STOP >= 8 or _os.environ.get('DEV_ALLBLK')) else 1):
                    mixer_block(l, b, b == 0)
                P.barrier()
                if STOP >= 9:
                    ffn_phase(l)
            P.barrier()
            store_seq(s)
        P.add("sp", None, r=[("y", s, t) for s in range(nseq) for t in range(SEQ // 128)])
        stats = P.finalize(st)
        print("PROG", stats)
        P.emit()
    return nc


def host_consts(inp):
    f = np.float32
    cst = np.zeros((L, 128, NCST), f)

    def put(l, name, vec):
        v = np.asarray(vec, f).reshape(-1, 128).T
        cst[l, :, CO[name]:CO[name] + v.shape[1]] = v
    for l in range(L):
        put(l, "g1", inp["norm1_g"][l]); put(l, "g2", inp["norm2_g"][l]); put(l, "mu", inp["mu_shift"][l])
        put(l, "w0", inp["w0"][l]); put(l, "a0", inp["a0"][l]); put(l, "kk", inp["k_k"][l]); put(l, "ka", inp["k_a"][l])
        put(l, "rk", inp["r_k"][l].reshape(-1)); put(l, "gng", inp["gn_g"][l]); put(l, "gnb", inp["gn_b"][l])
        put(l, "dwb", inp["dw_b"][l]); put(l, "clg", inp["cln_g"][l]); put(l, "clb", inp["cln_b"][l])
        dw = np.asarray(inp["dw_w"][l], f)
        cst[l, :, CO["dww"]:CO["dww"] + 124] = dw.T.reshape(4, 128, 31).transpose(1, 0, 2).reshape(128, 124)
    fg = np.ascontiguousarray(np.asarray(inp["final_g"], f).reshape(8, 128).T)
    waup = np.zeros((L, 2, 128, 512), f)
    waup[:, 0, 0:64, :] = np.asarray(inp["w_up"], f)
    waup[:, 1, 64:128, :] = np.asarray(inp["a_up"], f)
    return cst, fg, waup


_NC_CACHE = {}


def kernel(**inputs):
    inp = {k: np.asarray(v) for k, v in inputs.items()}
    n = 8
    cst, fg, waup = host_consts(inp)
    if "nc" not in _NC_CACHE:
        _NC_CACHE["nc"] = build_nc()
    nc = _NC_CACHE["nc"]
    x = np.ascontiguousarray(inp["x"], np.float32)
    shared = dict(cst=cst, fg=fg, waup=waup, gup=np.ascontiguousarray(inp["g_up"], np.float32),
                  w_in=np.ascontiguousarray(inp["w_in"], np.float32), w_out=np.ascontiguousarray(inp["w_out"], np.float32),
                  w_ff1=np.ascontiguousarray(inp["w_ff1"], np.float32), w_ff2=np.ascontiguousarray(inp["w_ff2"], np.float32))
    in_maps = [dict(shared, x=x[2 * c:2 * c + 2]) for c in range(n)]
    res = run_bass_kernel_spmd(nc, in_maps, core_ids=list(range(n)))
    return np.concatenate([r["y"] for r in res.results], axis=0)
```
